# Optimizing a Trainium2 kernel written in Bass

```python
import jax, jax.numpy as jnp
from jax import lax
import numpy as np

D_MODEL = 1024
BATCH = 8
SEQ = 2048
DEPTH = 4
DEC_BATCH = 128
DEC_SEQ = 1
PAST_LEN = 16384
PAGE_SIZE = 128

N_META = 16
N_MIXERS = 3
N_RET_LAYERS = (DEPTH + N_MIXERS - 1) // N_MIXERS
RET_HEADS = 4
RET_DK = D_MODEL // RET_HEADS
RET_DV = 2 * D_MODEL // RET_HEADS
RET_CHUNK = 128
ROPE_BASE = 10000.0
RWKV_HEAD = 64
RWKV_HEADS = D_MODEL // RWKV_HEAD
RWKV_DECAY_LORA = 64
RWKV_A_LORA = 64
RWKV_GATE_LORA = 160
CONV_WIDTH = 3
D_FF = 4 * D_MODEL
EPS = 1e-6
RWKV_GN_EPS = 64e-5

kernel_name = 'hybrid_retention_rwkv7_shortconv_step'


def rmsnorm(x, g):
    xf = x.astype(jnp.float32)
    y = xf * lax.rsqrt(jnp.mean(xf * xf, axis=-1, keepdims=True) + EPS)
    return (y * g.astype(jnp.float32)).astype(x.dtype)


def rotate_pairs(x, pos):
    inv = 1.0 / (ROPE_BASE ** jnp.linspace(0.0, 1.0, RET_DK // 2, dtype=jnp.float32))
    ang = pos.astype(jnp.float32)[:, None] * inv[None, :]
    cos = jnp.cos(ang)[None, :, None, :]
    sin = jnp.sin(ang)[None, :, None, :]
    xf = x.astype(jnp.float32).reshape(x.shape[:-1] + (RET_DK // 2, 2))
    x1, x2 = xf[..., 0], xf[..., 1]
    return jnp.stack([x1 * cos - x2 * sin, x1 * sin + x2 * cos], axis=-1).reshape(x.shape)


def retention_chunk(S, q, k, v, log_gamma):
    L = q.shape[1]
    idx = jnp.arange(L, dtype=jnp.float32)
    diff = idx[:, None] - idx[None, :]
    mask = jnp.where(diff >= 0, jnp.exp(jnp.maximum(diff, 0.0)[None] * log_gamma[:, None, None]), 0.0)
    scores = jnp.einsum('blhd,bmhd->bhlm', q, k) * mask[None]
    inner = jnp.einsum('bhlm,bmhe->blhe', scores, v)
    q_decay = jnp.exp((idx + 1.0)[:, None] * log_gamma[None, :])
    cross = jnp.einsum('blhd,bhde->blhe', q * q_decay[None, :, :, None], S)
    k_decay = jnp.exp((L - 1.0 - idx)[:, None] * log_gamma[None, :])
    S_new = S * jnp.exp(L * log_gamma)[None, :, None, None] + jnp.einsum('blhd,blhe->bhde', k * k_decay[None, :, :, None], v)
    return S_new, inner + cross


def retention_blocks(S, q, k, v, log_gamma):
    B, T = q.shape[0], q.shape[1]
    n_full = T // RET_CHUNK
    outs = []
    if n_full > 0:
        def to_blocks(a):
            return a[:, :n_full * RET_CHUNK].reshape((B, n_full, RET_CHUNK) + a.shape[2:]).swapaxes(0, 1)

        def step(carry, qkv):
            return retention_chunk(carry, qkv[0], qkv[1], qkv[2], log_gamma)

        S, o = lax.scan(step, S, (to_blocks(q), to_blocks(k), to_blocks(v)))
        outs.append(o.swapaxes(0, 1).reshape(B, n_full * RET_CHUNK, RET_HEADS, RET_DV))
    if T % RET_CHUNK:
        t0 = n_full * RET_CHUNK
        S, o = retention_chunk(S, q[:, t0:], k[:, t0:], v[:, t0:], log_gamma)
        outs.append(o)
    return S, jnp.concatenate(outs, axis=1)


def retention_mixer(xn, S0, pos, lead, w_in, w_out):
    B, T, _ = xn.shape
    hk, hv = RET_HEADS * RET_DK, RET_HEADS * RET_DV
    q, k, v, g = jnp.split(xn @ w_in, [hk, 2 * hk, 2 * hk + hv], axis=-1)
    q = rotate_pairs(q.reshape(B, T, RET_HEADS, RET_DK), pos)
    k = rotate_pairs(k.reshape(B, T, RET_HEADS, RET_DK), pos) * (RET_DK ** -0.5)
    v = v.reshape(B, T, RET_HEADS, RET_DV).astype(jnp.float32)
    log_gamma = jnp.log(1.0 - 2.0 ** (-5.0 - jnp.arange(RET_HEADS, dtype=jnp.float32)))
    S = S0.astype(jnp.float32)
    if lead > 0:
        S, o_lead = retention_chunk(S, q[:, :lead], k[:, :lead], v[:, :lead], log_gamma)
        S, o_rest = retention_blocks(S, q[:, lead:], k[:, lead:], v[:, lead:], log_gamma)
        o = jnp.concatenate([o_lead, o_rest], axis=1)
    else:
        S, o = retention_blocks(S, q, k, v, log_gamma)
    o = o * lax.rsqrt(jnp.mean(o * o, axis=-1, keepdims=True) + EPS)
    y = jax.nn.silu(g.astype(jnp.float32)) * o.reshape(B, T, hv)
    return y.astype(xn.dtype) @ w_out, S.astype(xn.dtype)


def wkv7_scan(S0, r, decay, k, v, kk, a):
    def step(S, inp):
        r_t, w_t, k_t, v_t, kk_t, a_t = inp
        sa = jnp.einsum('bhvk,bhk->bhv', S, -kk_t)
        S = S * w_t[:, :, None, :] + sa[..., None] * (kk_t * a_t)[:, :, None, :] + v_t[..., None] * k_t[:, :, None, :]
        return S, jnp.einsum('bhvk,bhk->bhv', S, r_t)

    xs = tuple(t.swapaxes(0, 1) for t in (r, decay, k, v, kk, a))
    S, y = lax.scan(step, S0, xs)
    return S, y.swapaxes(0, 1)


def rwkv7_mixer(xn, shift_prev, S0, mix, w_rkv, w0, w1, w2, a0, a1, a2, g1, g2, k_k, k_a, r_k, ln_g, ln_b, w_o):
    B, T, D = xn.shape
    H, N = RWKV_HEADS, RWKV_HEAD
    x_prev = jnp.concatenate([shift_prev[:, None].astype(xn.dtype), xn[:, :-1]], axis=1)
    xx = x_prev - xn
    xm = xn[:, :, None, :] + xx[:, :, None, :] * mix[None, None]
    r, k, v = jnp.split(jnp.einsum('btjd,jde->btje', xm[:, :, :3], w_rkv), 3, axis=2)
    r, k, v = r[:, :, 0].astype(jnp.float32), k[:, :, 0].astype(jnp.float32), v[:, :, 0].astype(jnp.float32)
    xw, xa, xg = xm[:, :, 3], xm[:, :, 4], xm[:, :, 5]
    w = -jax.nn.softplus(-(w0 + jnp.tanh(xw @ w1) @ w2).astype(jnp.float32)) - 0.5
    decay = jnp.exp(-jnp.exp(w))
    a = jax.nn.sigmoid((a0 + (xa @ a1) @ a2).astype(jnp.float32))
    g = (jax.nn.sigmoid(xg @ g1) @ g2).astype(jnp.float32)
    kk = (k * k_k.astype(jnp.float32)).reshape(B, T, H, N)
    kk = kk / jnp.maximum(jnp.sqrt(jnp.sum(kk * kk, axis=-1, keepdims=True)), 1e-12)
    k = k * (1.0 + (a - 1.0) * k_a.astype(jnp.float32))
    hs = lambda t: t.reshape(B, T, H, N)
    r, decay, k, v, a = hs(r), hs(decay), hs(k), hs(v), hs(a)
    S, y = wkv7_scan(S0.astype(jnp.float32), r, decay, k, v, kk, a)
    mu = jnp.mean(y, axis=-1, keepdims=True)
    var = jnp.mean((y - mu) ** 2, axis=-1, keepdims=True)
    y = ((y - mu) * lax.rsqrt(var + RWKV_GN_EPS)).reshape(B, T, D) * ln_g.astype(jnp.float32) + ln_b.astype(jnp.float32)
    bonus = jnp.sum(r * k * r_k.astype(jnp.float32)[None, None], axis=-1, keepdims=True) * v
    y = (y + bonus.reshape(B, T, D)) * g
    return y.astype(xn.dtype) @ w_o, xn[:, -1], S.astype(xn.dtype)


def short_conv_mixer(xn, buf, w_in, conv_w, w_out):
    T = xn.shape[1]
    b, c, h = jnp.split(xn @ w_in, 3, axis=-1)
    u = c * h
    up = jnp.concatenate([buf.astype(u.dtype), u], axis=1)
    y = conv_w[0] * up[:, :T] + conv_w[1] * up[:, 1:T + 1] + conv_w[2] * up[:, 2:T + 2]
    return (b * y) @ w_out, up[:, -(CONV_WIDTH - 1):]


def sq_relu_mlp(xn, w1, w2):
    return jnp.square(jax.nn.relu(xn @ w1)) @ w2


def trunk(h, pos, lead, ret_states, rwkv_shift, rwkv_state, conv_buf,
          norm_mix, norm_mlp, norm_final, ret_w_in, ret_w_out,
          rwkv_mix, rwkv_w_rkv, rwkv_w0, rwkv_w1, rwkv_w2, rwkv_a0, rwkv_a1, rwkv_a2,
          rwkv_g1, rwkv_g2, rwkv_k_k, rwkv_k_a, rwkv_r_k, rwkv_ln_g, rwkv_ln_b, rwkv_w_o,
          conv_w_in, conv_w, conv_w_out, mlp_w1, mlp_w2):
    new_ret = []
    for i in range(DEPTH):
        xn = rmsnorm(h, norm_mix[i])
        kind = i % N_MIXERS
        if kind == 0:
            j = i // N_MIXERS
            out, s = retention_mixer(xn, ret_states[j], pos, lead, ret_w_in[j], ret_w_out[j])
            new_ret.append(s)
        elif kind == 1:
            out, rwkv_shift, rwkv_state = rwkv7_mixer(xn, rwkv_shift, rwkv_state, rwkv_mix, rwkv_w_rkv, rwkv_w0, rwkv_w1, rwkv_w2,
                                                      rwkv_a0, rwkv_a1, rwkv_a2, rwkv_g1, rwkv_g2, rwkv_k_k, rwkv_k_a, rwkv_r_k,
                                                      rwkv_ln_g, rwkv_ln_b, rwkv_w_o)
        else:
            out, conv_buf = short_conv_mixer(xn, conv_buf, conv_w_in, conv_w, conv_w_out)
        h = h + out
        h = h + sq_relu_mlp(rmsnorm(h, norm_mlp[i]), mlp_w1[i], mlp_w2[i])
    return rmsnorm(h, norm_final), new_ret, rwkv_shift, rwkv_state, conv_buf


def setup_inputs(seed: int = 0) -> dict:
    key = jax.random.key(seed)
    ks = iter(jax.random.split(key, 48))

    def nrm(shape, scale):
        return jax.random.normal(next(ks), shape, jnp.float32) * scale

    D = D_MODEL
    hk, hv = RET_HEADS * RET_DK, RET_HEADS * RET_DV
    return {
        'x_prompt': nrm((BATCH, SEQ, D), 1.0),
        'x_sample': nrm((DEC_BATCH, DEC_SEQ, D), 1.0),
        'state_ret_l0': nrm((DEC_BATCH, RET_HEADS, RET_DK, RET_DV), 0.5),
        'state_rwkv_shift_l1': nrm((DEC_BATCH, D), 1.0),
        'state_rwkv_wkv_l1': nrm((DEC_BATCH, RWKV_HEADS, RWKV_HEAD, RWKV_HEAD), 0.3),
        'state_conv_l2': nrm((DEC_BATCH, CONV_WIDTH - 1, D), 1.0),
        'state_ret_l3': nrm((DEC_BATCH, RET_HEADS, RET_DK, RET_DV), 0.5),
        'meta_tokens': nrm((N_META, D), 1.0),
        'norm_mix': 1.0 + nrm((DEPTH, D), 0.02),
        'norm_mlp': 1.0 + nrm((DEPTH, D), 0.02),
        'norm_final': 1.0 + nrm((D,), 0.02),
        'ret_w_in': nrm((N_RET_LAYERS, D, 2 * hk + 2 * hv), D ** -0.5),
        'ret_w_out': nrm((N_RET_LAYERS, hv, D), hv ** -0.5),
        'rwkv_mix': jax.random.uniform(next(ks), (6, D), jnp.float32, 0.0, 1.0),
        'rwkv_w_rkv': nrm((3, D, D), D ** -0.5),
        'rwkv_w0': jnp.linspace(-6.0, -1.0, D, dtype=jnp.float32) + nrm((D,), 0.1),
        'rwkv_w1': nrm((D, RWKV_DECAY_LORA), D ** -0.5),
        'rwkv_w2': nrm((RWKV_DECAY_LORA, D), 0.1 * RWKV_DECAY_LORA ** -0.5),
        'rwkv_a0': nrm((D,), 0.1),
        'rwkv_a1': nrm((D, RWKV_A_LORA), D ** -0.5),
        'rwkv_a2': nrm((RWKV_A_LORA, D), 0.5 * RWKV_A_LORA ** -0.5),
        'rwkv_g1': nrm((D, RWKV_GATE_LORA), D ** -0.5),
        'rwkv_g2': nrm((RWKV_GATE_LORA, D), RWKV_GATE_LORA ** -0.5),
        'rwkv_k_k': 0.85 + nrm((D,), 0.05),
        'rwkv_k_a': 1.0 + nrm((D,), 0.05),
        'rwkv_r_k': nrm((RWKV_HEADS, RWKV_HEAD), 0.1),
        'rwkv_ln_g': 1.0 + nrm((D,), 0.02),
        'rwkv_ln_b': nrm((D,), 0.02),
        'rwkv_w_o': nrm((D, D), D ** -0.5),
        'conv_w_in': nrm((D, 3 * D), D ** -0.5),
        'conv_w': nrm((CONV_WIDTH, D), CONV_WIDTH ** -0.5),
        'conv_w_out': nrm((D, D), D ** -0.5),
        'mlp_w1': nrm((DEPTH, D, D_FF), D ** -0.5),
        'mlp_w2': nrm((DEPTH, D_FF, D), D_FF ** -0.5),
    }


def reference(x_prompt, x_sample, state_ret_l0, state_rwkv_shift_l1, state_rwkv_wkv_l1, state_conv_l2, state_ret_l3,
              meta_tokens, norm_mix, norm_mlp, norm_final, ret_w_in, ret_w_out,
              rwkv_mix, rwkv_w_rkv, rwkv_w0, rwkv_w1, rwkv_w2, rwkv_a0, rwkv_a1, rwkv_a2,
              rwkv_g1, rwkv_g2, rwkv_k_k, rwkv_k_a, rwkv_r_k, rwkv_ln_g, rwkv_ln_b, rwkv_w_o,
              conv_w_in, conv_w, conv_w_out, mlp_w1, mlp_w2):
    weights = (norm_mix, norm_mlp, norm_final, ret_w_in, ret_w_out,
               rwkv_mix, rwkv_w_rkv, rwkv_w0, rwkv_w1, rwkv_w2, rwkv_a0, rwkv_a1, rwkv_a2,
               rwkv_g1, rwkv_g2, rwkv_k_k, rwkv_k_a, rwkv_r_k, rwkv_ln_g, rwkv_ln_b, rwkv_w_o,
               conv_w_in, conv_w, conv_w_out, mlp_w1, mlp_w2)
    dt = x_prompt.dtype
    B = x_prompt.shape[0]

    h_p = jnp.concatenate([jnp.broadcast_to(meta_tokens[None].astype(dt), (B, N_META, D_MODEL)), x_prompt], axis=1)
    pos_p = jnp.arange(SEQ + N_META)
    zero_ret = [jnp.zeros((B, RET_HEADS, RET_DK, RET_DV), dt) for _ in range(N_RET_LAYERS)]
    out_p, ret_p, shift_p, wkv_p, conv_p = trunk(
        h_p, pos_p, N_META, zero_ret, jnp.zeros((B, D_MODEL), dt),
        jnp.zeros((B, RWKV_HEADS, RWKV_HEAD, RWKV_HEAD), dt), jnp.zeros((B, CONV_WIDTH - 1, D_MODEL), dt), *weights)
    y_prompt = out_p[:, N_META:]

    pos_s = PAST_LEN + jnp.arange(DEC_SEQ)
    y_sample, ret_s, shift_s, wkv_s, conv_s = trunk(
        x_sample, pos_s, 0, [state_ret_l0, state_ret_l3], state_rwkv_shift_l1, state_rwkv_wkv_l1, state_conv_l2, *weights)

    return (y_prompt, y_sample, ret_p[0], ret_s[0], shift_p, shift_s, wkv_p, wkv_s, conv_p, conv_s, ret_p[1], ret_s[1])
```

```python
import contextlib
import math
import numpy as np
import concourse.bass as bass
import concourse.mybir as mybir
from concourse.bass_utils import run_bass_kernel_spmd

F32 = mybir.dt.float32
F32R = mybir.dt.float32r
BF16 = mybir.dt.bfloat16
AF = mybir.ActivationFunctionType
ALU = mybir.AluOpType
AX = mybir.AxisListType

D = 1024
NCH = 8
TP = 2064
NSMP = 16
TT = TP + NSMP
EPS = 1e-6
GN_EPS = 64e-5
N_CORES = 8
GAMMAS = [1.0 - 2.0 ** (-5.0 - h) for h in range(4)]

GROUPS = [
    dict(c0=0, c1=1024, tiles=[(0, 512), (512, 1024)]),
    dict(c0=1024, c1=2080, tiles=[(1024, 1376), (1376, 1728), (1728, 2080)]),
]
V_NMIX, V_NMLP, V_NFIN, V_RMIX, V_W0, V_A0, V_KK, V_KA, V_RK, V_CW, V_LNG, V_LNB = 0, 4, 8, 9, 15, 16, 17, 18, 19, 20, 23, 24
NVEC = 25
RW_BLOCKS = [(128, 2048)] * 16 + [(128, 1024), (128, 1280), (64, 2048), (128, 1024), (32, 1024)]
RW_ORDER = list(range(8)) + [16, 18, 8, 9, 10, 11, 17, 19, 20, 12, 13, 14, 15]
NRING = 6
PBATCH = 3
CH_DT = F32R


class Res:
    __slots__ = ("w", "r", "name")

    def __init__(self, name="", r=None):
        self.w = None
        self.r = dict(r) if r else {}
        self.name = name


class SemCtr:
    __slots__ = ("sem", "count", "name")

    def __init__(self, sem, name):
        self.sem = sem
        self.count = 0
        self.name = name


class Eng:
    def __init__(self, h, name, self_sync):
        self.h = h
        self.ctr = None
        self.name = name
        self.known = {}
        self.self_sync = self_sync


SEM_LIMIT = 12000


class Sched:
    def __init__(self, nc, n_dma_sems=20):
        self.nc = nc
        self._cms = []
        self._nsem = 0
        self.pe = Eng(nc.tensor, "pe", False)
        self.act = Eng(nc.scalar, "act", True)
        self.dve = Eng(nc.vector, "dve", True)
        self.pool = Eng(nc.gpsimd, "pool", True)
        self.sp = Eng(nc.sync, "sp", False)
        self.engines = (self.pe, self.act, self.dve, self.pool)
        for e in self.engines:
            e.ctr = self._mk(e.name)
        self.dsems = [self._mk("dma%d" % i) for i in range(n_dma_sems)]
        self.dnext = 0
        self.out_dmas = []
        self.old_ctrs = []

    def _mk(self, name):
        cm = self.nc.semaphore("s%d_%s" % (self._nsem, name))
        self._nsem += 1
        s = cm.__enter__()
        self._cms.append(cm)
        return SemCtr(s, name)

    def close(self):
        for cm in reversed(self._cms):
            cm.__exit__(None, None, None)

    def new_res(self, name=""):
        r = {}
        for e in self.engines:
            if e.ctr.count:
                r[e.ctr] = e.ctr.count
        for c in self.dsems:
            if c.count:
                r[c] = c.count
        return Res(name, r)

    def _wait(self, eng, deps):
        need = {}
        for (ctr, val) in deps:
            if eng.ctr is ctr and not eng.self_sync:
                continue
            if need.get(ctr, 0) < val:
                need[ctr] = val
        for ctr, val in need.items():
            if eng.known.get(ctr, 0) < val:
                eng.h.wait_ge(ctr.sem, val)
                eng.known[ctr] = val

    @staticmethod
    def _deps(reads, writes):
        deps = []
        for r in reads:
            if r.w is not None:
                deps.append(r.w)
        for w in writes:
            if w.w is not None:
                deps.append(w.w)
            deps.extend(w.r.items())
        return deps

    @staticmethod
    def _mark(reads, writes, tag):
        ctr, val = tag
        for r in reads:
            if r.r.get(ctr, 0) < val:
                r.r[ctr] = val
        for w in writes:
            w.w = tag
            w.r = {}

    def op(self, eng, fn, reads=(), writes=(), inc=True):
        self._wait(eng, self._deps(reads, writes))
        ins = fn(eng.h)
        if inc:
            eng.ctr.count += 1
            ins.then_inc(eng.ctr.sem, 1)
            tag = (eng.ctr, eng.ctr.count)
        else:
            tag = (eng.ctr, eng.ctr.count + 1)
        self._mark(reads, writes, tag)
        if inc and eng.ctr.count >= SEM_LIMIT:
            eng.ctr = self._mk(eng.name)
        return ins

    def dma(self, eng, out, in_, reads=(), writes=(), is_output=False):
        ctr = self.dsems[self.dnext]
        if ctr.count >= SEM_LIMIT:
            self.old_ctrs.append(ctr)
            prev = (ctr, ctr.count)
            ctr = self._mk("dma%d" % self.dnext)
            self.dsems[self.dnext] = ctr
        else:
            prev = (ctr, ctr.count) if ctr.count else None
        self.dnext = (self.dnext + 1) % len(self.dsems)
        deps = self._deps(reads, writes)
        if prev is not None:
            deps.append(prev)
        self._wait(eng, deps)
        ctr.count += 16
        eng.h.dma_start(out=out, in_=in_).then_inc(ctr.sem, 16)
        tag = (ctr, ctr.count)
        self._mark(reads, writes, tag)
        if is_output:
            self.out_dmas.append(tag)
        return tag

    def finish(self):
        deps = list(self.out_dmas)
        for c in list(self.dsems) + self.old_ctrs:
            if c.count:
                deps.append((c, c.count))
        for e in self.engines:
            if e.ctr.count:
                deps.append((e.ctr, e.ctr.count))
        self._wait(self.sp, deps)


class Prog:
    def __init__(self, layers=4, with_rwkv=True):
        self.layers = layers
        self.with_rwkv = with_rwkv
        self.nc = bass.Bass("TRN2", target_bir_lowering=False)
        self.S = Sched(self.nc)
        self.es = contextlib.ExitStack()
        self.dram = {}
        self.inputs = set()
        self._uid = 0

    def din(self, name, shape, dt=F32):
        t = self.nc.dram_tensor(name, list(shape), dt, kind="ExternalInput").ap()
        self.dram[name] = t
        self.inputs.add(name)
        return t

    def dout(self, name, shape, dt=F32):
        t = self.nc.dram_tensor(name, list(shape), dt, kind="ExternalOutput").ap()
        self.dram[name] = t
        return t

    def sb(self, es, name, shape, dt):
        self._uid += 1
        return es.enter_context(self.nc.sbuf_tensor("%s_%d" % (name, self._uid), list(shape), dt))

    def bank(self):
        for _ in range(8):
            i = self.bnext
            self.bnext = (self.bnext + 1) % 8
            if i not in self.breserved:
                return self.banks[i], self.bres[i]
        raise RuntimeError("no psum bank")

    def reserve_bank(self):
        bk, r = self.bank()
        i = self.banks.index(bk)
        self.breserved.add(i)
        return bk, r, i

    def ws_add(self, ap, p, n, dst=None, dst_res=None):
        self.wplan.append(dict(ap=ap, p=p, n=n, dst=dst, dst_res=dst_res))

    def _ws_conv(self):
        i = self.w_cv
        spec = self.wplan[i]
        bf, ores = self.wbufs[i % len(self.wbufs)]
        p, n = spec["p"], spec["n"]
        if spec.get("cached") is not None:
            ci = spec["cached"]
            self.S.dma(self.S.sp, bf[0:p, 0:n], self.wcache[ci, 0:p, 0:n], reads=[self.wcres[ci]], writes=[ores])
        else:
            src = spec["ap"]
            dst = bf[0:p, 0:n]
            if len(src.shape) == 3:
                dst = dst.rearrange("p (c e) -> p c e", c=src.shape[1])
            self.S.dma(self.S.pool, dst, src, writes=[ores])
        self.w_cv += 1

    def ws_next(self, p, n, kc=None):
        i = self.w_use
        spec = self.wplan[i]
        assert spec["p"] == p and spec["n"] == n, (i, spec["p"], spec["n"], p, n)
        while self.w_cv <= min(i + len(self.wbufs) - 2, len(self.wplan) - 1):
            self._ws_conv()
        self.w_use += 1
        bf, r = self.wbufs[i % len(self.wbufs)]
        v = bf[0:p, 0:n]
        if kc is not None:
            v = v.rearrange("p (c e) -> p c e", c=kc)
        return v, r

    def rmsnorm_tile(self, c0, c1, hres, gi, out_fn, out_res, out_f32_fn=None):
        S = self.S
        w = c1 - c0
        ps, pr = self.bank()
        for c in range(NCH):
            sq, sqr = self.sqb[c % 2]
            S.op(S.act, lambda e: e.activation(out=sq[:, 0:w], in_=self.hT[:, c, c0:c1], func=AF.Square),
                 reads=hres, writes=[sqr])
            S.op(S.pe, lambda e: e.matmul(ps[:, 0:w], lhsT=self.onesD[:], rhs=sq[:, 0:w],
                                          start=(c == 0), stop=(c == NCH - 1)),
                 reads=[sqr, self.cres], writes=[pr], inc=True)
        rs, rr = self.rsb
        S.op(S.act, lambda e: e.activation(out=rs[:, 0:w], in_=ps[:, 0:w], func=AF.Ln, bias=self.epsv[:, 0:1]),
             reads=[pr, self.cres], writes=[rr])
        S.op(S.act, lambda e: e.activation(out=rs[:, 0:w], in_=rs[:, 0:w], func=AF.Exp, scale=-0.5),
             reads=[rr], writes=[rr])
        for c in range(NCH):
            S.op(S.dve, lambda e: e.scalar_tensor_tensor(out=out_fn(c), in0=self.hT[:, c, c0:c1],
                                                         scalar=self.vecs[:, gi, c:c + 1], in1=rs[:, 0:w],
                                                         op0=ALU.mult, op1=ALU.mult),
                 reads=hres + [rr, self.cres], writes=[out_res])

    def norm_group(self, g, gi):
        G = GROUPS[g]
        for ti, (a, b) in enumerate(G["tiles"]):
            la = a - G["c0"]
            self.rmsnorm_tile(a, b, [self.hres[(g, ti)]], gi,
                              lambda c: self.xn[:, c, la:la + (b - a)], self.xnres[ti])

    def plan_mlp(self, li):
        w1 = self.dram["mlp_w1"]
        w2 = self.dram["mlp_w2"]
        for blk in range(16):
            self.ws_add(w1[li].rearrange("(c p) e -> p c e", p=128)[:, :, blk * 256:(blk + 1) * 256], 128, 2048)
        for cb in range(4):
            for kg in range(4):
                self.ws_add(w2[li, kg * 1024:(kg + 1) * 1024, cb * 256:(cb + 1) * 256]
                            .rearrange("(c p) e -> p c e", p=128), 128, 2048)

    def mlp_group(self, li, g, h1, h1res, rl):
        S = self.S
        G = GROUPS[g]
        c0 = G["c0"]
        self.norm_group(g, V_NMLP + li)
        for blk in range(16):
            wb, wr = self.ws_next(128, 2048, kc=8)
            for ti, (a, b) in enumerate(G["tiles"]):
                w = b - a
                for oc in range(2):
                    ps, pr = self.bank()
                    for c in range(NCH):
                        S.op(S.pe, lambda e: e.matmul(ps[:, 0:w], lhsT=wb[:, c, oc * 128:(oc + 1) * 128],
                                                      rhs=self.xn[:, c, a - c0:b - c0],
                                                      start=(c == 0), stop=(c == NCH - 1)),
                             reads=[wr, self.xnres[ti]], writes=[pr], inc=(c == NCH - 1))
                    rt, rtr = rl[(blk * 2 + oc) % 2]
                    S.op(S.act, lambda e: e.activation(out=rt[:, 0:w], in_=ps[:, 0:w], func=AF.Relu),
                         reads=[pr], writes=[rtr])
                    fc = blk * 2 + oc
                    S.op(S.dve, lambda e: e.tensor_tensor(out=h1[:, fc, a - c0:b - c0], in0=rt[:, 0:w],
                                                          in1=rt[:, 0:w], op=ALU.mult),
                         reads=[rtr], writes=[h1res[ti]])
        for cb in range(4):
            accs = {}
            for ti in range(len(G["tiles"])):
                for oc in range(2):
                    accs[(ti, oc)] = self.reserve_bank()
            for kg in range(4):
                wb, wr = self.ws_next(128, 2048, kc=8)
                for ti, (a, b) in enumerate(G["tiles"]):
                    w = b - a
                    for oc in range(2):
                        ps, pr, _ = accs[(ti, oc)]
                        for c in range(NCH):
                            first = (kg == 0 and c == 0)
                            last = (kg == 3 and c == NCH - 1)
                            S.op(S.pe, lambda e: e.matmul(ps[:, 0:w], lhsT=wb[:, c, oc * 128:(oc + 1) * 128],
                                                          rhs=h1[:, kg * 8 + c, a - c0:b - c0],
                                                          start=first, stop=last),
                                 reads=[wr, h1res[ti]], writes=[pr], inc=(c == NCH - 1))
            for ti, (a, b) in enumerate(G["tiles"]):
                w = b - a
                for oc in range(2):
                    ps, pr, bi = accs[(ti, oc)]
                    ec = cb * 2 + oc
                    S.op(S.dve, lambda e: e.tensor_tensor(out=self.hT[:, ec, a:b], in0=self.hT[:, ec, a:b],
                                                          in1=ps[:, 0:w], op=ALU.add),
                         reads=[pr], writes=[self.hres[(g, ti)]])
                    self.breserved.discard(bi)

    def mlp_layer(self, li):
        with contextlib.ExitStack() as es:
            self.xn = self.sb(es, "xn", [128, NCH, 1056], BF16)
            self.xnres = [self.S.new_res("xn%d" % i) for i in range(3)]
            h1 = self.sb(es, "h1", [128, 32, 1056], BF16)
            h1res = [self.S.new_res("h1_%d" % i) for i in range(3)]
            rl = [(self.sb(es, "relu", [128, 512], F32), self.S.new_res("relu")) for _ in range(2)]
            for g in range(2):
                self.mlp_group(li, g, h1, h1res, rl)

    def plan_conv(self):
        wi = self.dram["conv_w_in"].rearrange("(c p) e -> p c e", p=128)
        wo = self.dram["conv_w_out"].rearrange("(c p) e -> p c e", p=128)
        for g in range(2):
            for fp in range(4):
                for part in range(3):
                    self.ws_add(wi[:, :, part * 1024 + fp * 256: part * 1024 + (fp + 1) * 256], 128, 2048)
            for blk in range(4):
                self.ws_add(wo[:, :, blk * 256:(blk + 1) * 256], 128, 2048)

    def conv_layer(self, li):
        S = self.S
        with contextlib.ExitStack() as es:
            self.xn = self.sb(es, "xn", [128, NCH, 1056], BF16)
            self.xnres = [self.S.new_res("xn%d" % i) for i in range(3)]
            bB = self.sb(es, "cv_b", [128, 2, 1056], F32)
            cB = self.sb(es, "cv_c", [128, 2, 1056], F32)
            uB = self.sb(es, "cv_u", [128, 2, 1058], F32)
            yB = self.sb(es, "cv_y", [128, 1056], F32)
            zT = self.sb(es, "cv_z", [128, 8, 1056], BF16)
            carry = self.sb(es, "cv_carry", [128, 8, 2], F32)
            st = self.sb(es, "cv_st", [128, 8, 2, NSMP], F32)
            snew = self.sb(es, "cv_snew", [128, 8, 2, NSMP], F32)
            pout = self.sb(es, "cv_pout", [128, 8, 2], F32)
            bres, cres_, ures, yres, zres, carres, stres, snres, pores = [S.new_res(n) for n in
                                                                          "b c u y z carry st snew pout".split()]
            S.dma(S.sp, st[:], self.dram["s_convT"].rearrange("(c p) j b -> p c j b", p=128), writes=[stres])
            S.op(S.pool, lambda e: e.memset(carry[:], 0.0), writes=[carres])
            for g in range(2):
                G = GROUPS[g]
                c0 = G["c0"]
                tg = G["c1"] - c0
                tp = tg - (NSMP if g == 1 else 0)
                self.norm_group(g, V_NMIX + li)
                for fp in range(4):
                    for part, (buf, br) in enumerate(((bB, bres), (cB, cres_), (None, None))):
                        wb, wr = self.ws_next(128, 2048, kc=8)
                        for ti, (a, b) in enumerate(G["tiles"]):
                            w = b - a
                            la = a - c0
                            for oc in range(2):
                                ps, pr = self.bank()
                                for c in range(NCH):
                                    S.op(S.pe, lambda e: e.matmul(ps[:, 0:w], lhsT=wb[:, c, oc * 128:(oc + 1) * 128],
                                                                  rhs=self.xn[:, c, la:la + w],
                                                                  start=(c == 0), stop=(c == NCH - 1)),
                                         reads=[wr, self.xnres[ti]], writes=[pr], inc=(c == NCH - 1))
                                if part < 2:
                                    S.op(S.act, lambda e: e.activation(out=buf[:, oc, la:la + w], in_=ps[:, 0:w],
                                                                       func=AF.Copy),
                                         reads=[pr], writes=[br])
                                else:
                                    S.op(S.dve, lambda e: e.tensor_tensor(out=uB[:, oc, 2 + la:2 + la + w],
                                                                          in0=ps[:, 0:w], in1=cB[:, oc, la:la + w],
                                                                          op=ALU.mult),
                                         reads=[pr, cres_], writes=[ures])
                    for oc in range(2):
                        fc = fp * 2 + oc
                        cw = lambda j: self.vecs[:, V_CW + j, fc:fc + 1]
                        S.op(S.dve, lambda e: e.tensor_copy(out=uB[:, oc, 0:2], in_=carry[:, fc, :]),
                             reads=[carres], writes=[ures])
                        S.op(S.dve, lambda e: e.tensor_scalar(out=yB[:, 0:tp], in0=uB[:, oc, 0:tp], scalar1=cw(0),
                                                              scalar2=None, op0=ALU.mult),
                             reads=[ures, self.cres], writes=[yres])
                        S.op(S.dve, lambda e: e.scalar_tensor_tensor(out=yB[:, 0:tp], in0=uB[:, oc, 1:tp + 1],
                                                                     scalar=cw(1), in1=yB[:, 0:tp],
                                                                     op0=ALU.mult, op1=ALU.add),
                             reads=[ures, yres, self.cres], writes=[yres])
                        S.op(S.dve, lambda e: e.scalar_tensor_tensor(out=yB[:, 0:tp], in0=uB[:, oc, 2:tp + 2],
                                                                     scalar=cw(2), in1=yB[:, 0:tp],
                                                                     op0=ALU.mult, op1=ALU.add),
                             reads=[ures, yres, self.cres], writes=[yres])
                        if g == 1:
                            S.op(S.dve, lambda e: e.tensor_scalar(out=yB[:, tp:tg], in0=st[:, fc, 0, :], scalar1=cw(0),
                                                                  scalar2=None, op0=ALU.mult),
                                 reads=[stres, self.cres], writes=[yres])
                            S.op(S.dve, lambda e: e.scalar_tensor_tensor(out=yB[:, tp:tg], in0=st[:, fc, 1, :],
                                                                         scalar=cw(1), in1=yB[:, tp:tg],
                                                                         op0=ALU.mult, op1=ALU.add),
                                 reads=[stres, yres, self.cres], writes=[yres])
                            S.op(S.dve, lambda e: e.scalar_tensor_tensor(out=yB[:, tp:tg], in0=uB[:, oc, 2 + tp:2 + tg],
                                                                         scalar=cw(2), in1=yB[:, tp:tg],
                                                                         op0=ALU.mult, op1=ALU.add),
                                 reads=[ures, yres, self.cres], writes=[yres])
                            S.op(S.act, lambda e: e.activation(out=snew[:, fc, 0, :], in_=st[:, fc, 1, :], func=AF.Copy),
                                 reads=[stres], writes=[snres])
                            S.op(S.act, lambda e: e.activation(out=snew[:, fc, 1, :], in_=uB[:, oc, 2 + tp:2 + tg],
                                                               func=AF.Copy),
                                 reads=[ures], writes=[snres])
                            S.op(S.act, lambda e: e.activation(out=pout[:, fc, :], in_=uB[:, oc, tp:tp + 2], func=AF.Copy),
                                 reads=[ures], writes=[pores])
                        else:
                            S.op(S.act, lambda e: e.activation(out=carry[:, fc, :], in_=uB[:, oc, tp:tp + 2], func=AF.Copy),
                                 reads=[ures], writes=[carres])
                        S.op(S.dve, lambda e: e.tensor_tensor(out=zT[:, fc, 0:tg], in0=yB[:, 0:tg], in1=bB[:, oc, 0:tg],
                                                              op=ALU.mult),
                             reads=[yres, bres], writes=[zres])
                for blk in range(4):
                    wb, wr = self.ws_next(128, 2048, kc=8)
                    for ti, (a, b) in enumerate(G["tiles"]):
                        w = b - a
                        la = a - c0
                        for oc in range(2):
                            ps, pr = self.bank()
                            for c in range(NCH):
                                S.op(S.pe, lambda e: e.matmul(ps[:, 0:w], lhsT=wb[:, c, oc * 128:(oc + 1) * 128],
                                                              rhs=zT[:, c, la:la + w],
                                                              start=(c == 0), stop=(c == NCH - 1)),
                                     reads=[wr, zres], writes=[pr], inc=(c == NCH - 1))
                            ec = blk * 2 + oc
                            S.op(S.dve, lambda e: e.tensor_tensor(out=self.hT[:, ec, a:b], in0=self.hT[:, ec, a:b],
                                                                  in1=ps[:, 0:w], op=ALU.add),
                                 reads=[pr], writes=[self.hres[(g, ti)]])
            S.dma(S.sp, self.dram["convT_p"].rearrange("(c p) j -> p c j", p=128), pout[:], reads=[pores],
                  is_output=True)
            S.dma(S.sp, self.dram["convT_s"].rearrange("(c p) j b -> p c j b", p=128), snew[:], reads=[snres],
                  is_output=True)

    def plan_ret(self, j):
        win = self.dram["ret_w_in"][j].rearrange("(c p) e -> p c e", p=128)
        wout = self.dram["ret_w_out"][j]
        for g in range(2):
            for h in range(4):
                self.ws_add(win[:, :, h * 256:(h + 1) * 256], 128, 2048)
                self.ws_add(win[:, :, 1024 + h * 256:1024 + (h + 1) * 256], 128, 2048)
                for half in range(2):
                    o = 2048 + h * 512 + half * 256
                    self.ws_add(win[:, :, o:o + 256], 128, 2048)
                for half in range(2):
                    o = 4096 + h * 512 + half * 256
                    self.ws_add(win[:, :, o:o + 256], 128, 2048)
                for half in range(2):
                    self.ws_add(wout[h * 512:(h + 1) * 512, half * 512:(half + 1) * 512]
                                .rearrange("(c p) e -> p c e", p=128), 128, 2048)

    def ret_layer(self, li, j):
        S = self.S
        s_in = self.dram["s_ret%d" % li]
        o_p = self.dram["ret%d_p" % li]
        o_s = self.dram["ret%d_s" % li]
        with contextlib.ExitStack() as es:
            self.xn = self.sb(es, "xn", [128, NCH, 1056], BF16)
            self.xnres = [self.S.new_res("xn%d" % i) for i in range(3)]
            self.causT = self.sb(es, "causT", [128, 128], F32)
            self.delta16 = self.sb(es, "delta16", [128, 16], F32)
            self.deltaR = self.sb(es, "deltaR", [128, 16, 16], F32)
            rcst = S.new_res("retconst")
            S.dma(S.sp, self.causT[:], self.dram["consts"][:, 2, :], writes=[rcst])
            S.dma(S.sp, self.delta16[:], self.dram["consts"][:, 0, 0:16], writes=[rcst])
            S.dma(S.sp, self.deltaR[:], self.dram["deltaR"][:, :, :], writes=[rcst])
            cosT = self.sb(es, "cos", [128, 1056], F32)
            sinT = self.sb(es, "sin", [128, 1056], F32)
            qd = self.sb(es, "qd", [128, 512], F32)
            kd = self.sb(es, "kd", [128, 512], F32)
            qT = self.sb(es, "qT", [128, 2, 1056], BF16)
            kT = self.sb(es, "kT", [128, 2, 1056], BF16)
            gT = self.sb(es, "gT", [128, 4, 1056], BF16)
            vS = self.sb(es, "vS", [128, 10, 512], BF16)
            Sf = self.sb(es, "Sf", [128, 4, 2, 512], F32)
            Sb = self.sb(es, "Sb", [128, 2, 512], BF16)
            sring = [(self.sb(es, "sring", [128, 2, 512], F32), S.new_res("sring")) for _ in range(3)]
            sbf = self.sb(es, "sbf", [128, 2, 512], BF16)
            tmp = [(self.sb(es, "rtmp", [128, 512], F32), S.new_res("rtmp")) for _ in range(3)]
            sT2 = [self.sb(es, "sT", [128, 128], BF16) for _ in range(2)]
            onb2 = [self.sb(es, "onb", [128, 512], BF16) for _ in range(2)]
            kTok2 = [self.sb(es, "kTok", [128, 256], BF16) for _ in range(2)]
            sT2r = [S.new_res("sT") for _ in range(2)]
            onb2r = [S.new_res("onb") for _ in range(2)]
            kTok2r = [S.new_res("kTok") for _ in range(2)]
            Sfres2 = [S.new_res("Sf0"), S.new_res("Sf1")]
            Sbres2 = [S.new_res("Sb0"), S.new_res("Sb1")]
            kmask = self.sb(es, "kmask", [16, 16, 256], BF16)
            qmask = self.sb(es, "qmask", [128, 2, 16, 16], BF16)
            ssq = self.sb(es, "ssq", [128, 2], F32)
            junk = self.sb(es, "junk", [128, 512], BF16)
            (tabres, qdres, kdres, qres, kres, gres, vres, Sfres, Sbres, sbfres, sTres, onres, kTokres, kmres, qmres,
             ssres, jres) = [S.new_res(n) for n in
                             "tab qd kd q k g v Sf Sb sbf sT on kTok kmask qmask ss junk".split()]
            S.op(S.pool, lambda e: e.memset(Sf[:], 0.0), writes=Sfres2)
            for g in range(2):
                G = GROUPS[g]
                c0, c1 = G["c0"], G["c1"]
                tg = c1 - c0
                self.norm_group(g, V_NMIX + li)
                S.dma(S.sp, cosT[:, 0:tg], self.dram["cos_t"][:, c0:c1], writes=[tabres])
                S.dma(S.sp, sinT[:, 0:tg], self.dram["sin_t"][:, c0:c1], writes=[tabres])
                chunks = [(i * 128, 128, False) for i in range(8)]
                if g == 1:
                    chunks += [(1024, 16, False), (1040, 16, True)]
                for h in range(4):
                    gam = GAMMAS[h]
                    if g == 0:
                        S.op(S.pool, lambda e: e.memset(Sb[:], 0.0), writes=Sbres2)
                    else:
                        S.op(S.act, lambda e: e.activation(out=Sb[:], in_=Sf[:, h, :, :], func=AF.Copy),
                             reads=Sfres2, writes=Sbres2)
                    for (dst, dres, dtab, drow, tres) in ((qT, qres, qd, h, qdres), (kT, kres, kd, 4 + h, kdres)):
                        wb, wr = self.ws_next(128, 2048, kc=8)
                        for ti, (a, b) in enumerate(G["tiles"]):
                            w = b - a
                            la = a - c0
                            S.dma(S.sp, dtab[:, 0:w], self.dram["qkd_t"][drow, a:b].partition_broadcast(128), writes=[tres])
                            pA, pAr = self.bank()
                            pB, pBr = self.bank()
                            for (pp, ppr, o) in ((pA, pAr, 0), (pB, pBr, 128)):
                                for c in range(NCH):
                                    S.op(S.pe, lambda e: e.matmul(pp[:, 0:w], lhsT=wb[:, c, o:o + 128],
                                                                  rhs=self.xn[:, c, la:la + w],
                                                                  start=(c == 0), stop=(c == NCH - 1)),
                                         reads=[wr, self.xnres[ti]], writes=[ppr], inc=(c == NCH - 1))
                            (t1, r1), (t2, r2), (t3, r3) = tmp
                            co = cosT[:, la:la + w]
                            si = sinT[:, la:la + w]
                            dd = dtab[:, 0:w]
                            tt = lambda e, o_, a_, b_, op_: e.tensor_tensor(out=o_, in0=a_, in1=b_, op=op_)
                            S.op(S.dve, lambda e: tt(e, t1[:, 0:w], pA[:, 0:w], co, ALU.mult), reads=[pAr, tabres], writes=[r1])
                            S.op(S.dve, lambda e: tt(e, t2[:, 0:w], pB[:, 0:w], si, ALU.mult), reads=[pBr, tabres], writes=[r2])
                            S.op(S.dve, lambda e: tt(e, t1[:, 0:w], t1[:, 0:w], t2[:, 0:w], ALU.subtract), reads=[r1, r2], writes=[r1])
                            S.op(S.dve, lambda e: tt(e, dst[:, 0, la:la + w], t1[:, 0:w], dd, ALU.mult), reads=[r1, tres], writes=[dres])
                            S.op(S.dve, lambda e: tt(e, t2[:, 0:w], pA[:, 0:w], si, ALU.mult), reads=[pAr, tabres], writes=[r2])
                            S.op(S.dve, lambda e: tt(e, t3[:, 0:w], pB[:, 0:w], co, ALU.mult), reads=[pBr, tabres], writes=[r3])
                            S.op(S.dve, lambda e: tt(e, t2[:, 0:w], t2[:, 0:w], t3[:, 0:w], ALU.add), reads=[r2, r3], writes=[r2])
                            S.op(S.dve, lambda e: tt(e, dst[:, 1, la:la + w], t2[:, 0:w], dd, ALU.mult), reads=[r2, tres], writes=[dres])
                    for half in range(2):
                        wb, wr = self.ws_next(128, 2048, kc=8)
                        for ci, (lo, L, smp) in enumerate(chunks):
                            ps, pr = self.bank()
                            for c in range(NCH):
                                S.op(S.pe, lambda e: e.matmul(ps[0:L, 0:256], lhsT=self.xn[:, c, lo:lo + L],
                                                              rhs=wb[:, c, :], start=(c == 0), stop=(c == NCH - 1)),
                                     reads=[wr] + self.xnres, writes=[pr], inc=(c == NCH - 1))
                            S.op(S.act, lambda e: e.activation(out=vS[0:L, ci, half * 256:(half + 1) * 256],
                                                               in_=ps[0:L, 0:256], func=AF.Copy),
                                 reads=[pr], writes=[vres])
                    for half in range(2):
                        wb, wr = self.ws_next(128, 2048, kc=8)
                        for ti, (a, b) in enumerate(G["tiles"]):
                            w = b - a
                            la = a - c0
                            for oc in range(2):
                                ps, pr = self.bank()
                                for c in range(NCH):
                                    S.op(S.pe, lambda e: e.matmul(ps[:, 0:w], lhsT=wb[:, c, oc * 128:(oc + 1) * 128],
                                                                  rhs=self.xn[:, c, la:la + w],
                                                                  start=(c == 0), stop=(c == NCH - 1)),
                                         reads=[wr, self.xnres[ti]], writes=[pr], inc=(c == NCH - 1))
                                S.op(S.act, lambda e: e.activation(out=gT[:, half * 2 + oc, la:la + w], in_=ps[:, 0:w],
                                                                   func=AF.Silu),
                                     reads=[pr], writes=[gres])
                    def st_SK(ci, lo, L):
                        cs = slice(lo, lo + L)
                        k2 = ci % 2
                        psS, psSr = self.bank()
                        for dc in range(2):
                            S.op(S.pe, lambda e: e.matmul(psS[0:L, 0:L], lhsT=kT[:, dc, cs], rhs=qT[:, dc, cs],
                                                          start=(dc == 0), stop=(dc == 1)),
                                 reads=[kres, qres], writes=[psSr], inc=(dc == 1))
                        S.op(S.dve, lambda e: e.tensor_tensor(out=sT2[k2][0:L, 0:L], in0=psS[0:L, 0:L],
                                                              in1=self.causT[0:L, 0:L], op=ALU.mult),
                             reads=[psSr, self.cres, rcst], writes=[sT2r[k2]])
                        psK, psKr = self.bank()
                        pk = psK[:].bitcast(BF16)
                        for dc in range(2):
                            S.op(S.pe, lambda e: e.transpose(pk[0:L, dc * 128:(dc + 1) * 128], kT[:, dc, cs], self.identb[:, :]),
                                 reads=[kres, self.cres, rcst], writes=[psKr], inc=(dc == 1))
                        S.op(S.act, lambda e: e.activation(out=kTok2[k2][0:L, :], in_=pk[0:L, 0:256], func=AF.Copy,
                                                           scale=gam ** L),
                             reads=[psKr], writes=[kTok2r[k2]])

                    def st_O(ci, lo, L):
                        cs = slice(lo, lo + L)
                        k2 = ci % 2
                        psO, psOr = self.bank()
                        S.op(S.pe, lambda e: e.matmul(psO[0:L, :], lhsT=sT2[k2][0:L, 0:L], rhs=vS[0:L, ci, :],
                                                      start=True, stop=False),
                             reads=[sT2r[k2], vres], writes=[psOr], inc=False)
                        for dc in range(2):
                            S.op(S.pe, lambda e: e.matmul(psO[0:L, :], lhsT=qT[:, dc, cs], rhs=Sb[:, dc, :],
                                                          start=False, stop=(dc == 1)),
                                 reads=[qres, Sbres2[dc]], writes=[psOr], inc=(dc == 1))
                        return psO, psOr

                    def st_U(ci, lo, L, last_prompt):
                        k2 = ci % 2
                        gL = gam ** L
                        for dc in range(2):
                            ps2, ps2r = self.bank()
                            S.op(S.pe, lambda e: e.matmul(ps2[:, :], lhsT=kTok2[k2][0:L, dc * 128:(dc + 1) * 128],
                                                          rhs=vS[0:L, ci, :], start=True, stop=True),
                                 reads=[kTok2r[k2], vres], writes=[ps2r])
                            S.op(S.dve, lambda e: e.scalar_tensor_tensor(out=Sf[:, h, dc, :], in0=Sf[:, h, dc, :],
                                                                         scalar=gL, in1=ps2[:, :],
                                                                         op0=ALU.mult, op1=ALU.add),
                                 reads=[ps2r, Sfres2[dc]], writes=[Sfres2[dc]])
                            if not last_prompt:
                                if dc == 0:
                                    S.op(S.act, lambda e: e.activation(out=Sb[:, 0, :], in_=Sf[:, h, 0, :], func=AF.Copy),
                                         reads=[Sfres2[0]], writes=[Sbres2[0]])
                                else:
                                    S.op(S.act, lambda e: e.activation(out=Sb[:, 1, :], in_=Sf[:, h, 1, :], func=AF.Copy),
                                         reads=[Sfres2[1]], writes=[Sbres2[1]])
                        if last_prompt:
                            S.dma(S.sp, o_p[h].rearrange("(p two) e -> p two e", two=2), Sf[:, h, :, :],
                                  reads=Sfres2, is_output=True)

                    def st_N(ci, L, psO, psOr):
                        k2 = ci % 2
                        S.op(S.act, lambda e: e.activation(out=junk[0:L, :], in_=psO[0:L, :], func=AF.Square,
                                                           accum_out=ssq[0:L, 0:1]),
                             reads=[psOr], writes=[jres, ssres])
                        S.op(S.act, lambda e: e.activation(out=ssq[0:L, 1:2], in_=ssq[0:L, 0:1], func=AF.Ln,
                                                           bias=self.epsv[0:L, 0:1], scale=1.0 / 512.0),
                             reads=[ssres, self.cres, rcst], writes=[ssres])
                        S.op(S.act, lambda e: e.activation(out=ssq[0:L, 1:2], in_=ssq[0:L, 1:2], func=AF.Exp, scale=-0.5),
                             reads=[ssres], writes=[ssres])
                        S.op(S.act, lambda e: e.activation(out=onb2[k2][0:L, :], in_=psO[0:L, :], func=AF.Copy,
                                                           scale=ssq[0:L, 1:2]),
                             reads=[psOr, ssres], writes=[onb2r[k2]])

                    def st_T(ci, lo, L):
                        cs = slice(lo, lo + L)
                        k2 = ci % 2
                        psT, psTr = self.bank()
                        pt = psT[:].bitcast(BF16)
                        for dvc in range(4):
                            S.op(S.pe, lambda e: e.transpose(pt[:, dvc * 128:dvc * 128 + L], onb2[k2][0:L, dvc * 128:(dvc + 1) * 128],
                                                             self.identb[0:L, 0:L]),
                                 reads=[onb2r[k2], self.cres, rcst], writes=[psTr], inc=(dvc == 3))
                        S.op(S.dve, lambda e: e.tensor_tensor(
                            out=gT[:, :, cs], in0=pt[:, 0:512].rearrange("p (c l) -> p c l", c=4)[:, :, 0:L],
                            in1=gT[:, :, cs], op=ALU.mult),
                             reads=[psTr, gres], writes=[gres])

                    pch = [(ci, lo, L) for ci, (lo, L, smp) in enumerate(chunks) if not smp]
                    smp_chunk = [(ci, lo, L) for ci, (lo, L, smp) in enumerate(chunks) if smp]
                    sstate = {}
                    if smp_chunk:
                        sci, slo, sL = smp_chunk[0]
                        scs = slice(slo, slo + sL)
                        psK, psKr = self.bank()
                        pk = psK[:].bitcast(BF16)
                        for dc in range(2):
                            S.op(S.pe, lambda e: e.transpose(pk[0:sL, dc * 128:(dc + 1) * 128], kT[:, dc, scs], self.identb[:, :]),
                                 reads=[kres, self.cres, rcst], writes=[psKr], inc=(dc == 1))
                        S.op(S.dve, lambda e: e.scalar_tensor_tensor(
                            out=kmask[:, :, :], in0=pk[0:16, 0:256].unsqueeze(1).to_broadcast([16, 16, 256]),
                            scalar=gam, in1=self.delta16[0:16, :].unsqueeze(2).to_broadcast([16, 16, 256]),
                            op0=ALU.mult, op1=ALU.mult),
                             reads=[psKr, self.cres, rcst], writes=[kmres])
                        S.op(S.dve, lambda e: e.tensor_tensor(
                            out=qmask[:, :, :, :], in0=qT[:, :, scs].unsqueeze(2).to_broadcast([128, 2, 16, 16]),
                            in1=self.deltaR[:, :, :].unsqueeze(1).to_broadcast([128, 2, 16, 16]), op=ALU.mult),
                             reads=[qres, self.cres, rcst], writes=[qmres])
                        psOs, psOsr, psOsi = self.reserve_bank()

                        def sload(bq):
                            st_, sr_ = sring[bq % 3]
                            S.dma(S.sp, st_[:], s_in[bq, h].rearrange("(p two) e -> p two e", two=2), writes=[sr_])

                        def sample_step(b):
                            st, sr = sring[b % 3]
                            if b + 2 < NSMP:
                                sload(b + 2)
                            for dc in range(2):
                                ps2, ps2r = self.bank()
                                S.op(S.pe, lambda e: e.matmul(ps2[:, :], lhsT=kmask[0:16, b, dc * 128:(dc + 1) * 128],
                                                              rhs=vS[0:16, sci, :], start=True, stop=True),
                                     reads=[kmres, vres], writes=[ps2r])
                                S.op(S.dve, lambda e: e.scalar_tensor_tensor(out=st[:, dc, :], in0=st[:, dc, :],
                                                                             scalar=gam, in1=ps2[:, :],
                                                                             op0=ALU.mult, op1=ALU.add),
                                     reads=[ps2r, sr], writes=[sr])
                            S.dma(S.sp, o_s[b, h].rearrange("(p two) e -> p two e", two=2), st[:], reads=[sr],
                                  is_output=True)
                            S.op(S.act, lambda e: e.activation(out=sbf[:], in_=st[:], func=AF.Copy),
                                 reads=[sr], writes=[sbfres])
                            for dc in range(2):
                                last = (b == NSMP - 1 and dc == 1)
                                S.op(S.pe, lambda e: e.matmul(psOs[0:16, :], lhsT=qmask[:, dc, b, :], rhs=sbf[:, dc, :],
                                                              start=(b == 0 and dc == 0), stop=last),
                                     reads=[qmres, sbfres], writes=[psOsr], inc=(dc == 1))

                        sload(0)
                        sload(1)
                        sstate["next"] = 0

                    def samples(n):
                        if not smp_chunk:
                            return
                        for _ in range(n):
                            if sstate["next"] < NSMP:
                                sample_step(sstate["next"])
                                sstate["next"] += 1

                    st_SK(*pch[0])
                    prevT = None
                    for idx, (ci, lo, L) in enumerate(pch):
                        if idx + 1 < len(pch):
                            st_SK(*pch[idx + 1])
                        psO, psOr = st_O(ci, lo, L)
                        st_U(ci, lo, L, g == 1 and idx == len(pch) - 1)
                        st_N(ci, L, psO, psOr)
                        if prevT is not None:
                            st_T(*prevT)
                        prevT = (ci, lo, L)
                        samples(2)
                    st_T(*prevT)
                    if smp_chunk:
                        samples(NSMP)
                        self.breserved.discard(psOsi)
                        st_N(sci, sL, psOs, psOsr)
                        st_T(sci, slo, sL)
                    for half in range(2):
                        wb, wr = self.ws_next(128, 2048, kc=4)
                        for ti, (a, b) in enumerate(G["tiles"]):
                            w = b - a
                            la = a - c0
                            for oc in range(4):
                                ps, pr = self.bank()
                                for dvc in range(4):
                                    S.op(S.pe, lambda e: e.matmul(ps[:, 0:w], lhsT=wb[:, dvc, oc * 128:(oc + 1) * 128],
                                                                  rhs=gT[:, dvc, la:la + w],
                                                                  start=(dvc == 0), stop=(dvc == 3)),
                                         reads=[wr, gres], writes=[pr], inc=(dvc == 3))
                                ec = half * 4 + oc
                                S.op(S.dve, lambda e: e.tensor_tensor(out=self.hT[:, ec, a:b], in0=self.hT[:, ec, a:b],
                                                                      in1=ps[:, 0:w], op=ALU.add),
                                     reads=[pr], writes=[self.hres[(g, ti)]])

    def build(self):
        nc, S = self.nc, self.S
        es = self.es
        xT = self.din("xT", [D, TT])
        self.din("vecs", [128, NVEC, 8])
        self.din("cos_t", [128, TT])
        self.din("sin_t", [128, TT])
        self.din("qkd_t", [8, TT])
        self.din("consts", [128, 5, 128])
        self.din("deltaR", [128, 16, 16])
        L = self.layers
        if L >= 1:
            self.din("s_ret0", [NSMP, 4, 256, 512])
            self.din("ret_w_in", [2, D, 6144])
            self.din("ret_w_out", [2, 2048, D])
            self.din("mlp_w1", [4, D, 4 * D])
            self.din("mlp_w2", [4, 4 * D, D])
        if L >= 2:
            self.din("s_shiftT", [D, NSMP])
            self.din("s_wkv", [NSMP, 16, 64, 64])
            self.din("rwkv_w_rkv", [3, D, D])
            self.din("rwkv_w_o", [D, D])
            self.din("rwkv_w1a1", [D, 128])
            self.din("rwkv_g1", [D, 160])
            self.din("rwkv_w2a2", [64, 2048])
            self.din("rwkv_g2", [160, D])
            self.din("ln_gb", [3, D])
            self.din("rconst", [128, 11, 128])
            self.wcache = self.nc.dram_tensor("wcache", [len(RW_BLOCKS), 128, 2048], BF16).ap()
            self.wcres = [Res("wc%d" % i) for i in range(len(RW_BLOCKS))]
        if L >= 3:
            self.din("s_convT", [D, 2, NSMP])
            self.din("conv_w_in", [D, 3 * D])
            self.din("conv_w_out", [D, D])
        if L >= 4:
            self.din("s_ret3", [NSMP, 4, 256, 512])
        yT = self.dout("yT", [D, TT])
        if L >= 1:
            self.dout("ret0_p", [4, 256, 512])
            self.dout("ret0_s", [NSMP, 4, 256, 512])
        if L >= 2:
            self.dout("shiftT", [D, 1 + NSMP])
            self.dout("wkv_p", [16, 64, 64])
            self.dout("wkv_s", [NSMP, 16, 64, 64])
        if L >= 3:
            self.dout("convT_p", [D, 2])
            self.dout("convT_s", [D, 2, NSMP])
        if L >= 4:
            self.dout("ret3_p", [4, 256, 512])
            self.dout("ret3_s", [NSMP, 4, 256, 512])

        self.hT = self.sb(es, "hT", [128, NCH, TT], F32)
        self.hres = {}
        for g, G in enumerate(GROUPS):
            for ti in range(len(G["tiles"])):
                self.hres[(g, ti)] = Res("h%d_%d" % (g, ti))
        self.vecs = self.sb(es, "vecs", [128, NVEC, 8], F32)
        self.identb = self.sb(es, "identb", [128, 128], BF16)
        self.onesD = self.sb(es, "onesD", [128, 128], BF16)
        self.epsv = self.sb(es, "epsv", [128, 1], F32)
        self.cres = Res("consts")
        self.sqb = [(self.sb(es, "sq", [128, 512], BF16), Res("sq")) for _ in range(2)]
        self.rsb = (self.sb(es, "rs", [128, 512], F32), Res("rs"))
        self.wbufs = [(self.sb(es, "wbf", [128, 2048], BF16), Res("wbf")) for _ in range(NRING)]
        self.banks = [es.enter_context(nc.psum_tensor("bank%d" % i, [128, 512], F32)) for i in range(8)]
        self.bres = [Res("bank%d" % i) for i in range(8)]
        self.bnext = 0
        self.breserved = set()
        self.wplan = []
        self.w_dma = self.w_cv = self.w_use = 0

        for li in range(self.layers):
            kind = li % 3
            if kind == 0:
                self.plan_ret(li // 3)
            elif kind == 1:
                self.plan_rwkv()
            else:
                self.plan_conv()
            for g in range(2):
                self.plan_mlp(li)

        es_c = contextlib.ExitStack()
        cst = self.sb(es_c, "cst", [128, 5, 128], F32)
        S.dma(S.sp, self.hT[:], xT.rearrange("(c p) t -> p c t", p=128),
              writes=[self.hres[k] for k in self.hres])
        S.dma(S.sp, self.vecs[:], self.dram["vecs"][:, :, :], writes=[self.cres])
        S.dma(S.sp, cst[:], self.dram["consts"][:, :, :], writes=[self.cres])
        S.op(S.pool, lambda e: e.tensor_copy(out=self.identb[:], in_=cst[:, 0, :]), reads=[self.cres], writes=[self.cres])
        S.op(S.pool, lambda e: e.tensor_copy(out=self.onesD[:], in_=cst[:, 1, :]), reads=[self.cres], writes=[self.cres])
        S.op(S.pool, lambda e: e.memset(self.epsv[:], EPS), writes=[self.cres])
        es_c.close()

        for li in range(self.layers):
            kind = li % 3
            if kind == 0:
                self.ret_layer(li, li // 3)
            elif kind == 1:
                self.rwkv_layer(li)
            else:
                self.conv_layer(li)
            self.mlp_layer(li)

        with contextlib.ExitStack() as es2:
            yo = [(self.sb(es2, "yo", [128, NCH, 512], F32), S.new_res("yo")) for _ in range(2)]
            k = 0
            for g, G in enumerate(GROUPS):
                for ti, (a, b) in enumerate(G["tiles"]):
                    yb, yr = yo[k % 2]
                    k += 1
                    self.rmsnorm_tile(a, b, [self.hres[(g, ti)]], V_NFIN, lambda c: yb[:, c, 0:b - a], yr)
                    S.dma(S.sp, yT.rearrange("(c p) t -> p c t", p=128)[:, :, a:b], yb[:, :, 0:b - a], reads=[yr],
                          is_output=True)
        S.finish()
        es.close()
        S.close()
        return nc

    def hres_for(self, a, b):
        out = []
        for g, G in enumerate(GROUPS):
            for ti, (x, y) in enumerate(G["tiles"]):
                if x < b and a < y:
                    out.append(self.hres[(g, ti)])
        return out

    def rw_src(self, i):
        d = self.dram
        if i < 12:
            j, blk = divmod(i, 4)
            return d["rwkv_w_rkv"][j].rearrange("(c p) e -> p c e", p=128)[:, :, blk * 256:(blk + 1) * 256]
        if i < 16:
            blk = i - 12
            return d["rwkv_w_o"].rearrange("(c p) e -> p c e", p=128)[:, :, blk * 256:(blk + 1) * 256]
        if i == 16:
            return d["rwkv_w1a1"].rearrange("(c p) e -> p c e", p=128)
        if i == 17:
            return d["rwkv_g1"].rearrange("(c p) e -> p c e", p=128)
        if i == 18:
            return d["rwkv_w2a2"][:, :]
        if i == 19:
            return d["rwkv_g2"][0:128, :]
        return d["rwkv_g2"][128:160, :]

    def plan_rwkv(self):
        for i, (p, n) in enumerate(RW_BLOCKS):
            self.ws_add(self.rw_src(i), p, n)
        for t in range(17):
            for i in RW_ORDER:
                p, n = RW_BLOCKS[i]
                self.wplan.append(dict(ap=None, p=p, n=n, dst=None, dst_res=None, cached=i))

    def rwkv_layer(self, li):
        S = self.S
        C0 = math.exp(-0.5)
        with contextlib.ExitStack() as es:
            sb = lambda n, shp, dt: self.sb(es, n, shp, dt)
            R = S.new_res
            for i, (p, n) in enumerate(RW_BLOCKS):
                wb, wr = self.ws_next(p, n)
                S.dma(S.sp, self.wcache[i, 0:p, 0:n], wb, reads=[wr], writes=[self.wcres[i]])
            rc = sb("rconst", [128, 11, 128], F32)
            rcr = R("rconst")
            S.dma(S.sp, rc[:], self.dram["rconst"][:, :, :], writes=[rcr])
            mask_su = lambda L: rc[0:L, 3, 0:L]
            mask2 = lambda L: rc[0:L, 3:5, 0:L]
            mask_sl = lambda L: rc[0:L, 5, 0:L]
            mask_un = lambda L: rc[0:L, 6, 0:L]
            blockones = rc[:, 7, :]
            headsel = rc[:, 8, 0:2]
            id64 = rc[:, 8, 2:66]
            selS = rc[0:32, 8, 66:82]
            identf = rc[:, 9, :]
            lng_b = sb("lng_b", [128, D], F32)
            lnb_b = sb("lnb_b", [128, D], F32)
            w0_b = sb("w0_b", [128, D], F32)
            S.dma(S.sp, lng_b[:], self.dram["ln_gb"][0, :].partition_broadcast(128), writes=[rcr])
            S.dma(S.sp, lnb_b[:], self.dram["ln_gb"][1, :].partition_broadcast(128), writes=[rcr])
            S.dma(S.sp, w0_b[:], self.dram["ln_gb"][2, :].partition_broadcast(128), writes=[rcr])
            gnv = sb("gnv", [128, 1], F32)
            S.op(S.pool, lambda e: e.memset(gnv[:], GN_EPS), writes=[rcr])
            xnf = sb("xnf", [128, NCH, 129], F32)
            xx = sb("xx", [128, NCH, 128], F32)
            xm = [sb("xm", [128, NCH, 128], BF16) for _ in range(2)]
            RB = sb("RB", [128, NCH, 128], F32)
            KB = sb("KB", [128, NCH, 128], F32)
            KKB = sb("KKB", [128, NCH, 128], F32)
            AB = sb("AB", [128, NCH, 128], F32)
            GB = sb("GB", [128, NCH, 128], BF16)
            thw = sb("thw", [64, 128], BF16)
            la = sb("la", [64, 128], BF16)
            sg = sb("sg", [128, 128], BF16)
            sg2 = sb("sg2", [32, 128], BF16)
            SIGW = sb("SIGW", [128, D], F32)
            Vf = sb("Vf", [128, D], F32)
            Vb = sb("Vb", [128, D], BF16)
            yT = sb("yT", [128, NCH, 128], BF16)
            Hf = sb("Hf", [128, NCH, 128], F32)
            Hb = sb("Hb", [128, NCH, 128], BF16)
            shS = sb("shS", [128, NCH, NSMP], F32)
            vTs = sb("vTs", [128, NCH, NSMP], F32)
            (xnr, xxr, RBr, KBr, KKBr, ABr, GBr, lorar, SIGWr, Vfr, Vbr, yTr, shr, vTsr) = [
                R(n) for n in "xnf xx RB KB KKB AB GB lora SIGW Vf Vb yT shS vTs".split()]
            Hfr = [R("Hf%d" % i) for i in range(NCH)]
            Hbr = [R("Hb%d" % i) for i in range(NCH)]
            xmr = [R("xm0"), R("xm1")]
            S.dma(S.sp, shS[:], self.dram["s_shiftT"].rearrange("(c p) b -> p c b", p=128), writes=[shr])
            S.op(S.pool, lambda e: e.memset(Hf[:], 0.0), writes=Hfr)
            S.op(S.pool, lambda e: e.memset(Hb[:], 0.0), writes=Hbr)
            S.op(S.pool, lambda e: e.memset(xnf[:, :, 0:1], 0.0), writes=[xnr])

            def mm(out, lhsT, rhs, reads, writes, start=True, stop=True, inc=True, **kw):
                return S.op(S.pe, lambda e: e.matmul(out, lhsT=lhsT, rhs=rhs, start=start, stop=stop, **kw),
                            reads=reads, writes=writes, inc=inc)

            def act(out, in_, func, reads, writes, **kw):
                return S.op(S.act, lambda e: e.activation(out=out, in_=in_, func=func, **kw), reads=reads, writes=writes)

            def tt(eng, out, in0, in1, op, reads, writes):
                return S.op(eng, lambda e: e.tensor_tensor(out=out, in0=in0, in1=in1, op=op), reads=reads, writes=writes)

            def stt(eng, out, in0, scalar, in1, op0, op1, reads, writes):
                return S.op(eng, lambda e: e.scalar_tensor_tensor(out=out, in0=in0, scalar=scalar, in1=in1,
                                                                 op0=op0, op1=op1), reads=reads, writes=writes)

            def bc(ap, shape, axis):
                return ap.unsqueeze(axis).to_broadcast(shape)

            es2 = contextlib.ExitStack()
            sb2 = lambda n, shp, dt: self.sb(es2, n, shp, dt)
            ART = sb2("ART", [128, NCH, 2, 128], BF16)
            BH = sb2("BH", [128, NCH, 128], BF16)
            KH = sb2("KH", [128, NCH, 128], BF16)
            KBt = [sb2("KBt", [128, 2, 128], BF16) for _ in range(2)]
            A_tok = sb2("A_tok", [128, D], BF16)
            K_tok = sb2("K_tok", [128, D], BF16)
            B_tok = sb2("B_tok", [128, D], BF16)
            e3s = [(sb2("e3", [128, 3, 128], F32), R("e3")) for _ in range(2)]
            eis = [(sb2("ei", [128, 128], F32), R("ei")) for _ in range(2)]
            gC = sb2("gC", [128, NCH], F32)
            BS = sb2("BS", [128, 16], F32)
            st = sb2("st", [128, 64], F32)
            Ytok = sb2("Ytok", [128, D], F32)
            NHB = 2 * PBATCH
            PM = [sb2("PM", [128, 2, 128], CH_DT) for _ in range(NHB)]
            PN = [sb2("PN", [128, 128], CH_DT) for _ in range(NHB)]
            N0 = [sb2("N0", [128, 128], CH_DT) for _ in range(NHB)]
            N0r = [R("N0") for _ in range(NHB)]
            identr = sb2("identr", [128, 128], CH_DT)
            S.op(S.pool, lambda e: e.tensor_copy(out=identr[:], in_=rc[:, 9, :]), reads=[rcr], writes=[rcr])
            MTb = [sb2("MTb", [128, 128], BF16) for _ in range(NHB)]
            AKRK = [sb2("AKRK", [128, 2, 128], BF16) for _ in range(NHB)]
            ARB = [sb2("ARB", [128, 128], BF16) for _ in range(NHB)]
            Psb = [sb2("Psb", [128, 64], BF16) for _ in range(NHB)]
            Gp = [sb2("Gp", [128, 128], F32) for _ in range(PBATCH)]
            WTs = [sb2("WTs", [128, 128], BF16) for _ in range(PBATCH)]
            Us = [sb2("Us", [128, 128], BF16) for _ in range(PBATCH)]
            (ARTr, BHr, KHr, Atr, Ktr, Btr, gCr, BSr, str_, Ytr) = [
                R(n) for n in "ART BH KH A_tok K_tok B_tok gC BS st Ytok".split()]
            KBtr = [R("KBt0"), R("KBt1")]
            PMr = [R("PM") for _ in range(NHB)]
            PNr = [R("PN") for _ in range(NHB)]
            MTbr = [R("MTb") for _ in range(NHB)]
            AKRKr = [R("AKRK") for _ in range(NHB)]
            ARBr = [R("ARB") for _ in range(NHB)]
            Psbr = [R("Psb") for _ in range(NHB)]
            Gpr = [R("Gp") for _ in range(PBATCH)]
            WTsr = [R("WTs") for _ in range(PBATCH)]
            Usr = [R("Us") for _ in range(PBATCH)]

            def tile_dims(ti_):
                last_ = (ti_ == 16)
                a_ = ti_ * 128
                W_ = 32 if last_ else 128
                L_ = 16 if last_ else 128
                return last_, a_, W_, L_, a_ + W_

            def build_xm(j, k, W_):
                for c in range(NCH):
                    stt(S.dve, xm[k][:, c, 0:W_], xx[:, c, 0:W_], self.vecs[:, V_RMIX + j, c:c + 1],
                        xnf[:, c, 1:1 + W_], ALU.mult, ALU.add, [xxr, xnr, self.cres], [xmr[k]])

            def prep(ti_):
                last_, a_, W_, L_, b_ = tile_dims(ti_)
                hrs_ = self.hres_for(a_, b_)
                if ti_ > 0:
                    S.op(S.dve, lambda e: e.tensor_copy(out=xnf[:, :, 0:1], in_=xnf[:, :, 128:129]),
                         reads=[xnr], writes=[xnr])
                self.rmsnorm_tile(a_, b_, hrs_, V_NMIX + li, lambda c: xnf[:, c, 1:1 + W_], xnr)
                tt(S.pool, xx[:, :, 0:L_], xnf[:, :, 0:L_], xnf[:, :, 1:L_ + 1], ALU.subtract, [xnr], [xxr])
                if last_:
                    tt(S.pool, xx[:, :, 16:32], shS[:, :, :], xnf[:, :, 17:33], ALU.subtract, [xnr, shr], [xxr])
                    S.dma(S.sp, self.dram["shiftT"].rearrange("(c p) t -> p c t", p=128), xnf[:, :, 16:33],
                          reads=[xnr], is_output=True)
                build_xm(0, 0, W_)
                build_xm(1, 1, W_)

            prep(0)
            for ti in range(17):
                last, a, W, L, b = tile_dims(ti)
                hrs = self.hres_for(a, b)
                for j, (dst, dr) in enumerate(((RB, RBr), (KB, KBr))):
                    for blk in range(4):
                        wb, wr = self.ws_next(128, 2048, kc=8)
                        for oc in range(2):
                            ps, pr = self.bank()
                            for c in range(NCH):
                                mm(ps[:, 0:W], wb[:, c, oc * 128:(oc + 1) * 128], xm[j][:, c, 0:W], [wr, xmr[j]], [pr],
                                   start=(c == 0), stop=(c == NCH - 1), inc=(c == NCH - 1))
                            act(dst[:, blk * 2 + oc, 0:W], ps[:, 0:W], AF.Copy, [pr], [dr])
                build_xm(3, 1, W)
                build_xm(4, 0, W)
                wb, wr = self.ws_next(128, 1024, kc=8)
                psW, psWr = self.bank()
                psA, psAr = self.bank()
                for c in range(NCH):
                    mm(psW[0:64, 0:W], wb[:, c, 0:64], xm[1][:, c, 0:W], [wr, xmr[1]], [psWr],
                       start=(c == 0), stop=(c == NCH - 1), inc=(c == NCH - 1))
                for c in range(NCH):
                    mm(psA[0:64, 0:W], wb[:, c, 64:128], xm[0][:, c, 0:W], [wr, xmr[0]], [psAr],
                       start=(c == 0), stop=(c == NCH - 1), inc=(c == NCH - 1))
                act(thw[0:64, 0:W], psW[0:64, 0:W], AF.Tanh, [psWr], [lorar])
                act(la[0:64, 0:W], psA[0:64, 0:W], AF.Copy, [psAr], [lorar])
                wb, wr = self.ws_next(64, 2048)
                for half in range(2):
                    ps, pr = self.bank()
                    mm(ps[0:W, :], thw[0:64, 0:W], wb[0:64, half * 512:(half + 1) * 512], [wr, lorar], [pr])
                    tt(S.dve, SIGW[0:W, half * 512:(half + 1) * 512], ps[0:W, :], w0_b[0:W, half * 512:(half + 1) * 512],
                       ALU.add, [pr, rcr], [SIGWr])
                act(SIGW[0:W, :], SIGW[0:W, :], AF.Sigmoid, [SIGWr], [SIGWr])
                for fc in range(NCH):
                    ps, pr = self.bank()
                    mm(ps[:, 0:W], wb[0:64, 1024 + fc * 128:1024 + (fc + 1) * 128], la[0:64, 0:W], [wr, lorar], [pr])
                    act(AB[:, fc, 0:W], ps[:, 0:W], AF.Sigmoid, [pr, self.cres], [ABr], bias=self.vecs[:, V_A0, fc:fc + 1])
                build_xm(2, 0, W)
                build_xm(5, 1, W)
                for blk in range(4):
                    wb, wr = self.ws_next(128, 2048, kc=8)
                    ps, pr = self.bank()
                    for c in range(NCH):
                        mm(ps[0:W, 0:256], xm[0][:, c, 0:W], wb[:, c, :], [wr, xmr[0]], [pr],
                           start=(c == 0), stop=(c == NCH - 1), inc=(c == NCH - 1))
                    act(Vf[0:W, blk * 256:(blk + 1) * 256], ps[0:W, 0:256], AF.Copy, [pr], [Vfr])
                    if last:
                        for oc in range(2):
                            ps, pr = self.bank()
                            for c in range(NCH):
                                mm(ps[:, 0:NSMP], wb[:, c, oc * 128:(oc + 1) * 128], xm[0][:, c, 16:32], [wr, xmr[0]], [pr],
                                   start=(c == 0), stop=(c == NCH - 1), inc=(c == NCH - 1))
                            act(vTs[:, blk * 2 + oc, :], ps[:, 0:NSMP], AF.Copy, [pr], [vTsr])
                vb = lambda row: bc(self.vecs[:, row, :], [128, NCH, W], 2)
                SCR = xx
                tt(S.dve, KKB[:, :, 0:W], KB[:, :, 0:W], vb(V_KK), ALU.mult, [KBr, self.cres], [KKBr])
                tt(S.pool, SCR[:, :, 0:W], KKB[:, :, 0:W], KKB[:, :, 0:W], ALU.mult, [KKBr], [xxr])
                for half in range(2):
                    ps, pr = self.bank()
                    pv = ps[:, 0:4 * W].rearrange("p (c w) -> p c w", c=4)
                    mm(pv, blockones, SCR[:, half * 4:(half + 1) * 4, 0:W], [xxr, rcr], [pr])
                    S.op(S.dve, lambda e: e.tensor_scalar(out=SCR[:, half * 4:(half + 1) * 4, 0:W], in0=pv, scalar1=1e-24,
                                                          scalar2=None, op0=ALU.max), reads=[pr], writes=[xxr])
                act(SCR[:, :, 0:W], SCR[:, :, 0:W], AF.Ln, [xxr], [xxr])
                act(SCR[:, :, 0:W], SCR[:, :, 0:W], AF.Exp, [xxr], [xxr], scale=-0.5)
                tt(S.dve, KKB[:, :, 0:W], KKB[:, :, 0:W], SCR[:, :, 0:W], ALU.mult, [KKBr, xxr], [KKBr])
                stt(S.dve, SCR[:, :, 0:W], AB[:, :, 0:W], -1.0, vb(V_KA), ALU.add, ALU.mult, [ABr, self.cres], [xxr])
                stt(S.dve, KB[:, :, 0:W], SCR[:, :, 0:W], 1.0, KB[:, :, 0:W], ALU.add, ALU.mult, [xxr, KBr], [KBr])
                tt(S.pool, AB[:, :, 0:W], KKB[:, :, 0:W], AB[:, :, 0:W], ALU.mult, [KKBr, ABr], [ABr])
                tt(S.pool, SCR[:, :, 0:W], RB[:, :, 0:W], vb(V_RK), ALU.mult, [RBr, self.cres], [xxr])
                tt(S.pool, SCR[:, :, 0:W], SCR[:, :, 0:W], KB[:, :, 0:W], ALU.mult, [xxr, KBr], [xxr])
                wb, wr = self.ws_next(128, 1280, kc=8)
                psG, psGr = self.bank()
                psG2, psG2r = self.bank()
                for c in range(NCH):
                    mm(psG[:, 0:W], wb[:, c, 0:128], xm[1][:, c, 0:W], [wr, xmr[1]], [psGr],
                       start=(c == 0), stop=(c == NCH - 1), inc=(c == NCH - 1))
                for c in range(NCH):
                    mm(psG2[0:32, 0:W], wb[:, c, 128:160], xm[1][:, c, 0:W], [wr, xmr[1]], [psG2r],
                       start=(c == 0), stop=(c == NCH - 1), inc=(c == NCH - 1))
                act(sg[:, 0:W], psG[:, 0:W], AF.Sigmoid, [psGr], [lorar])
                act(sg2[0:32, 0:W], psG2[0:32, 0:W], AF.Sigmoid, [psG2r], [lorar])
                wbA, wrA = self.ws_next(128, 1024)
                wbB, wrB = self.ws_next(32, 1024)
                for fc in range(NCH):
                    ps, pr = self.bank()
                    mm(ps[:, 0:W], wbA[:, fc * 128:(fc + 1) * 128], sg[:, 0:W], [wrA, lorar], [pr], start=True, stop=False, inc=False)
                    mm(ps[:, 0:W], wbB[0:32, fc * 128:(fc + 1) * 128], sg2[0:32, 0:W], [wrB, lorar], [pr], start=False, stop=True)
                    act(GB[:, fc, 0:W], ps[:, 0:W], AF.Copy, [pr], [GBr])
                S.op(S.pool, lambda e: e.tensor_copy(out=Vb[0:W, :], in_=Vf[0:W, :]), reads=[Vfr], writes=[Vbr])
                psB, psBr = self.bank()
                for c in range(NCH):
                    mm(psB[0:L, 2 * c:2 * c + 2], SCR[:, c, 0:L], headsel, [xxr, rcr], [psBr], inc=(c == NCH - 1))
                act(BS[0:L, :], psB[0:L, 0:16], AF.Copy, [psBr], [BSr])
                if ti + 1 < 17:
                    prep(ti + 1)
                psKt, psKtr, iKt = self.reserve_bank()
                psBt, psBtr, iBt = self.reserve_bank()
                pkt = psKt[:].bitcast(BF16)
                pbt = psBt[:].bitcast(BF16)
                p3s = {}

                def dec_mm(fc_):
                    ps_, pr_ = self.bank()
                    p3_ = ps_[:, 0:3 * L].rearrange("p (c w) -> p c w", c=3)
                    mm(p3_, SIGW[0:L, fc_ * 128:(fc_ + 1) * 128], rc[0:L, 0:3, 0:L], [SIGWr, rcr], [pr_])
                    p3s[fc_] = (p3_, pr_)

                dec_mm(0)
                for fc in range(NCH):
                    if fc + 1 < NCH:
                        dec_mm(fc + 1)
                    p3, pr = p3s.pop(fc)
                    e3, e3r = e3s[fc % 2]
                    ei, eir = eis[fc % 2]
                    act(e3[:, :, 0:L], p3, AF.Exp, [pr], [e3r])
                    act(ei[:, 0:L], p3[:, 0, :], AF.Exp, [pr], [eir], scale=-1.0)
                    act(gC[:, fc:fc + 1], e3[:, 0, L - 1:L], AF.Copy, [e3r], [gCr])
                    tt(S.dve, ART[:, fc, 0, 0:L], KKB[:, fc, 0:L], e3[:, 1, 0:L], ALU.mult, [KKBr, e3r], [ARTr])
                    tt(S.dve, ART[:, fc, 1, 0:L], RB[:, fc, 0:L], e3[:, 0, 0:L], ALU.mult, [RBr, e3r], [ARTr])
                    tt(S.pool, BH[:, fc, 0:L], AB[:, fc, 0:L], ei[:, 0:L], ALU.mult, [ABr, eir], [BHr])
                    tt(S.pool, KH[:, fc, 0:L], KB[:, fc, 0:L], ei[:, 0:L], ALU.mult, [KBr, eir], [KHr])
                    kbt, kbtr = KBt[fc % 2], KBtr[fc % 2]
                    tt(S.dve, kbt[:, 0, 0:L], KB[:, fc, 0:L], e3[:, 2, 0:L], ALU.mult, [KBr, e3r], [kbtr])
                    stt(S.dve, kbt[:, 1, 0:L], AB[:, fc, 0:L], -1.0, e3[:, 2, 0:L], ALU.mult, ALU.mult, [ABr, e3r], [kbtr])
                    S.op(S.pe, lambda e: e.transpose(pkt[0:L, fc * 128:(fc + 1) * 128], kbt[:, 0, 0:L], self.identb[:, :]),
                         reads=[kbtr, self.cres], writes=[psKtr], inc=False)
                    S.op(S.pe, lambda e: e.transpose(pbt[0:L, fc * 128:(fc + 1) * 128], kbt[:, 1, 0:L], self.identb[:, :]),
                         reads=[kbtr, self.cres], writes=[psBtr], inc=True)
                act(K_tok[0:L, :], pkt[0:L, :], AF.Copy, [psKtr], [Ktr])
                act(B_tok[0:L, :], pbt[0:L, :], AF.Copy, [psBtr], [Btr])
                self.breserved.discard(iKt)
                self.breserved.discard(iBt)
                psAt, psAtr = self.bank()
                pat = psAt[:].bitcast(BF16)
                for fc in range(NCH):
                    S.op(S.pe, lambda e: e.transpose(pat[0:L, fc * 128:(fc + 1) * 128], ART[:, fc, 0, 0:L], self.identb[:, :]),
                         reads=[ARTr, self.cres], writes=[psAtr], inc=(fc == NCH - 1))
                act(A_tok[0:L, :], pat[0:L, :], AF.Copy, [psAtr], [Atr])
                nlev = int(round(math.log2(L))) - 1
                for p0 in range(0, NCH, PBATCH):
                    pairs = list(range(p0, min(NCH, p0 + PBATCH)))
                    heads = [(p, j) for p in pairs for j in range(2)]
                    slot = {hj: i for i, hj in enumerate(heads)}

                    def hv(p, j):
                        rows = slice(64 * j, 64 * j + 64)
                        hd = 2 * p + j
                        return rows, slice(hd * 64, hd * 64 + 64)

                    for (p, j) in heads:
                        k = slot[(p, j)]
                        rows, hs = hv(p, j)
                        arT = ART[rows, p, :, 0:L]
                        aT = ART[rows, p, 0, 0:L]
                        bT = BH[rows, p, 0:L]
                        kT = KH[rows, p, 0:L]
                        pA, pAr = self.bank()
                        pAv = pA[0:L, 0:2 * L].rearrange("p (c w) -> p c w", c=2)
                        mm(pAv, bT, arT, [BHr, ARTr], [pAr])
                        pB, pBr = self.bank()
                        pBv = pB[0:L, 0:2 * L].rearrange("p (c w) -> p c w", c=2)
                        mm(pBv, kT, arT, [KHr, ARTr], [pBr])
                        pC, pCr = self.bank()
                        mm(pC[0:L, 0:L], aT, bT, [ARTr, BHr], [pCr])
                        tt(S.dve, PM[k][0:L, 0, 0:L], pAv[:, 0, :], mask_su(L), ALU.mult, [pAr, rcr], [PMr[k]])
                        tt(S.dve, ARB[k][0:L, 0:L], pAv[:, 1, :], mask_un(L), ALU.mult, [pAr, rcr], [ARBr[k]])
                        tt(S.dve, AKRK[k][0:L, :, 0:L], pBv, mask2(L), ALU.mult, [pBr, rcr], [AKRKr[k]])
                        tt(S.dve, N0[k][0:L, 0:L], pC[0:L, 0:L], mask_sl(L), ALU.mult, [pCr, rcr], [N0r[k]])
                        tt(S.pool, PM[k][0:L, 1, 0:L], identf[0:L, 0:L], PM[k][0:L, 0, 0:L], ALU.subtract, [PMr[k], rcr], [PMr[k]])
                    for (p, j) in heads:
                        k = slot[(p, j)]
                        p1, p1r = self.bank()
                        p2, p2r = self.bank()
                        mm(p1[0:L, 0:L], PM[k][0:L, 0, 0:L], N0[k][0:L, 0:L], [PMr[k], N0r[k]], [p1r])
                        mm(p2[0:L, 0:L], N0[k][0:L, 0:L], PM[k][0:L, 0, 0:L], [PMr[k], N0r[k]], [p2r])
                        act(PN[k][0:L, 0:L], p1[0:L, 0:L], AF.Copy, [p1r], [PNr[k]])
                        act(PM[k][0:L, 0, 0:L], p2[0:L, 0:L], AF.Copy, [p2r], [PMr[k]])
                    for lev in range(1, nlev + 1):
                        for (p, j) in heads:
                            k = slot[(p, j)]
                            if lev < nlev:
                                pX, pXr = self.bank()
                                pXv = pX[0:L, 0:2 * L].rearrange("p (c w) -> p c w", c=2)
                                mm(pXv, PN[k][0:L, 0:L], PM[k][0:L, :, 0:L], [PMr[k], PNr[k]], [pXr])
                                pY, pYr = self.bank()
                                mm(pY[0:L, 0:L], PM[k][0:L, 0, 0:L], PN[k][0:L, 0:L], [PMr[k], PNr[k]], [pYr])
                                act(PM[k][0:L, 0, 0:L], pXv[:, 0, :], AF.Copy, [pXr], [PMr[k]])
                                tt(S.dve, PM[k][0:L, 1, 0:L], PM[k][0:L, 1, 0:L], pXv[:, 1, :], ALU.add, [pXr, PMr[k]], [PMr[k]])
                                act(PN[k][0:L, 0:L], pY[0:L, 0:L], AF.Copy, [pYr], [PNr[k]])
                            else:
                                pX, pXr = self.bank()
                                mm(pX[0:L, 0:L], PN[k][0:L, 0:L], PM[k][0:L, 1, 0:L], [PMr[k], PNr[k]], [pXr])
                                tt(S.dve, PM[k][0:L, 1, 0:L], PM[k][0:L, 1, 0:L], pX[0:L, 0:L], ALU.add, [pXr, PMr[k]], [PMr[k]])
                    for (p, j) in heads:
                        k = slot[(p, j)]
                        pTb, pTr = self.bank()
                        pTv = pTb[:].bitcast(CH_DT)
                        S.op(S.pe, lambda e: e.transpose(pTv[0:L, 0:L], PM[k][0:L, 1, 0:L], identr[0:L, 0:L]),
                             reads=[PMr[k], rcr], writes=[pTr])
                        pZ, pZr = self.bank()
                        mm(pZ[0:L, 0:L], N0[k][0:L, 0:L], PM[k][0:L, 1, 0:L], [N0r[k], PMr[k]], [pZr])
                        act(PN[k][0:L, 0:L], pTv[0:L, 0:L], AF.Copy, [pTr], [PNr[k]])
                        tt(S.pool, PM[k][0:L, 0, 0:L], rc[0:L, 10, 0:L], PM[k][0:L, 1, 0:L], ALU.subtract, [PMr[k], rcr], [PMr[k]])
                        tt(S.dve, PM[k][0:L, 0, 0:L], PM[k][0:L, 0, 0:L], pZ[0:L, 0:L], ALU.subtract, [pZr, PMr[k]], [PMr[k]])
                    for (p, j) in heads:
                        k = slot[(p, j)]
                        rows, hs = hv(p, j)
                        pM1, pM1r = self.bank()
                        mm(pM1[0:L, 0:L], PN[k][0:L, 0:L], PM[k][0:L, 0, 0:L], [PNr[k], PMr[k]], [pM1r])
                        pP, pPr = self.bank()
                        mm(pP[0:L, 0:64], AKRK[k][0:L, 0, 0:L], Vb[0:L, hs], [AKRKr[k], Vbr], [pPr])
                        act(MTb[k][0:L, 0:L], pM1[0:L, 0:L], AF.Copy, [pM1r], [MTbr[k]])
                        act(Psb[k][0:L, :], pP[0:L, 0:64], AF.Copy, [pPr], [Psbr[k]])
                    for p in pairs:
                        q = p - p0
                        psWT, psWTr, iWT = self.reserve_bank()
                        pG, pGr = self.bank()
                        for j in range(2):
                            k = slot[(p, j)]
                            rows, hs = hv(p, j)
                            mm(pG[0:L, 64 * j:64 * j + 64], MTb[k][0:L, 0:L], Psb[k][0:L, :], [MTbr[k], Psbr[k]], [pGr])
                            if j == 0:
                                mm(psWT[0:64, 0:L], A_tok[0:L, hs], MTb[k][0:L, 0:L], [Atr, MTbr[k]], [psWTr])
                            else:
                                mm(psWT[64:128, 0:L], A_tok[0:L, hs], MTb[k][0:L, 0:L], [Atr, MTbr[k]], [psWTr],
                                   tile_position=(0, 64))
                        act(Gp[q][0:L, :], pG[0:L, 0:128], AF.Copy, [pGr], [Gpr[q]])
                        act(WTs[q][:, 0:L], psWT[:, 0:L], AF.Copy, [psWTr], [WTsr[q]])
                        self.breserved.discard(iWT)
                    for p in pairs:
                        q = p - p0
                        pU, pUr = self.bank()
                        mm(pU[0:L, 0:128], WTs[q][:, 0:L], Hb[:, p, :], [WTsr[q], Hbr[p]], [pUr])
                        tt(S.dve, Us[q][0:L, :], pU[0:L, 0:128], Gp[q][0:L, :], ALU.add, [pUr, Gpr[q]], [Usr[q]])
                    for p in pairs:
                        q = p - p0
                        pYo, pYor = self.bank()
                        mm(pYo[0:L, 0:128], ART[:, p, 1, 0:L], Hb[:, p, :], [ARTr, Hbr[p]], [pYor], start=True, stop=False, inc=False)
                        for j in range(2):
                            k = slot[(p, j)]
                            rows, hs = hv(p, j)
                            cs = slice(64 * j, 64 * j + 64)
                            mm(pYo[0:L, cs], AKRK[k][0:L, 1, 0:L], Vb[0:L, hs], [AKRKr[k], Vbr], [pYor], start=False, stop=False, inc=False)
                            mm(pYo[0:L, cs], ARB[k][0:L, 0:L], Us[q][0:L, cs], [ARBr[k], Usr[q]], [pYor], start=False, stop=(j == 1),
                               inc=(j == 1))
                        pH, pHr = self.bank()
                        ps_ = slice(p * 128, (p + 1) * 128)
                        mm(pH[:, 0:128], K_tok[0:L, ps_], Vb[0:L, ps_], [Ktr, Vbr], [pHr], start=True, stop=False, inc=False)
                        mm(pH[:, 0:128], B_tok[0:L, ps_], Us[q][0:L, :], [Btr, Usr[q]], [pHr], start=False, stop=True)
                        act(Ytok[0:L, p * 128:(p + 1) * 128], pYo[0:L, 0:128], AF.Copy, [pYor], [Ytr])
                        for j in range(2):
                            rows = slice(64 * j, 64 * j + 64)
                            stt(S.dve, Hf[rows, p, rows], Hf[rows, p, rows], gC[rows, p:p + 1], pH[rows, rows], ALU.mult, ALU.add,
                                [pHr, Hfr[p], gCr], [Hfr[p]])
                        act(Hb[:, p, :], Hf[:, p, :], AF.Copy, [Hfr[p]], [Hbr[p]])
                Y3 = lambda L_: Ytok[0:L_, :].rearrange("p (h v) -> p h v", h=16)
                YSQ = SIGW
                S.op(S.dve, lambda e: e.tensor_reduce(out=st[0:L, 0:16], in_=Y3(L), axis=AX.X, op=ALU.add),
                     reads=[Ytr], writes=[str_])
                act(YSQ[0:L, :], Ytok[0:L, :], AF.Square, [Ytr], [SIGWr])
                S.op(S.dve, lambda e: e.tensor_reduce(out=st[0:L, 16:32], in_=YSQ[0:L, :].rearrange("p (h v) -> p h v", h=16),
                                                      axis=AX.X, op=ALU.add), reads=[SIGWr], writes=[str_])
                S.op(S.dve, lambda e: e.tensor_scalar(out=st[0:L, 0:16], in0=st[0:L, 0:16], scalar1=1.0 / 64.0, scalar2=None,
                                                      op0=ALU.mult), reads=[str_], writes=[str_])
                tt(S.dve, st[0:L, 32:48], st[0:L, 0:16], st[0:L, 0:16], ALU.mult, [str_], [str_])
                stt(S.dve, st[0:L, 32:48], st[0:L, 16:32], 1.0 / 64.0, st[0:L, 32:48], ALU.mult, ALU.subtract, [str_], [str_])
                act(st[0:L, 48:64], st[0:L, 32:48], AF.Ln, [str_, rcr], [str_], bias=gnv[0:L, 0:1])
                act(st[0:L, 48:64], st[0:L, 48:64], AF.Exp, [str_], [str_], scale=-0.5)
                tt(S.dve, Y3(L), Y3(L), bc(st[0:L, 0:16], [L, 16, 64], 2), ALU.subtract, [Ytr, str_], [Ytr])
                tt(S.dve, Y3(L), Y3(L), bc(st[0:L, 48:64], [L, 16, 64], 2), ALU.mult, [Ytr, str_], [Ytr])
                tt(S.dve, Ytok[0:L, :], Ytok[0:L, :], lng_b[0:L, :], ALU.mult, [Ytr, rcr], [Ytr])
                tt(S.dve, Ytok[0:L, :], Ytok[0:L, :], lnb_b[0:L, :], ALU.add, [Ytr, rcr], [Ytr])
                tt(S.pool, YSQ[0:L, :].rearrange("p (h v) -> p h v", h=16), Vf[0:L, :].rearrange("p (h v) -> p h v", h=16),
                   bc(BS[0:L, :], [L, 16, 64], 2), ALU.mult, [Vfr, BSr], [SIGWr])
                tt(S.dve, Ytok[0:L, :], Ytok[0:L, :], YSQ[0:L, :], ALU.add, [Ytr, SIGWr], [Ytr])
                for half in range(2):
                    ps, pr = self.bank()
                    for q4 in range(4):
                        fc = half * 4 + q4
                        S.op(S.pe, lambda e: e.transpose(ps[:, q4 * L:(q4 + 1) * L], Ytok[0:L, fc * 128:(fc + 1) * 128],
                                                         identf[0:L, 0:L]),
                             reads=[Ytr, rcr], writes=[pr], inc=(q4 == 3))
                    tt(S.dve, yT[:, half * 4:(half + 1) * 4, 0:L], ps[:, 0:4 * L].rearrange("p (c w) -> p c w", c=4),
                       GB[:, half * 4:(half + 1) * 4, 0:L], ALU.mult, [pr, GBr], [yTr])
                if last:
                    for j in range(2):
                        rows = slice(64 * j, 64 * j + 64)
                        S.dma(S.sp, self.dram["wkv_p"].rearrange("(p two) k v -> two k p v", two=2)[j],
                              Hf[rows, :, rows], reads=Hfr, is_output=True)
                    es2.close()
                    self.rwkv_samples(es, li, dict(RB=RB, KB=KB, KKB=KKB, AB=AB, GB=GB, SIGW=SIGW, vTs=vTs, SCR=SCR, yT=yT,
                                                   RBr=RBr, KBr=KBr, KKBr=KKBr, ABr=ABr, GBr=GBr, SIGWr=SIGWr, vTsr=vTsr,
                                                   SCRr=xxr, yTr=yTr, rc=rc, rcr=rcr, gnv=gnv))
                for blk in range(4):
                    wb, wr = self.ws_next(128, 2048, kc=8)
                    for oc in range(2):
                        ps, pr = self.bank()
                        for fc in range(NCH):
                            mm(ps[:, 0:W], wb[:, fc, oc * 128:(oc + 1) * 128], yT[:, fc, 0:W], [wr, yTr], [pr],
                               start=(fc == 0), stop=(fc == NCH - 1), inc=(fc == NCH - 1))
                        ec = blk * 2 + oc
                        tt(S.dve, self.hT[:, ec, a:b], self.hT[:, ec, a:b], ps[:, 0:W], ALU.add, [pr] + hrs, hrs)

    def rwkv_samples(self, es, li, T):
        S = self.S
        R = S.new_res
        rc, rcr = T["rc"], T["rcr"]
        blockones = rc[:, 7, :]
        id64 = rc[:, 8, 2:66]
        selS = rc[0:32, 8, 66:82]
        NB = 2
        sbx = lambda n, shp, dt: self.sb(es, n, shp, dt)
        DEC = sbx("DEC", [128, NCH, NSMP], F32)
        SS = sbx("SS", [128, NB, NCH, 64], F32)
        XD = sbx("XD", [128, NB, NCH, 64], F32)
        TT_ = sbx("TTs", [128, NB, NCH, 64], F32)
        XB = [sbx("XB", [128, NB, NCH, 64], F32) for _ in range(5)]
        sa = sbx("sa", [128, NB, NCH], F32)
        YS = sbx("YS", [128, NCH, NSMP], F32)
        YQ = sbx("YQ", [128, NCH, NSMP], F32)
        MU = sbx("MU", [128, NCH, NSMP], F32)
        RS = sbx("RS", [128, NCH, NSMP], F32)
        DECr, SSr, XDr, TTr, sar, YSr, YQr, MUr, RSr = [R(n) for n in "DEC SS XD TT sa YS YQ MU RS".split()]
        XBr = [R("XB%d" % i) for i in range(5)]

        def mm(out, lhsT, rhs, reads, writes, **kw):
            return S.op(S.pe, lambda e: e.matmul(out, lhsT=lhsT, rhs=rhs, start=True, stop=True, **kw), reads=reads, writes=writes)

        def act(out, in_, func, reads, writes, **kw):
            return S.op(S.act, lambda e: e.activation(out=out, in_=in_, func=func, **kw), reads=reads, writes=writes)

        def tt(eng, out, in0, in1, op, reads, writes):
            return S.op(eng, lambda e: e.tensor_tensor(out=out, in0=in0, in1=in1, op=op), reads=reads, writes=writes)

        sw = self.dram["s_wkv"].rearrange("b (p two) v k -> b two v p k", two=2)
        ow = self.dram["wkv_s"].rearrange("b (p two) v k -> b two v p k", two=2)
        for fc in range(NCH):
            ps, pr = self.bank()
            mm(ps[:, 0:NSMP], T["SIGW"][0:32, fc * 128:(fc + 1) * 128], selS, [T["SIGWr"], rcr], [pr])
            act(DEC[:, fc, :], ps[:, 0:NSMP], AF.Exp, [pr], [DECr])
        srcs = [(T["KKB"], T["KKBr"], 16), (DEC, DECr, 0), (T["AB"], T["ABr"], 16), (T["KB"], T["KBr"], 16), (T["RB"], T["RBr"], 16)]
        for b0 in range(0, NSMP, NB):
            for bb in range(NB):
                for j in range(2):
                    S.dma(S.sp, SS[64 * j:64 * j + 64, bb, :, :], sw[b0 + bb, j], writes=[SSr])
            for qi, (src, srcr, off) in enumerate(srcs):
                xin = src[:, :, off + b0:off + b0 + NB].rearrange("q p b -> q b p").unsqueeze(3).to_broadcast([128, NB, NCH, 64])
                idb = id64.unsqueeze(1).unsqueeze(1).to_broadcast([128, NB, NCH, 64])
                tt(S.dve, XD[:, :, :, :], xin, idb, ALU.mult, [srcr, rcr], [XDr])
                xdf = XD[:, :, :, :].rearrange("q b p k -> q (b p k)")
                xbf = XB[qi][:, :, :, :].rearrange("q b p k -> q (b p k)")
                for i in range(NB * NCH * 64 // 512):
                    ps, pr = self.bank()
                    mm(ps[:, :], blockones, xdf[:, i * 512:(i + 1) * 512], [XDr, rcr], [pr])
                    act(xbf[:, i * 512:(i + 1) * 512], ps[:, :], AF.Copy, [pr], [XBr[qi]])
            red = lambda out, in_, rd, wr: S.op(S.dve, lambda e: e.tensor_reduce(out=out, in_=in_, axis=AX.X, op=ALU.add),
                                                reads=rd, writes=wr)
            A4 = lambda t: t[:, :, :, :]
            tt(S.dve, A4(TT_), A4(SS), A4(XB[0]), ALU.mult, [SSr, XBr[0]], [TTr])
            red(sa[:, :, :], A4(TT_), [TTr], [sar])
            tt(S.dve, A4(SS), A4(SS), A4(XB[1]), ALU.mult, [SSr, XBr[1]], [SSr])
            tt(S.dve, A4(TT_), A4(XB[2]), sa[:, :, :].unsqueeze(3).to_broadcast([128, NB, NCH, 64]), ALU.mult, [XBr[2], sar, TTr], [TTr])
            tt(S.dve, A4(SS), A4(SS), A4(TT_), ALU.subtract, [SSr, TTr], [SSr])
            vsb = T["vTs"][:, :, b0:b0 + NB].rearrange("q p b -> q b p").unsqueeze(3).to_broadcast([128, NB, NCH, 64])
            tt(S.dve, A4(TT_), A4(XB[3]), vsb, ALU.mult, [XBr[3], T["vTsr"], TTr], [TTr])
            tt(S.dve, A4(SS), A4(SS), A4(TT_), ALU.add, [SSr, TTr], [SSr])
            tt(S.dve, A4(TT_), A4(SS), A4(XB[4]), ALU.mult, [SSr, XBr[4], TTr], [TTr])
            red(YS[:, :, b0:b0 + NB].rearrange("q p b -> q b p"), A4(TT_), [TTr], [YSr])
            for bb in range(NB):
                for j in range(2):
                    S.dma(S.sp, ow[b0 + bb, j], SS[64 * j:64 * j + 64, bb, :, :], reads=[SSr], is_output=True)
        fl = lambda t: t[:, :, :].rearrange("q p b -> q (p b)")
        ps1, ps1r = self.bank()
        mm(ps1[:, 0:128], blockones, fl(YS), [YSr, rcr], [ps1r])
        tt(S.pool, YQ[:, :, :], YS[:, :, :], YS[:, :, :], ALU.mult, [YSr], [YQr])
        ps2, ps2r = self.bank()
        mm(ps2[:, 0:128], blockones, fl(YQ), [YQr, rcr], [ps2r])
        act(fl(MU), ps1[:, 0:128], AF.Copy, [ps1r], [MUr], scale=1.0 / 64.0)
        tt(S.dve, fl(YQ), fl(MU), fl(MU), ALU.mult, [MUr, ps2r], [YQr])
        S.op(S.dve, lambda e: e.scalar_tensor_tensor(out=fl(RS), in0=ps2[:, 0:128], scalar=1.0 / 64.0, in1=fl(YQ),
                                                     op0=ALU.mult, op1=ALU.subtract), reads=[ps2r, YQr], writes=[RSr])
        act(fl(RS), fl(RS), AF.Ln, [RSr, rcr], [RSr], bias=T["gnv"][:, 0:1])
        act(fl(RS), fl(RS), AF.Exp, [RSr], [RSr], scale=-0.5)
        tt(S.dve, fl(YS), fl(YS), fl(MU), ALU.subtract, [YSr, MUr], [YSr])
        tt(S.dve, fl(YS), fl(YS), fl(RS), ALU.mult, [YSr, RSr], [YSr])
        vb = lambda row: self.vecs[:, row, :].unsqueeze(2).to_broadcast([128, NCH, NSMP])
        tt(S.dve, YS[:, :, :], YS[:, :, :], vb(V_LNG), ALU.mult, [YSr, self.cres], [YSr])
        tt(S.dve, YS[:, :, :], YS[:, :, :], vb(V_LNB), ALU.add, [YSr, self.cres], [YSr])
        S.op(S.pool, lambda e: e.tensor_copy(out=YQ[:, :, :], in_=T["SCR"][:, :, 16:32]), reads=[T["SCRr"]], writes=[YQr])
        ps3, ps3r = self.bank()
        mm(ps3[:, 0:128], blockones, fl(YQ), [YQr, rcr], [ps3r])
        tt(S.dve, fl(MU), ps3[:, 0:128], fl(T["vTs"]), ALU.mult, [ps3r, T["vTsr"]], [MUr])
        tt(S.dve, fl(YS), fl(YS), fl(MU), ALU.add, [YSr, MUr], [YSr])
        tt(S.dve, T["yT"][:, :, 16:32], YS[:, :, :], T["GB"][:, :, 16:32], ALU.mult, [YSr, T["GBr"]], [T["yTr"]])


_LAYERS = 4


def _const_tables():
    inv = (1.0 / (np.float32(10000.0) ** np.linspace(0.0, 1.0, 128, dtype=np.float32))).astype(np.float32)
    pos = np.concatenate([np.arange(TP), np.full(NSMP, 16384)]).astype(np.float32)
    ang = (pos[None, :] * inv[:, None]).astype(np.float32).astype(np.float64)
    cos_t = np.cos(ang).astype(np.float32)
    sin_t = np.sin(ang).astype(np.float32)
    l = np.concatenate([np.arange(TP) % 128, np.zeros(NSMP)]).astype(np.float64)
    qkd = np.zeros((8, TT), np.float64)
    for h in range(4):
        qkd[h] = GAMMAS[h] ** (l + 1.0)
        qkd[4 + h] = GAMMAS[h] ** (-(l + 1.0)) * (256.0 ** -0.5)
    consts = np.zeros((128, 5, 128), np.float32)
    consts[:, 0, :] = np.eye(128)
    consts[:, 1, :] = 1.0 / D
    m = np.arange(128)
    consts[:, 2, :] = (m[None, :] >= m[:, None])
    deltaR = np.broadcast_to(np.eye(16, dtype=np.float32)[None], (128, 16, 16)).copy()
    c0 = math.exp(-0.5)
    row = m[:, None]
    col = m[None, :]
    rconst = np.zeros((128, 11, 128), np.float32)
    rconst[:, 0, :] = -c0 * (row <= col)
    rconst[:, 1, :] = -c0 * (row < col)
    rconst[:, 2, :] = -c0 * (row > col)
    rconst[:, 3, :] = (col > row)
    rconst[:, 4, :] = (col >= row)
    rconst[:, 5, :] = (row > col)
    rconst[:, 6, :] = -1.0 * (col >= row)
    rconst[:, 7, :] = (row // 64 == col // 64)
    rconst[:, 8, 0:2] = (m[:, None] // 64 == np.arange(2)[None, :])
    rconst[:, 8, 2:66] = (m[:, None] % 64 == np.arange(64)[None, :])
    rconst[0:32, 8, 66:82] = -c0 * (np.arange(32)[:, None] == 16 + np.arange(16)[None, :])
    rconst[:, 9, :] = np.eye(128)
    rconst[:, 10, :] = 2.0 * np.eye(128)
    return cos_t, sin_t, qkd.astype(np.float32), consts, deltaR, rconst


def _vec_layout(v):
    return np.ascontiguousarray(np.asarray(v, np.float32).reshape(8, 128).T)


def kernel(**inp):
    f = lambda k: np.asarray(inp[k], np.float32)
    cos_t, sin_t, qkd, consts, deltaR, rconst = _const_tables()
    rows = ([f("norm_mix")[i] for i in range(4)] + [f("norm_mlp")[i] for i in range(4)] + [f("norm_final")]
            + [f("rwkv_mix")[i] for i in range(6)]
            + [f("rwkv_w0"), f("rwkv_a0"), f("rwkv_k_k"), f("rwkv_k_a"), f("rwkv_r_k").reshape(-1)]
            + [f("conv_w")[i] for i in range(3)] + [f("rwkv_ln_g"), f("rwkv_ln_b")])
    assert len(rows) == NVEC
    vecs = np.ascontiguousarray(np.stack([_vec_layout(r) for r in rows], axis=1))
    w_in = f("ret_w_in").copy()
    for base in (0, 1024):
        blk = w_in[:, :, base:base + 1024].reshape(2, D, 4, 128, 2)
        w_in[:, :, base:base + 1024] = blk.transpose(0, 1, 2, 4, 3).reshape(2, D, 1024)
    shared = {
        "vecs": vecs, "cos_t": cos_t, "sin_t": sin_t, "qkd_t": qkd, "consts": consts, "deltaR": deltaR,
        "ret_w_in": w_in, "ret_w_out": f("ret_w_out"), "rwkv_w_rkv": f("rwkv_w_rkv"), "rwkv_w_o": f("rwkv_w_o"),
        "conv_w_in": f("conv_w_in"), "conv_w_out": f("conv_w_out"), "mlp_w1": f("mlp_w1"), "mlp_w2": f("mlp_w2"),
        "rwkv_w1a1": np.ascontiguousarray(np.concatenate([f("rwkv_w1"), f("rwkv_a1")], axis=1)),
        "rwkv_g1": f("rwkv_g1"),
        "rwkv_w2a2": np.ascontiguousarray(np.concatenate([f("rwkv_w2"), f("rwkv_a2")], axis=1)),
        "rwkv_g2": f("rwkv_g2"),
        "ln_gb": np.ascontiguousarray(np.stack([f("rwkv_ln_g"), f("rwkv_ln_b"), f("rwkv_w0")])),
        "rconst": rconst,
    }
    xp, xs, meta = f("x_prompt"), f("x_sample"), f("meta_tokens")
    in_maps = []
    for b in range(N_CORES):
        sl = slice(NSMP * b, NSMP * (b + 1))
        xall = np.concatenate([meta, xp[b], xs[sl, 0, :]], axis=0)
        m = dict(shared)
        m["xT"] = np.ascontiguousarray(xall.T)
        m["s_ret0"] = np.ascontiguousarray(f("state_ret_l0")[sl])
        m["s_ret3"] = np.ascontiguousarray(f("state_ret_l3")[sl])
        m["s_shiftT"] = np.ascontiguousarray(f("state_rwkv_shift_l1")[sl].T)
        m["s_wkv"] = np.ascontiguousarray(f("state_rwkv_wkv_l1")[sl])
        m["s_convT"] = np.ascontiguousarray(f("state_conv_l2")[sl].transpose(2, 1, 0))
        in_maps.append(m)
    prog = Prog(layers=_LAYERS)
    nc = prog.build()
    in_maps = [{k: v for k, v in m.items() if k in prog.inputs} for m in in_maps]
    res = run_bass_kernel_spmd(nc, in_maps, core_ids=list(range(N_CORES)))
    R = res.results
    B = N_CORES
    y_p = np.stack([R[b]["yT"][:, 16:TP].T for b in range(B)])
    y_s = np.concatenate([R[b]["yT"][:, TP:TT].T for b in range(B)])[:, None, :]
    zz = {"ret0_p": (4, 256, 512), "ret3_p": (4, 256, 512), "ret0_s": (NSMP, 4, 256, 512), "ret3_s": (NSMP, 4, 256, 512),
          "shiftT": (D, 1 + NSMP), "wkv_p": (16, 64, 64), "wkv_s": (NSMP, 16, 64, 64), "convT_p": (D, 2),
          "convT_s": (D, 2, NSMP)}
    for b in range(B):
        for k, shp in zz.items():
            if k not in R[b]:
                R[b][k] = np.zeros(shp, np.float32)
    cat = lambda k: np.concatenate([R[b][k] for b in range(B)], axis=0)
    stk = lambda k: np.stack([R[b][k] for b in range(B)])
    shift_p = np.stack([R[b]["shiftT"][:, 0] for b in range(B)])
    shift_s = np.concatenate([R[b]["shiftT"][:, 1:].T for b in range(B)])
    conv_p = np.stack([R[b]["convT_p"].T for b in range(B)])
    conv_s = np.concatenate([R[b]["convT_s"].transpose(2, 1, 0) for b in range(B)])
    outs = (y_p, y_s, stk("ret0_p"), cat("ret0_s"), shift_p, shift_s, stk("wkv_p").transpose(0, 1, 3, 2), cat("wkv_s"),
            conv_p, conv_s, stk("ret3_p"), cat("ret3_s"))
    return tuple(np.ascontiguousarray(o, dtype=np.float32) for o in outs)
```

```python
import contextlib
import math
import numpy as np
import concourse.bass as bass
import concourse.mybir as mybir
from concourse.bass_utils import run_bass_kernel_spmd

F32 = mybir.dt.float32
F32R = mybir.dt.float32r
BF16 = mybir.dt.bfloat16
AF = mybir.ActivationFunctionType
ALU = mybir.AluOpType
AX = mybir.AxisListType

D = 1024
NCH = 8
TP = 2064
NSMP = 16
TT = TP + NSMP
EPS = 1e-6
GN_EPS = 64e-5
N_CORES = 8
GAMMAS = [1.0 - 2.0 ** (-5.0 - h) for h in range(4)]

GROUPS = [
    dict(c0=0, c1=1024, tiles=[(0, 512), (512, 1024)]),
    dict(c0=1024, c1=2080, tiles=[(1024, 1376), (1376, 1728), (1728, 2080)]),
]
V_NMIX, V_NMLP, V_NFIN, V_RMIX, V_W0, V_A0, V_KK, V_KA, V_RK, V_CW, V_LNG, V_LNB = 0, 4, 8, 9, 15, 16, 17, 18, 19, 20, 23, 24
NVEC = 25
RW_BLOCKS = [(128, 2048)] * 16 + [(128, 1024), (128, 1280), (64, 2048), (128, 1024), (32, 1024)]
RW_ORDER = list(range(8)) + [16, 18, 8, 9, 10, 11, 17, 19, 20, 12, 13, 14, 15]
NRING = 6
PBATCH = 3
CH_DT = F32R


class Res:
    __slots__ = ("w", "r", "name")

    def __init__(self, name="", r=None):
        self.w = None
        self.r = dict(r) if r else {}
        self.name = name


class SemCtr:
    __slots__ = ("sem", "count", "name")

    def __init__(self, sem, name):
        self.sem = sem
        self.count = 0
        self.name = name


class Eng:
    def __init__(self, h, name, self_sync):
        self.h = h
        self.ctr = None
        self.name = name
        self.known = {}
        self.self_sync = self_sync


SEM_LIMIT = 12000


class Sched:
    def __init__(self, nc, n_dma_sems=20):
        self.nc = nc
        self._cms = []
        self._nsem = 0
        self.pe = Eng(nc.tensor, "pe", False)
        self.act = Eng(nc.scalar, "act", True)
        self.dve = Eng(nc.vector, "dve", True)
        self.pool = Eng(nc.gpsimd, "pool", True)
        self.sp = Eng(nc.sync, "sp", False)
        self.engines = (self.pe, self.act, self.dve, self.pool)
        for e in self.engines:
            e.ctr = self._mk(e.name)
        self.dsems = [self._mk("dma%d" % i) for i in range(n_dma_sems)]
        self.dnext = 0
        self.out_dmas = []
        self.old_ctrs = []

    def _mk(self, name):
        cm = self.nc.semaphore("s%d_%s" % (self._nsem, name))
        self._nsem += 1
        s = cm.__enter__()
        self._cms.append(cm)
        return SemCtr(s, name)

    def close(self):
        for cm in reversed(self._cms):
            cm.__exit__(None, None, None)

    def new_res(self, name=""):
        r = {}
        for e in self.engines:
            if e.ctr.count:
                r[e.ctr] = e.ctr.count
        for c in self.dsems:
            if c.count:
                r[c] = c.count
        return Res(name, r)

    def _wait(self, eng, deps):
        need = {}
        for (ctr, val) in deps:
            if eng.ctr is ctr and not eng.self_sync:
                continue
            if need.get(ctr, 0) < val:
                need[ctr] = val
        for ctr, val in need.items():
            if eng.known.get(ctr, 0) < val:
                eng.h.wait_ge(ctr.sem, val)
                eng.known[ctr] = val

    @staticmethod
    def _deps(reads, writes):
        deps = []
        for r in reads:
            if r.w is not None:
                deps.append(r.w)
        for w in writes:
            if w.w is not None:
                deps.append(w.w)
            deps.extend(w.r.items())
        return deps

    @staticmethod
    def _mark(reads, writes, tag):
        ctr, val = tag
        for r in reads:
            if r.r.get(ctr, 0) < val:
                r.r[ctr] = val
        for w in writes:
            w.w = tag
            w.r = {}

    def op(self, eng, fn, reads=(), writes=(), inc=True):
        self._wait(eng, self._deps(reads, writes))
        ins = fn(eng.h)
        if inc:
            eng.ctr.count += 1
            ins.then_inc(eng.ctr.sem, 1)
            tag = (eng.ctr, eng.ctr.count)
        else:
            tag = (eng.ctr, eng.ctr.count + 1)
        self._mark(reads, writes, tag)
        if inc and eng.ctr.count >= SEM_LIMIT:
            eng.ctr = self._mk(eng.name)
        return ins

    def dma(self, eng, out, in_, reads=(), writes=(), is_output=False):
        ctr = self.dsems[self.dnext]
        if ctr.count >= SEM_LIMIT:
            self.old_ctrs.append(ctr)
            prev = (ctr, ctr.count)
            ctr = self._mk("dma%d" % self.dnext)
            self.dsems[self.dnext] = ctr
        else:
            prev = (ctr, ctr.count) if ctr.count else None
        self.dnext = (self.dnext + 1) % len(self.dsems)
        deps = self._deps(reads, writes)
        if prev is not None:
            deps.append(prev)
        self._wait(eng, deps)
        ctr.count += 16
        eng.h.dma_start(out=out, in_=in_).then_inc(ctr.sem, 16)
        tag = (ctr, ctr.count)
        self._mark(reads, writes, tag)
        if is_output:
            self.out_dmas.append(tag)
        return tag

    def finish(self):
        deps = list(self.out_dmas)
        for c in list(self.dsems) + self.old_ctrs:
            if c.count:
                deps.append((c, c.count))
        for e in self.engines:
            if e.ctr.count:
                deps.append((e.ctr, e.ctr.count))
        self._wait(self.sp, deps)


class Prog:
    def __init__(self, layers=4, with_rwkv=True):
        self.layers = layers
        self.with_rwkv = with_rwkv
        self.nc = bass.Bass("TRN2", target_bir_lowering=False)
        self.S = Sched(self.nc)
        self.es = contextlib.ExitStack()
        self.dram = {}
        self.inputs = set()
        self._uid = 0

    def din(self, name, shape, dt=F32):
        t = self.nc.dram_tensor(name, list(shape), dt, kind="ExternalInput").ap()
        self.dram[name] = t
        self.inputs.add(name)
        return t

    def dout(self, name, shape, dt=F32):
        t = self.nc.dram_tensor(name, list(shape), dt, kind="ExternalOutput").ap()
        self.dram[name] = t
        return t

    def sb(self, es, name, shape, dt):
        self._uid += 1
        return es.enter_context(self.nc.sbuf_tensor("%s_%d" % (name, self._uid), list(shape), dt))

    def bank(self):
        for _ in range(8):
            i = self.bnext
            self.bnext = (self.bnext + 1) % 8
            if i not in self.breserved:
                return self.banks[i], self.bres[i]
        raise RuntimeError("no psum bank")

    def reserve_bank(self):
        bk, r = self.bank()
        i = self.banks.index(bk)
        self.breserved.add(i)
        return bk, r, i

    def ws_add(self, ap, p, n, dst=None, dst_res=None):
        self.wplan.append(dict(ap=ap, p=p, n=n, dst=dst, dst_res=dst_res))

    def _ws_conv(self):
        i = self.w_cv
        spec = self.wplan[i]
        bf, ores = self.wbufs[i % len(self.wbufs)]
        p, n = spec["p"], spec["n"]
        if spec.get("cached") is not None:
            ci = spec["cached"]
            self.S.dma(self.S.sp, bf[0:p, 0:n], self.wcache[ci, 0:p, 0:n], reads=[self.wcres[ci]], writes=[ores])
        else:
            src = spec["ap"]
            dst = bf[0:p, 0:n]
            if len(src.shape) == 3:
                dst = dst.rearrange("p (c e) -> p c e", c=src.shape[1])
            self.S.dma(self.S.pool, dst, src, writes=[ores])
        self.w_cv += 1

    def ws_next(self, p, n, kc=None):
        i = self.w_use
        spec = self.wplan[i]
        assert spec["p"] == p and spec["n"] == n, (i, spec["p"], spec["n"], p, n)
        while self.w_cv <= min(i + len(self.wbufs) - 2, len(self.wplan) - 1):
            self._ws_conv()
        self.w_use += 1
        bf, r = self.wbufs[i % len(self.wbufs)]
        v = bf[0:p, 0:n]
        if kc is not None:
            v = v.rearrange("p (c e) -> p c e", c=kc)
        return v, r

    def rmsnorm_tile(self, c0, c1, hres, gi, out_fn, out_res, out_f32_fn=None):
        S = self.S
        w = c1 - c0
        ps, pr = self.bank()
        for c in range(NCH):
            sq, sqr = self.sqb[c % 2]
            S.op(S.act, lambda e: e.activation(out=sq[:, 0:w], in_=self.hT[:, c, c0:c1], func=AF.Square),
                 reads=hres, writes=[sqr])
            S.op(S.pe, lambda e: e.matmul(ps[:, 0:w], lhsT=self.onesD[:], rhs=sq[:, 0:w],
                                          start=(c == 0), stop=(c == NCH - 1)),
                 reads=[sqr, self.cres], writes=[pr], inc=True)
        rs, rr = self.rsb
        S.op(S.act, lambda e: e.activation(out=rs[:, 0:w], in_=ps[:, 0:w], func=AF.Ln, bias=self.epsv[:, 0:1]),
             reads=[pr, self.cres], writes=[rr])
        S.op(S.act, lambda e: e.activation(out=rs[:, 0:w], in_=rs[:, 0:w], func=AF.Exp, scale=-0.5),
             reads=[rr], writes=[rr])
        for c in range(NCH):
            S.op(S.dve, lambda e: e.scalar_tensor_tensor(out=out_fn(c), in0=self.hT[:, c, c0:c1],
                                                         scalar=self.vecs[:, gi, c:c + 1], in1=rs[:, 0:w],
                                                         op0=ALU.mult, op1=ALU.mult),
                 reads=hres + [rr, self.cres], writes=[out_res])

    def norm_group(self, g, gi):
        G = GROUPS[g]
        for ti, (a, b) in enumerate(G["tiles"]):
            la = a - G["c0"]
            self.rmsnorm_tile(a, b, [self.hres[(g, ti)]], gi,
                              lambda c: self.xn[:, c, la:la + (b - a)], self.xnres[ti])

    def plan_mlp(self, li):
        w1 = self.dram["mlp_w1"]
        w2 = self.dram["mlp_w2"]
        for blk in range(16):
            self.ws_add(w1[li].rearrange("(c p) e -> p c e", p=128)[:, :, blk * 256:(blk + 1) * 256], 128, 2048)
        for cb in range(4):
            for kg in range(4):
                self.ws_add(w2[li, kg * 1024:(kg + 1) * 1024, cb * 256:(cb + 1) * 256]
                            .rearrange("(c p) e -> p c e", p=128), 128, 2048)

    def mlp_group(self, li, g, h1, h1res, rl):
        S = self.S
        G = GROUPS[g]
        c0 = G["c0"]
        self.norm_group(g, V_NMLP + li)
        for blk in range(16):
            wb, wr = self.ws_next(128, 2048, kc=8)
            for ti, (a, b) in enumerate(G["tiles"]):
                w = b - a
                for oc in range(2):
                    ps, pr = self.bank()
                    for c in range(NCH):
                        S.op(S.pe, lambda e: e.matmul(ps[:, 0:w], lhsT=wb[:, c, oc * 128:(oc + 1) * 128],
                                                      rhs=self.xn[:, c, a - c0:b - c0],
                                                      start=(c == 0), stop=(c == NCH - 1)),
                             reads=[wr, self.xnres[ti]], writes=[pr], inc=(c == NCH - 1))
                    rt, rtr = rl[(blk * 2 + oc) % 2]
                    S.op(S.act, lambda e: e.activation(out=rt[:, 0:w], in_=ps[:, 0:w], func=AF.Relu),
                         reads=[pr], writes=[rtr])
                    fc = blk * 2 + oc
                    S.op(S.dve, lambda e: e.tensor_tensor(out=h1[:, fc, a - c0:b - c0], in0=rt[:, 0:w],
                                                          in1=rt[:, 0:w], op=ALU.mult),
                         reads=[rtr], writes=[h1res[ti]])
        for cb in range(4):
            accs = {}
            for ti in range(len(G["tiles"])):
                for oc in range(2):
                    accs[(ti, oc)] = self.reserve_bank()
            for kg in range(4):
                wb, wr = self.ws_next(128, 2048, kc=8)
                for ti, (a, b) in enumerate(G["tiles"]):
                    w = b - a
                    for oc in range(2):
                        ps, pr, _ = accs[(ti, oc)]
                        for c in range(NCH):
                            first = (kg == 0 and c == 0)
                            last = (kg == 3 and c == NCH - 1)
                            S.op(S.pe, lambda e: e.matmul(ps[:, 0:w], lhsT=wb[:, c, oc * 128:(oc + 1) * 128],
                                                          rhs=h1[:, kg * 8 + c, a - c0:b - c0],
                                                          start=first, stop=last),
                                 reads=[wr, h1res[ti]], writes=[pr], inc=(c == NCH - 1))
            for ti, (a, b) in enumerate(G["tiles"]):
                w = b - a
                for oc in range(2):
                    ps, pr, bi = accs[(ti, oc)]
                    ec = cb * 2 + oc
                    S.op(S.dve, lambda e: e.tensor_tensor(out=self.hT[:, ec, a:b], in0=self.hT[:, ec, a:b],
                                                          in1=ps[:, 0:w], op=ALU.add),
                         reads=[pr], writes=[self.hres[(g, ti)]])
                    self.breserved.discard(bi)

    def mlp_layer(self, li):
        with contextlib.ExitStack() as es:
            self.xn = self.sb(es, "xn", [128, NCH, 1056], BF16)
            self.xnres = [self.S.new_res("xn%d" % i) for i in range(3)]
            h1 = self.sb(es, "h1", [128, 32, 1056], BF16)
            h1res = [self.S.new_res("h1_%d" % i) for i in range(3)]
            rl = [(self.sb(es, "relu", [128, 512], F32), self.S.new_res("relu")) for _ in range(2)]
            for g in range(2):
                self.mlp_group(li, g, h1, h1res, rl)

    def plan_conv(self):
        wi = self.dram["conv_w_in"].rearrange("(c p) e -> p c e", p=128)
        wo = self.dram["conv_w_out"].rearrange("(c p) e -> p c e", p=128)
        for g in range(2):
            for fp in range(4):
                for part in range(3):
                    self.ws_add(wi[:, :, part * 1024 + fp * 256: part * 1024 + (fp + 1) * 256], 128, 2048)
            for blk in range(4):
                self.ws_add(wo[:, :, blk * 256:(blk + 1) * 256], 128, 2048)

    def conv_layer(self, li):
        S = self.S
        with contextlib.ExitStack() as es:
            self.xn = self.sb(es, "xn", [128, NCH, 1056], BF16)
            self.xnres = [self.S.new_res("xn%d" % i) for i in range(3)]
            bB = self.sb(es, "cv_b", [128, 2, 1056], F32)
            cB = self.sb(es, "cv_c", [128, 2, 1056], F32)
            uB = self.sb(es, "cv_u", [128, 2, 1058], F32)
            yB = self.sb(es, "cv_y", [128, 1056], F32)
            zT = self.sb(es, "cv_z", [128, 8, 1056], BF16)
            carry = self.sb(es, "cv_carry", [128, 8, 2], F32)
            st = self.sb(es, "cv_st", [128, 8, 2, NSMP], F32)
            snew = self.sb(es, "cv_snew", [128, 8, 2, NSMP], F32)
            pout = self.sb(es, "cv_pout", [128, 8, 2], F32)
            bres, cres_, ures, yres, zres, carres, stres, snres, pores = [S.new_res(n) for n in
                                                                          "b c u y z carry st snew pout".split()]
            S.dma(S.sp, st[:], self.dram["s_convT"].rearrange("(c p) j b -> p c j b", p=128), writes=[stres])
            S.op(S.pool, lambda e: e.memset(carry[:], 0.0), writes=[carres])
            for g in range(2):
                G = GROUPS[g]
                c0 = G["c0"]
                tg = G["c1"] - c0
                tp = tg - (NSMP if g == 1 else 0)
                self.norm_group(g, V_NMIX + li)
                for fp in range(4):
                    for part, (buf, br) in enumerate(((bB, bres), (cB, cres_), (None, None))):
                        wb, wr = self.ws_next(128, 2048, kc=8)
                        for ti, (a, b) in enumerate(G["tiles"]):
                            w = b - a
                            la = a - c0
                            for oc in range(2):
                                ps, pr = self.bank()
                                for c in range(NCH):
                                    S.op(S.pe, lambda e: e.matmul(ps[:, 0:w], lhsT=wb[:, c, oc * 128:(oc + 1) * 128],
                                                                  rhs=self.xn[:, c, la:la + w],
                                                                  start=(c == 0), stop=(c == NCH - 1)),
                                         reads=[wr, self.xnres[ti]], writes=[pr], inc=(c == NCH - 1))
                                if part < 2:
                                    S.op(S.act, lambda e: e.activation(out=buf[:, oc, la:la + w], in_=ps[:, 0:w],
                                                                       func=AF.Copy),
                                         reads=[pr], writes=[br])
                                else:
                                    S.op(S.dve, lambda e: e.tensor_tensor(out=uB[:, oc, 2 + la:2 + la + w],
                                                                          in0=ps[:, 0:w], in1=cB[:, oc, la:la + w],
                                                                          op=ALU.mult),
                                         reads=[pr, cres_], writes=[ures])
                    for oc in range(2):
                        fc = fp * 2 + oc
                        cw = lambda j: self.vecs[:, V_CW + j, fc:fc + 1]
                        S.op(S.dve, lambda e: e.tensor_copy(out=uB[:, oc, 0:2], in_=carry[:, fc, :]),
                             reads=[carres], writes=[ures])
                        S.op(S.dve, lambda e: e.tensor_scalar(out=yB[:, 0:tp], in0=uB[:, oc, 0:tp], scalar1=cw(0),
                                                              scalar2=None, op0=ALU.mult),
                             reads=[ures, self.cres], writes=[yres])
                        S.op(S.dve, lambda e: e.scalar_tensor_tensor(out=yB[:, 0:tp], in0=uB[:, oc, 1:tp + 1],
                                                                     scalar=cw(1), in1=yB[:, 0:tp],
                                                                     op0=ALU.mult, op1=ALU.add),
                             reads=[ures, yres, self.cres], writes=[yres])
                        S.op(S.dve, lambda e: e.scalar_tensor_tensor(out=yB[:, 0:tp], in0=uB[:, oc, 2:tp + 2],
                                                                     scalar=cw(2), in1=yB[:, 0:tp],
                                                                     op0=ALU.mult, op1=ALU.add),
                             reads=[ures, yres, self.cres], writes=[yres])
                        if g == 1:
                            S.op(S.dve, lambda e: e.tensor_scalar(out=yB[:, tp:tg], in0=st[:, fc, 0, :], scalar1=cw(0),
                                                                  scalar2=None, op0=ALU.mult),
                                 reads=[stres, self.cres], writes=[yres])
                            S.op(S.dve, lambda e: e.scalar_tensor_tensor(out=yB[:, tp:tg], in0=st[:, fc, 1, :],
                                                                         scalar=cw(1), in1=yB[:, tp:tg],
                                                                         op0=ALU.mult, op1=ALU.add),
                                 reads=[stres, yres, self.cres], writes=[yres])
                            S.op(S.dve, lambda e: e.scalar_tensor_tensor(out=yB[:, tp:tg], in0=uB[:, oc, 2 + tp:2 + tg],
                                                                         scalar=cw(2), in1=yB[:, tp:tg],
                                                                         op0=ALU.mult, op1=ALU.add),
                                 reads=[ures, yres, self.cres], writes=[yres])
                            S.op(S.act, lambda e: e.activation(out=snew[:, fc, 0, :], in_=st[:, fc, 1, :], func=AF.Copy),
                                 reads=[stres], writes=[snres])
                            S.op(S.act, lambda e: e.activation(out=snew[:, fc, 1, :], in_=uB[:, oc, 2 + tp:2 + tg],
                                                               func=AF.Copy),
                                 reads=[ures], writes=[snres])
                            S.op(S.act, lambda e: e.activation(out=pout[:, fc, :], in_=uB[:, oc, tp:tp + 2], func=AF.Copy),
                                 reads=[ures], writes=[pores])
                        else:
                            S.op(S.act, lambda e: e.activation(out=carry[:, fc, :], in_=uB[:, oc, tp:tp + 2], func=AF.Copy),
                                 reads=[ures], writes=[carres])
                        S.op(S.dve, lambda e: e.tensor_tensor(out=zT[:, fc, 0:tg], in0=yB[:, 0:tg], in1=bB[:, oc, 0:tg],
                                                              op=ALU.mult),
                             reads=[yres, bres], writes=[zres])
                for blk in range(4):
                    wb, wr = self.ws_next(128, 2048, kc=8)
                    for ti, (a, b) in enumerate(G["tiles"]):
                        w = b - a
                        la = a - c0
                        for oc in range(2):
                            ps, pr = self.bank()
                            for c in range(NCH):
                                S.op(S.pe, lambda e: e.matmul(ps[:, 0:w], lhsT=wb[:, c, oc * 128:(oc + 1) * 128],
                                                              rhs=zT[:, c, la:la + w],
                                                              start=(c == 0), stop=(c == NCH - 1)),
                                     reads=[wr, zres], writes=[pr], inc=(c == NCH - 1))
                            ec = blk * 2 + oc
                            S.op(S.dve, lambda e: e.tensor_tensor(out=self.hT[:, ec, a:b], in0=self.hT[:, ec, a:b],
                                                                  in1=ps[:, 0:w], op=ALU.add),
                                 reads=[pr], writes=[self.hres[(g, ti)]])
            S.dma(S.sp, self.dram["convT_p"].rearrange("(c p) j -> p c j", p=128), pout[:], reads=[pores],
                  is_output=True)
            S.dma(S.sp, self.dram["convT_s"].rearrange("(c p) j b -> p c j b", p=128), snew[:], reads=[snres],
                  is_output=True)

    def plan_ret(self, j):
        win = self.dram["ret_w_in"][j].rearrange("(c p) e -> p c e", p=128)
        wout = self.dram["ret_w_out"][j]
        for g in range(2):
            for h in range(4):
                self.ws_add(win[:, :, h * 256:(h + 1) * 256], 128, 2048)
                self.ws_add(win[:, :, 1024 + h * 256:1024 + (h + 1) * 256], 128, 2048)
                for half in range(2):
                    o = 2048 + h * 512 + half * 256
                    self.ws_add(win[:, :, o:o + 256], 128, 2048)
                for half in range(2):
                    o = 4096 + h * 512 + half * 256
                    self.ws_add(win[:, :, o:o + 256], 128, 2048)
                for half in range(2):
                    self.ws_add(wout[h * 512:(h + 1) * 512, half * 512:(half + 1) * 512]
                                .rearrange("(c p) e -> p c e", p=128), 128, 2048)

    def ret_layer(self, li, j):
        S = self.S
        s_in = self.dram["s_ret%d" % li]
        o_p = self.dram["ret%d_p" % li]
        o_s = self.dram["ret%d_s" % li]
        with contextlib.ExitStack() as es:
            self.xn = self.sb(es, "xn", [128, NCH, 1056], BF16)
            self.xnres = [self.S.new_res("xn%d" % i) for i in range(3)]
            self.causT = self.sb(es, "causT", [128, 128], F32)
            self.delta16 = self.sb(es, "delta16", [128, 16], F32)
            self.deltaR = self.sb(es, "deltaR", [128, 16, 16], F32)
            rcst = S.new_res("retconst")
            S.dma(S.sp, self.causT[:], self.dram["consts"][:, 2, :], writes=[rcst])
            S.dma(S.sp, self.delta16[:], self.dram["consts"][:, 0, 0:16], writes=[rcst])
            S.dma(S.sp, self.deltaR[:], self.dram["deltaR"][:, :, :], writes=[rcst])
            cosT = self.sb(es, "cos", [128, 1056], F32)
            sinT = self.sb(es, "sin", [128, 1056], F32)
            qd = self.sb(es, "qd", [128, 512], F32)
            kd = self.sb(es, "kd", [128, 512], F32)
            qT = self.sb(es, "qT", [128, 2, 1056], BF16)
            kT = self.sb(es, "kT", [128, 2, 1056], BF16)
            gT = self.sb(es, "gT", [128, 4, 1056], BF16)
            vS = self.sb(es, "vS", [128, 10, 512], BF16)
            Sf = self.sb(es, "Sf", [128, 4, 2, 512], F32)
            Sb = self.sb(es, "Sb", [128, 2, 512], BF16)
            sring = [(self.sb(es, "sring", [128, 2, 512], F32), S.new_res("sring")) for _ in range(3)]
            sbf = self.sb(es, "sbf", [128, 2, 512], BF16)
            tmp = [(self.sb(es, "rtmp", [128, 512], F32), S.new_res("rtmp")) for _ in range(3)]
            sT2 = [self.sb(es, "sT", [128, 128], BF16) for _ in range(2)]
            onb2 = [self.sb(es, "onb", [128, 512], BF16) for _ in range(2)]
            kTok2 = [self.sb(es, "kTok", [128, 256], BF16) for _ in range(2)]
            sT2r = [S.new_res("sT") for _ in range(2)]
            onb2r = [S.new_res("onb") for _ in range(2)]
            kTok2r = [S.new_res("kTok") for _ in range(2)]
            Sfres2 = [S.new_res("Sf0"), S.new_res("Sf1")]
            Sbres2 = [S.new_res("Sb0"), S.new_res("Sb1")]
            kmask = self.sb(es, "kmask", [16, 16, 256], BF16)
            qmask = self.sb(es, "qmask", [128, 2, 16, 16], BF16)
            ssq = self.sb(es, "ssq", [128, 2], F32)
            junk = self.sb(es, "junk", [128, 512], BF16)
            (tabres, qdres, kdres, qres, kres, gres, vres, Sfres, Sbres, sbfres, sTres, onres, kTokres, kmres, qmres,
             ssres, jres) = [S.new_res(n) for n in
                             "tab qd kd q k g v Sf Sb sbf sT on kTok kmask qmask ss junk".split()]
            S.op(S.pool, lambda e: e.memset(Sf[:], 0.0), writes=Sfres2)
            for g in range(2):
                G = GROUPS[g]
                c0, c1 = G["c0"], G["c1"]
                tg = c1 - c0
                self.norm_group(g, V_NMIX + li)
                S.dma(S.sp, cosT[:, 0:tg], self.dram["cos_t"][:, c0:c1], writes=[tabres])
                S.dma(S.sp, sinT[:, 0:tg], self.dram["sin_t"][:, c0:c1], writes=[tabres])
                chunks = [(i * 128, 128, False) for i in range(8)]
                if g == 1:
                    chunks += [(1024, 16, False), (1040, 16, True)]
                for h in range(4):
                    gam = GAMMAS[h]
                    if g == 0:
                        S.op(S.pool, lambda e: e.memset(Sb[:], 0.0), writes=Sbres2)
                    else:
                        S.op(S.act, lambda e: e.activation(out=Sb[:], in_=Sf[:, h, :, :], func=AF.Copy),
                             reads=Sfres2, writes=Sbres2)
                    for (dst, dres, dtab, drow, tres) in ((qT, qres, qd, h, qdres), (kT, kres, kd, 4 + h, kdres)):
                        wb, wr = self.ws_next(128, 2048, kc=8)
                        for ti, (a, b) in enumerate(G["tiles"]):
                            w = b - a
                            la = a - c0
                            S.dma(S.sp, dtab[:, 0:w], self.dram["qkd_t"][drow, a:b].partition_broadcast(128), writes=[tres])
                            pA, pAr = self.bank()
                            pB, pBr = self.bank()
                            for (pp, ppr, o) in ((pA, pAr, 0), (pB, pBr, 128)):
                                for c in range(NCH):
                                    S.op(S.pe, lambda e: e.matmul(pp[:, 0:w], lhsT=wb[:, c, o:o + 128],
                                                                  rhs=self.xn[:, c, la:la + w],
                                                                  start=(c == 0), stop=(c == NCH - 1)),
                                         reads=[wr, self.xnres[ti]], writes=[ppr], inc=(c == NCH - 1))
                            (t1, r1), (t2, r2), (t3, r3) = tmp
                            co = cosT[:, la:la + w]
                            si = sinT[:, la:la + w]
                            dd = dtab[:, 0:w]
                            tt = lambda e, o_, a_, b_, op_: e.tensor_tensor(out=o_, in0=a_, in1=b_, op=op_)
                            S.op(S.dve, lambda e: tt(e, t1[:, 0:w], pA[:, 0:w], co, ALU.mult), reads=[pAr, tabres], writes=[r1])
                            S.op(S.dve, lambda e: tt(e, t2[:, 0:w], pB[:, 0:w], si, ALU.mult), reads=[pBr, tabres], writes=[r2])
                            S.op(S.dve, lambda e: tt(e, t1[:, 0:w], t1[:, 0:w], t2[:, 0:w], ALU.subtract), reads=[r1, r2], writes=[r1])
                            S.op(S.dve, lambda e: tt(e, dst[:, 0, la:la + w], t1[:, 0:w], dd, ALU.mult), reads=[r1, tres], writes=[dres])
                            S.op(S.dve, lambda e: tt(e, t2[:, 0:w], pA[:, 0:w], si, ALU.mult), reads=[pAr, tabres], writes=[r2])
                            S.op(S.dve, lambda e: tt(e, t3[:, 0:w], pB[:, 0:w], co, ALU.mult), reads=[pBr, tabres], writes=[r3])
                            S.op(S.dve, lambda e: tt(e, t2[:, 0:w], t2[:, 0:w], t3[:, 0:w], ALU.add), reads=[r2, r3], writes=[r2])
                            S.op(S.dve, lambda e: tt(e, dst[:, 1, la:la + w], t2[:, 0:w], dd, ALU.mult), reads=[r2, tres], writes=[dres])
                    for half in range(2):
                        wb, wr = self.ws_next(128, 2048, kc=8)
                        for ci, (lo, L, smp) in enumerate(chunks):
                            ps, pr = self.bank()
                            for c in range(NCH):
                                S.op(S.pe, lambda e: e.matmul(ps[0:L, 0:256], lhsT=self.xn[:, c, lo:lo + L],
                                                              rhs=wb[:, c, :], start=(c == 0), stop=(c == NCH - 1)),
                                     reads=[wr] + self.xnres, writes=[pr], inc=(c == NCH - 1))
                            S.op(S.act, lambda e: e.activation(out=vS[0:L, ci, half * 256:(half + 1) * 256],
                                                               in_=ps[0:L, 0:256], func=AF.Copy),
                                 reads=[pr], writes=[vres])
                    for half in range(2):
                        wb, wr = self.ws_next(128, 2048, kc=8)
                        for ti, (a, b) in enumerate(G["tiles"]):
                            w = b - a
                            la = a - c0
                            for oc in range(2):
                                ps, pr = self.bank()
                                for c in range(NCH):
                                    S.op(S.pe, lambda e: e.matmul(ps[:, 0:w], lhsT=wb[:, c, oc * 128:(oc + 1) * 128],
                                                                  rhs=self.xn[:, c, la:la + w],
                                                                  start=(c == 0), stop=(c == NCH - 1)),
                                         reads=[wr, self.xnres[ti]], writes=[pr], inc=(c == NCH - 1))
                                S.op(S.act, lambda e: e.activation(out=gT[:, half * 2 + oc, la:la + w], in_=ps[:, 0:w],
                                                                   func=AF.Silu),
                                     reads=[pr], writes=[gres])
                    def st_SK(ci, lo, L):
                        cs = slice(lo, lo + L)
                        k2 = ci % 2
                        psS, psSr = self.bank()
                        for dc in range(2):
                            S.op(S.pe, lambda e: e.matmul(psS[0:L, 0:L], lhsT=kT[:, dc, cs], rhs=qT[:, dc, cs],
                                                          start=(dc == 0), stop=(dc == 1)),
                                 reads=[kres, qres], writes=[psSr], inc=(dc == 1))
                        S.op(S.dve, lambda e: e.tensor_tensor(out=sT2[k2][0:L, 0:L], in0=psS[0:L, 0:L],
                                                              in1=self.causT[0:L, 0:L], op=ALU.mult),
                             reads=[psSr, self.cres, rcst], writes=[sT2r[k2]])
                        psK, psKr = self.bank()
                        pk = psK[:].bitcast(BF16)
                        for dc in range(2):
                            S.op(S.pe, lambda e: e.transpose(pk[0:L, dc * 128:(dc + 1) * 128], kT[:, dc, cs], self.identb[:, :]),
                                 reads=[kres, self.cres, rcst], writes=[psKr], inc=(dc == 1))
                        S.op(S.act, lambda e: e.activation(out=kTok2[k2][0:L, :], in_=pk[0:L, 0:256], func=AF.Copy,
                                                           scale=gam ** L),
                             reads=[psKr], writes=[kTok2r[k2]])

                    def st_O(ci, lo, L):
                        cs = slice(lo, lo + L)
                        k2 = ci % 2
                        psO, psOr = self.bank()
                        S.op(S.pe, lambda e: e.matmul(psO[0:L, :], lhsT=sT2[k2][0:L, 0:L], rhs=vS[0:L, ci, :],
                                                      start=True, stop=False),
                             reads=[sT2r[k2], vres], writes=[psOr], inc=False)
                        for dc in range(2):
                            S.op(S.pe, lambda e: e.matmul(psO[0:L, :], lhsT=qT[:, dc, cs], rhs=Sb[:, dc, :],
                                                          start=False, stop=(dc == 1)),
                                 reads=[qres, Sbres2[dc]], writes=[psOr], inc=(dc == 1))
                        return psO, psOr

                    def st_U(ci, lo, L, last_prompt):
                        k2 = ci % 2
                        gL = gam ** L
                        for dc in range(2):
                            ps2, ps2r = self.bank()
                            S.op(S.pe, lambda e: e.matmul(ps2[:, :], lhsT=kTok2[k2][0:L, dc * 128:(dc + 1) * 128],
                                                          rhs=vS[0:L, ci, :], start=True, stop=True),
                                 reads=[kTok2r[k2], vres], writes=[ps2r])
                            S.op(S.dve, lambda e: e.scalar_tensor_tensor(out=Sf[:, h, dc, :], in0=Sf[:, h, dc, :],
                                                                         scalar=gL, in1=ps2[:, :],
                                                                         op0=ALU.mult, op1=ALU.add),
                                 reads=[ps2r, Sfres2[dc]], writes=[Sfres2[dc]])
                            if not last_prompt:
                                if dc == 0:
                                    S.op(S.act, lambda e: e.activation(out=Sb[:, 0, :], in_=Sf[:, h, 0, :], func=AF.Copy),
                                         reads=[Sfres2[0]], writes=[Sbres2[0]])
                                else:
                                    S.op(S.act, lambda e: e.activation(out=Sb[:, 1, :], in_=Sf[:, h, 1, :], func=AF.Copy),
                                         reads=[Sfres2[1]], writes=[Sbres2[1]])
                        if last_prompt:
                            S.dma(S.sp, o_p[h].rearrange("(p two) e -> p two e", two=2), Sf[:, h, :, :],
                                  reads=Sfres2, is_output=True)

                    def st_N(ci, L, psO, psOr):
                        k2 = ci % 2
                        S.op(S.act, lambda e: e.activation(out=junk[0:L, :], in_=psO[0:L, :], func=AF.Square,
                                                           accum_out=ssq[0:L, 0:1]),
                             reads=[psOr], writes=[jres, ssres])
                        S.op(S.act, lambda e: e.activation(out=ssq[0:L, 1:2], in_=ssq[0:L, 0:1], func=AF.Ln,
                                                           bias=self.epsv[0:L, 0:1], scale=1.0 / 512.0),
                             reads=[ssres, self.cres, rcst], writes=[ssres])
                        S.op(S.act, lambda e: e.activation(out=ssq[0:L, 1:2], in_=ssq[0:L, 1:2], func=AF.Exp, scale=-0.5),
                             reads=[ssres], writes=[ssres])
                        S.op(S.act, lambda e: e.activation(out=onb2[k2][0:L, :], in_=psO[0:L, :], func=AF.Copy,
                                                           scale=ssq[0:L, 1:2]),
                             reads=[psOr, ssres], writes=[onb2r[k2]])

                    def st_T(ci, lo, L):
                        cs = slice(lo, lo + L)
                        k2 = ci % 2
                        psT, psTr = self.bank()
                        pt = psT[:].bitcast(BF16)
                        for dvc in range(4):
                            S.op(S.pe, lambda e: e.transpose(pt[:, dvc * 128:dvc * 128 + L], onb2[k2][0:L, dvc * 128:(dvc + 1) * 128],
                                                             self.identb[0:L, 0:L]),
                                 reads=[onb2r[k2], self.cres, rcst], writes=[psTr], inc=(dvc == 3))
                        S.op(S.dve, lambda e: e.tensor_tensor(
                            out=gT[:, :, cs], in0=pt[:, 0:512].rearrange("p (c l) -> p c l", c=4)[:, :, 0:L],
                            in1=gT[:, :, cs], op=ALU.mult),
                             reads=[psTr, gres], writes=[gres])

                    pch = [(ci, lo, L) for ci, (lo, L, smp) in enumerate(chunks) if not smp]
                    smp_chunk = [(ci, lo, L) for ci, (lo, L, smp) in enumerate(chunks) if smp]
                    sstate = {}
                    if smp_chunk:
                        sci, slo, sL = smp_chunk[0]
                        scs = slice(slo, slo + sL)
                        psK, psKr = self.bank()
                        pk = psK[:].bitcast(BF16)
                        for dc in range(2):
                            S.op(S.pe, lambda e: e.transpose(pk[0:sL, dc * 128:(dc + 1) * 128], kT[:, dc, scs], self.identb[:, :]),
                                 reads=[kres, self.cres, rcst], writes=[psKr], inc=(dc == 1))
                        S.op(S.dve, lambda e: e.scalar_tensor_tensor(
                            out=kmask[:, :, :], in0=pk[0:16, 0:256].unsqueeze(1).to_broadcast([16, 16, 256]),
                            scalar=gam, in1=self.delta16[0:16, :].unsqueeze(2).to_broadcast([16, 16, 256]),
                            op0=ALU.mult, op1=ALU.mult),
                             reads=[psKr, self.cres, rcst], writes=[kmres])
                        S.op(S.dve, lambda e: e.tensor_tensor(
                            out=qmask[:, :, :, :], in0=qT[:, :, scs].unsqueeze(2).to_broadcast([128, 2, 16, 16]),
                            in1=self.deltaR[:, :, :].unsqueeze(1).to_broadcast([128, 2, 16, 16]), op=ALU.mult),
                             reads=[qres, self.cres, rcst], writes=[qmres])
                        psOs, psOsr, psOsi = self.reserve_bank()

                        def sload(bq):
                            st_, sr_ = sring[bq % 3]
                            S.dma(S.sp, st_[:], s_in[bq, h].rearrange("(p two) e -> p two e", two=2), writes=[sr_])

                        def sample_step(b):
                            st, sr = sring[b % 3]
                            if b + 2 < NSMP:
                                sload(b + 2)
                            for dc in range(2):
                                ps2, ps2r = self.bank()
                                S.op(S.pe, lambda e: e.matmul(ps2[:, :], lhsT=kmask[0:16, b, dc * 128:(dc + 1) * 128],
                                                              rhs=vS[0:16, sci, :], start=True, stop=True),
                                     reads=[kmres, vres], writes=[ps2r])
                                S.op(S.dve, lambda e: e.scalar_tensor_tensor(out=st[:, dc, :], in0=st[:, dc, :],
                                                                             scalar=gam, in1=ps2[:, :],
                                                                             op0=ALU.mult, op1=ALU.add),
                                     reads=[ps2r, sr], writes=[sr])
                            S.dma(S.sp, o_s[b, h].rearrange("(p two) e -> p two e", two=2), st[:], reads=[sr],
                                  is_output=True)
                            S.op(S.act, lambda e: e.activation(out=sbf[:], in_=st[:], func=AF.Copy),
                                 reads=[sr], writes=[sbfres])
                            for dc in range(2):
                                last = (b == NSMP - 1 and dc == 1)
                                S.op(S.pe, lambda e: e.matmul(psOs[0:16, :], lhsT=qmask[:, dc, b, :], rhs=sbf[:, dc, :],
                                                              start=(b == 0 and dc == 0), stop=last),
                                     reads=[qmres, sbfres], writes=[psOsr], inc=(dc == 1))

                        sload(0)
                        sload(1)
                        sstate["next"] = 0

                    def samples(n):
                        if not smp_chunk:
                            return
                        for _ in range(n):
                            if sstate["next"] < NSMP:
                                sample_step(sstate["next"])
                                sstate["next"] += 1

                    st_SK(*pch[0])
                    prevT = None
                    for idx, (ci, lo, L) in enumerate(pch):
                        if idx + 1 < len(pch):
                            st_SK(*pch[idx + 1])
                        psO, psOr = st_O(ci, lo, L)
                        st_U(ci, lo, L, g == 1 and idx == len(pch) - 1)
                        st_N(ci, L, psO, psOr)
                        if prevT is not None:
                            st_T(*prevT)
                        prevT = (ci, lo, L)
                        samples(2)
                    st_T(*prevT)
                    if smp_chunk:
                        samples(NSMP)
                        self.breserved.discard(psOsi)
                        st_N(sci, sL, psOs, psOsr)
                        st_T(sci, slo, sL)
                    for half in range(2):
                        wb, wr = self.ws_next(128, 2048, kc=4)
                        for ti, (a, b) in enumerate(G["tiles"]):
                            w = b - a
                            la = a - c0
                            for oc in range(4):
                                ps, pr = self.bank()
                                for dvc in range(4):
                                    S.op(S.pe, lambda e: e.matmul(ps[:, 0:w], lhsT=wb[:, dvc, oc * 128:(oc + 1) * 128],
                                                                  rhs=gT[:, dvc, la:la + w],
                                                                  start=(dvc == 0), stop=(dvc == 3)),
                                         reads=[wr, gres], writes=[pr], inc=(dvc == 3))
                                ec = half * 4 + oc
                                S.op(S.dve, lambda e: e.tensor_tensor(out=self.hT[:, ec, a:b], in0=self.hT[:, ec, a:b],
                                                                      in1=ps[:, 0:w], op=ALU.add),
                                     reads=[pr], writes=[self.hres[(g, ti)]])

    def build(self):
        nc, S = self.nc, self.S
        es = self.es
        xT = self.din("xT", [D, TT])
        self.din("vecs", [128, NVEC, 8])
        self.din("cos_t", [128, TT])
        self.din("sin_t", [128, TT])
        self.din("qkd_t", [8, TT])
        self.din("consts", [128, 5, 128])
        self.din("deltaR", [128, 16, 16])
        L = self.layers
        if L >= 1:
            self.din("s_ret0", [NSMP, 4, 256, 512])
            self.din("ret_w_in", [2, D, 6144])
            self.din("ret_w_out", [2, 2048, D])
            self.din("mlp_w1", [4, D, 4 * D])
            self.din("mlp_w2", [4, 4 * D, D])
        if L >= 2:
            self.din("s_shiftT", [D, NSMP])
            self.din("s_wkv", [NSMP, 16, 64, 64])
            self.din("rwkv_w_rkv", [3, D, D])
            self.din("rwkv_w_o", [D, D])
            self.din("rwkv_w1a1", [D, 128])
            self.din("rwkv_g1", [D, 160])
            self.din("rwkv_w2a2", [64, 2048])
            self.din("rwkv_g2", [160, D])
            self.din("ln_gb", [3, D])
            self.din("rconst", [128, 11, 128])
            self.wcache = self.nc.dram_tensor("wcache", [len(RW_BLOCKS), 128, 2048], BF16).ap()
            self.wcres = [Res("wc%d" % i) for i in range(len(RW_BLOCKS))]
        if L >= 3:
            self.din("s_convT", [D, 2, NSMP])
            self.din("conv_w_in", [D, 3 * D])
            self.din("conv_w_out", [D, D])
        if L >= 4:
            self.din("s_ret3", [NSMP, 4, 256, 512])
        yT = self.dout("yT", [D, TT])
        if L >= 1:
            self.dout("ret0_p", [4, 256, 512])
            self.dout("ret0_s", [NSMP, 4, 256, 512])
        if L >= 2:
            self.dout("shiftT", [D, 1 + NSMP])
            self.dout("wkv_p", [16, 64, 64])
            self.dout("wkv_s", [NSMP, 16, 64, 64])
        if L >= 3:
            self.dout("convT_p", [D, 2])
            self.dout("convT_s", [D, 2, NSMP])
        if L >= 4:
            self.dout("ret3_p", [4, 256, 512])
            self.dout("ret3_s", [NSMP, 4, 256, 512])

        self.hT = self.sb(es, "hT", [128, NCH, TT], F32)
        self.hres = {}
        for g, G in enumerate(GROUPS):
            for ti in range(len(G["tiles"])):
                self.hres[(g, ti)] = Res("h%d_%d" % (g, ti))
        self.vecs = self.sb(es, "vecs", [128, NVEC, 8], F32)
        self.identb = self.sb(es, "identb", [128, 128], BF16)
        self.onesD = self.sb(es, "onesD", [128, 128], BF16)
        self.epsv = self.sb(es, "epsv", [128, 1], F32)
        self.cres = Res("consts")
        self.sqb = [(self.sb(es, "sq", [128, 512], BF16), Res("sq")) for _ in range(2)]
        self.rsb = (self.sb(es, "rs", [128, 512], F32), Res("rs"))
        self.wbufs = [(self.sb(es, "wbf", [128, 2048], BF16), Res("wbf")) for _ in range(NRING)]
        self.banks = [es.enter_context(nc.psum_tensor("bank%d" % i, [128, 512], F32)) for i in range(8)]
        self.bres = [Res("bank%d" % i) for i in range(8)]
        self.bnext = 0
        self.breserved = set()
        self.wplan = []
        self.w_dma = self.w_cv = self.w_use = 0

        for li in range(self.layers):
            kind = li % 3
            if kind == 0:
                self.plan_ret(li // 3)
            elif kind == 1:
                self.plan_rwkv()
            else:
                self.plan_conv()
            for g in range(2):
                self.plan_mlp(li)

        es_c = contextlib.ExitStack()
        cst = self.sb(es_c, "cst", [128, 5, 128], F32)
        S.dma(S.sp, self.hT[:], xT.rearrange("(c p) t -> p c t", p=128),
              writes=[self.hres[k] for k in self.hres])
        S.dma(S.sp, self.vecs[:], self.dram["vecs"][:, :, :], writes=[self.cres])
        S.dma(S.sp, cst[:], self.dram["consts"][:, :, :], writes=[self.cres])
        S.op(S.pool, lambda e: e.tensor_copy(out=self.identb[:], in_=cst[:, 0, :]), reads=[self.cres], writes=[self.cres])
        S.op(S.pool, lambda e: e.tensor_copy(out=self.onesD[:], in_=cst[:, 1, :]), reads=[self.cres], writes=[self.cres])
        S.op(S.pool, lambda e: e.memset(self.epsv[:], EPS), writes=[self.cres])
        es_c.close()

        for li in range(self.layers):
            kind = li % 3
            if kind == 0:
                self.ret_layer(li, li // 3)
            elif kind == 1:
                self.rwkv_layer(li)
            else:
                self.conv_layer(li)
            self.mlp_layer(li)

        with contextlib.ExitStack() as es2:
            yo = [(self.sb(es2, "yo", [128, NCH, 512], F32), S.new_res("yo")) for _ in range(2)]
            k = 0
            for g, G in enumerate(GROUPS):
                for ti, (a, b) in enumerate(G["tiles"]):
                    yb, yr = yo[k % 2]
                    k += 1
                    self.rmsnorm_tile(a, b, [self.hres[(g, ti)]], V_NFIN, lambda c: yb[:, c, 0:b - a], yr)
                    S.dma(S.sp, yT.rearrange("(c p) t -> p c t", p=128)[:, :, a:b], yb[:, :, 0:b - a], reads=[yr],
                          is_output=True)
        S.finish()
        es.close()
        S.close()
        return nc

    def hres_for(self, a, b):
        out = []
        for g, G in enumerate(GROUPS):
            for ti, (x, y) in enumerate(G["tiles"]):
                if x < b and a < y:
                    out.append(self.hres[(g, ti)])
        return out

    def rw_src(self, i):
        d = self.dram
        if i < 12:
            j, blk = divmod(i, 4)
            return d["rwkv_w_rkv"][j].rearrange("(c p) e -> p c e", p=128)[:, :, blk * 256:(blk + 1) * 256]
        if i < 16:
            blk = i - 12
            return d["rwkv_w_o"].rearrange("(c p) e -> p c e", p=128)[:, :, blk * 256:(blk + 1) * 256]
        if i == 16:
            return d["rwkv_w1a1"].rearrange("(c p) e -> p c e", p=128)
        if i == 17:
            return d["rwkv_g1"].rearrange("(c p) e -> p c e", p=128)
        if i == 18:
            return d["rwkv_w2a2"][:, :]
        if i == 19:
            return d["rwkv_g2"][0:128, :]
        return d["rwkv_g2"][128:160, :]

    def plan_rwkv(self):
        for i, (p, n) in enumerate(RW_BLOCKS):
            self.ws_add(self.rw_src(i), p, n)
        for t in range(17):
            for i in RW_ORDER:
                p, n = RW_BLOCKS[i]
                self.wplan.append(dict(ap=None, p=p, n=n, dst=None, dst_res=None, cached=i))

    def rwkv_layer(self, li):
        S = self.S
        C0 = math.exp(-0.5)
        with contextlib.ExitStack() as es:
            sb = lambda n, shp, dt: self.sb(es, n, shp, dt)
            R = S.new_res
            for i, (p, n) in enumerate(RW_BLOCKS):
                wb, wr = self.ws_next(p, n)
                S.dma(S.sp, self.wcache[i, 0:p, 0:n], wb, reads=[wr], writes=[self.wcres[i]])
            rc = sb("rconst", [128, 11, 128], F32)
            rcr = R("rconst")
            S.dma(S.sp, rc[:], self.dram["rconst"][:, :, :], writes=[rcr])
            mask_su = lambda L: rc[0:L, 3, 0:L]
            mask2 = lambda L: rc[0:L, 3:5, 0:L]
            mask_sl = lambda L: rc[0:L, 5, 0:L]
            mask_un = lambda L: rc[0:L, 6, 0:L]
            blockones = rc[:, 7, :]
            headsel = rc[:, 8, 0:2]
            id64 = rc[:, 8, 2:66]
            selS = rc[0:32, 8, 66:82]
            identf = rc[:, 9, :]
            lng_b = sb("lng_b", [128, D], F32)
            lnb_b = sb("lnb_b", [128, D], F32)
            w0_b = sb("w0_b", [128, D], F32)
            S.dma(S.sp, lng_b[:], self.dram["ln_gb"][0, :].partition_broadcast(128), writes=[rcr])
            S.dma(S.sp, lnb_b[:], self.dram["ln_gb"][1, :].partition_broadcast(128), writes=[rcr])
            S.dma(S.sp, w0_b[:], self.dram["ln_gb"][2, :].partition_broadcast(128), writes=[rcr])
            gnv = sb("gnv", [128, 1], F32)
            S.op(S.pool, lambda e: e.memset(gnv[:], GN_EPS), writes=[rcr])
            xnf = sb("xnf", [128, NCH, 129], F32)
            xx = sb("xx", [128, NCH, 128], F32)
            xm = [sb("xm", [128, NCH, 128], BF16) for _ in range(2)]
            RB = sb("RB", [128, NCH, 128], F32)
            KB = sb("KB", [128, NCH, 128], F32)
            KKB = sb("KKB", [128, NCH, 128], F32)
            AB = sb("AB", [128, NCH, 128], F32)
            GB = sb("GB", [128, NCH, 128], BF16)
            thw = sb("thw", [64, 128], BF16)
            la = sb("la", [64, 128], BF16)
            sg = sb("sg", [128, 128], BF16)
            sg2 = sb("sg2", [32, 128], BF16)
            SIGW = sb("SIGW", [128, D], F32)
            Vf = sb("Vf", [128, D], F32)
            Vb = sb("Vb", [128, D], BF16)
            yT = sb("yT", [128, NCH, 128], BF16)
            Hf = sb("Hf", [128, NCH, 128], F32)
            Hb = sb("Hb", [128, NCH, 128], BF16)
            shS = sb("shS", [128, NCH, NSMP], F32)
            vTs = sb("vTs", [128, NCH, NSMP], F32)
            (xnr, xxr, RBr, KBr, KKBr, ABr, GBr, lorar, SIGWr, Vfr, Vbr, yTr, shr, vTsr) = [
                R(n) for n in "xnf xx RB KB KKB AB GB lora SIGW Vf Vb yT shS vTs".split()]
            Hfr = [R("Hf%d" % i) for i in range(NCH)]
            Hbr = [R("Hb%d" % i) for i in range(NCH)]
            xmr = [R("xm0"), R("xm1")]
            S.dma(S.sp, shS[:], self.dram["s_shiftT"].rearrange("(c p) b -> p c b", p=128), writes=[shr])
            S.op(S.pool, lambda e: e.memset(Hf[:], 0.0), writes=Hfr)
            S.op(S.pool, lambda e: e.memset(Hb[:], 0.0), writes=Hbr)
            S.op(S.pool, lambda e: e.memset(xnf[:, :, 0:1], 0.0), writes=[xnr])

            def mm(out, lhsT, rhs, reads, writes, start=True, stop=True, inc=True, **kw):
                return S.op(S.pe, lambda e: e.matmul(out, lhsT=lhsT, rhs=rhs, start=start, stop=stop, **kw),
                            reads=reads, writes=writes, inc=inc)

            def act(out, in_, func, reads, writes, **kw):
                return S.op(S.act, lambda e: e.activation(out=out, in_=in_, func=func, **kw), reads=reads, writes=writes)

            def tt(eng, out, in0, in1, op, reads, writes):
                return S.op(eng, lambda e: e.tensor_tensor(out=out, in0=in0, in1=in1, op=op), reads=reads, writes=writes)

            def stt(eng, out, in0, scalar, in1, op0, op1, reads, writes):
                return S.op(eng, lambda e: e.scalar_tensor_tensor(out=out, in0=in0, scalar=scalar, in1=in1,
                                                                 op0=op0, op1=op1), reads=reads, writes=writes)

            def bc(ap, shape, axis):
                return ap.unsqueeze(axis).to_broadcast(shape)

            es2 = contextlib.ExitStack()
            sb2 = lambda n, shp, dt: self.sb(es2, n, shp, dt)
            ART = sb2("ART", [128, NCH, 2, 128], BF16)
            BH = sb2("BH", [128, NCH, 128], BF16)
            KH = sb2("KH", [128, NCH, 128], BF16)
            KBt = [sb2("KBt", [128, 2, 128], BF16) for _ in range(2)]
            A_tok = sb2("A_tok", [128, D], BF16)
            K_tok = sb2("K_tok", [128, D], BF16)
            B_tok = sb2("B_tok", [128, D], BF16)
            e3s = [(sb2("e3", [128, 3, 128], F32), R("e3")) for _ in range(2)]
            eis = [(sb2("ei", [128, 128], F32), R("ei")) for _ in range(2)]
            gC = sb2("gC", [128, NCH], F32)
            BS = sb2("BS", [128, 16], F32)
            st = sb2("st", [128, 64], F32)
            Ytok = sb2("Ytok", [128, D], F32)
            NHB = 2 * PBATCH
            PM = [sb2("PM", [128, 2, 128], CH_DT) for _ in range(NHB)]
            PN = [sb2("PN", [128, 128], CH_DT) for _ in range(NHB)]
            N0 = [sb2("N0", [128, 128], CH_DT) for _ in range(NHB)]
            N0r = [R("N0") for _ in range(NHB)]
            identr = sb2("identr", [128, 128], CH_DT)
            S.op(S.pool, lambda e: e.tensor_copy(out=identr[:], in_=rc[:, 9, :]), reads=[rcr], writes=[rcr])
            MTb = [sb2("MTb", [128, 128], BF16) for _ in range(NHB)]
            AKRK = [sb2("AKRK", [128, 2, 128], BF16) for _ in range(NHB)]
            ARB = [sb2("ARB", [128, 128], BF16) for _ in range(NHB)]
            Psb = [sb2("Psb", [128, 64], BF16) for _ in range(NHB)]
            Gp = [sb2("Gp", [128, 128], F32) for _ in range(PBATCH)]
            WTs = [sb2("WTs", [128, 128], BF16) for _ in range(PBATCH)]
            Us = [sb2("Us", [128, 128], BF16) for _ in range(PBATCH)]
            (ARTr, BHr, KHr, Atr, Ktr, Btr, gCr, BSr, str_, Ytr) = [
                R(n) for n in "ART BH KH A_tok K_tok B_tok gC BS st Ytok".split()]
            KBtr = [R("KBt0"), R("KBt1")]
            PMr = [R("PM") for _ in range(NHB)]
            PNr = [R("PN") for _ in range(NHB)]
            MTbr = [R("MTb") for _ in range(NHB)]
            AKRKr = [R("AKRK") for _ in range(NHB)]
            ARBr = [R("ARB") for _ in range(NHB)]
            Psbr = [R("Psb") for _ in range(NHB)]
            Gpr = [R("Gp") for _ in range(PBATCH)]
            WTsr = [R("WTs") for _ in range(PBATCH)]
            Usr = [R("Us") for _ in range(PBATCH)]

            def tile_dims(ti_):
                last_ = (ti_ == 16)
                a_ = ti_ * 128
                W_ = 32 if last_ else 128
                L_ = 16 if last_ else 128
                return last_, a_, W_, L_, a_ + W_

            def build_xm(j, k, W_):
                for c in range(NCH):
                    stt(S.dve, xm[k][:, c, 0:W_], xx[:, c, 0:W_], self.vecs[:, V_RMIX + j, c:c + 1],
                        xnf[:, c, 1:1 + W_], ALU.mult, ALU.add, [xxr, xnr, self.cres], [xmr[k]])

            def prep(ti_):
                last_, a_, W_, L_, b_ = tile_dims(ti_)
                hrs_ = self.hres_for(a_, b_)
                if ti_ > 0:
                    S.op(S.dve, lambda e: e.tensor_copy(out=xnf[:, :, 0:1], in_=xnf[:, :, 128:129]),
                         reads=[xnr], writes=[xnr])
                self.rmsnorm_tile(a_, b_, hrs_, V_NMIX + li, lambda c: xnf[:, c, 1:1 + W_], xnr)
                tt(S.pool, xx[:, :, 0:L_], xnf[:, :, 0:L_], xnf[:, :, 1:L_ + 1], ALU.subtract, [xnr], [xxr])
                if last_:
                    tt(S.pool, xx[:, :, 16:32], shS[:, :, :], xnf[:, :, 17:33], ALU.subtract, [xnr, shr], [xxr])
                    S.dma(S.sp, self.dram["shiftT"].rearrange("(c p) t -> p c t", p=128), xnf[:, :, 16:33],
                          reads=[xnr], is_output=True)
                build_xm(0, 0, W_)
                build_xm(1, 1, W_)

            prep(0)
            for ti in range(17):
                last, a, W, L, b = tile_dims(ti)
                hrs = self.hres_for(a, b)
                for j, (dst, dr) in enumerate(((RB, RBr), (KB, KBr))):
                    for blk in range(4):
                        wb, wr = self.ws_next(128, 2048, kc=8)
                        for oc in range(2):
                            ps, pr = self.bank()
                            for c in range(NCH):
                                mm(ps[:, 0:W], wb[:, c, oc * 128:(oc + 1) * 128], xm[j][:, c, 0:W], [wr, xmr[j]], [pr],
                                   start=(c == 0), stop=(c == NCH - 1), inc=(c == NCH - 1))
                            act(dst[:, blk * 2 + oc, 0:W], ps[:, 0:W], AF.Copy, [pr], [dr])
                build_xm(3, 1, W)
                build_xm(4, 0, W)
                wb, wr = self.ws_next(128, 1024, kc=8)
                psW, psWr = self.bank()
                psA, psAr = self.bank()
                for c in range(NCH):
                    mm(psW[0:64, 0:W], wb[:, c, 0:64], xm[1][:, c, 0:W], [wr, xmr[1]], [psWr],
                       start=(c == 0), stop=(c == NCH - 1), inc=(c == NCH - 1))
                for c in range(NCH):
                    mm(psA[0:64, 0:W], wb[:, c, 64:128], xm[0][:, c, 0:W], [wr, xmr[0]], [psAr],
                       start=(c == 0), stop=(c == NCH - 1), inc=(c == NCH - 1))
                act(thw[0:64, 0:W], psW[0:64, 0:W], AF.Tanh, [psWr], [lorar])
                act(la[0:64, 0:W], psA[0:64, 0:W], AF.Copy, [psAr], [lorar])
                wb, wr = self.ws_next(64, 2048)
                for half in range(2):
                    ps, pr = self.bank()
                    mm(ps[0:W, :], thw[0:64, 0:W], wb[0:64, half * 512:(half + 1) * 512], [wr, lorar], [pr])
                    tt(S.dve, SIGW[0:W, half * 512:(half + 1) * 512], ps[0:W, :], w0_b[0:W, half * 512:(half + 1) * 512],
                       ALU.add, [pr, rcr], [SIGWr])
                act(SIGW[0:W, :], SIGW[0:W, :], AF.Sigmoid, [SIGWr], [SIGWr])
                for fc in range(NCH):
                    ps, pr = self.bank()
                    mm(ps[:, 0:W], wb[0:64, 1024 + fc * 128:1024 + (fc + 1) * 128], la[0:64, 0:W], [wr, lorar], [pr])
                    act(AB[:, fc, 0:W], ps[:, 0:W], AF.Sigmoid, [pr, self.cres], [ABr], bias=self.vecs[:, V_A0, fc:fc + 1])
                build_xm(2, 0, W)
                build_xm(5, 1, W)
                for blk in range(4):
                    wb, wr = self.ws_next(128, 2048, kc=8)
                    ps, pr = self.bank()
                    for c in range(NCH):
                        mm(ps[0:W, 0:256], xm[0][:, c, 0:W], wb[:, c, :], [wr, xmr[0]], [pr],
                           start=(c == 0), stop=(c == NCH - 1), inc=(c == NCH - 1))
                    act(Vf[0:W, blk * 256:(blk + 1) * 256], ps[0:W, 0:256], AF.Copy, [pr], [Vfr])
                    if last:
                        for oc in range(2):
                            ps, pr = self.bank()
                            for c in range(NCH):
                                mm(ps[:, 0:NSMP], wb[:, c, oc * 128:(oc + 1) * 128], xm[0][:, c, 16:32], [wr, xmr[0]], [pr],
                                   start=(c == 0), stop=(c == NCH - 1), inc=(c == NCH - 1))
                            act(vTs[:, blk * 2 + oc, :], ps[:, 0:NSMP], AF.Copy, [pr], [vTsr])
                vb = lambda row: bc(self.vecs[:, row, :], [128, NCH, W], 2)
                SCR = xx
                tt(S.dve, KKB[:, :, 0:W], KB[:, :, 0:W], vb(V_KK), ALU.mult, [KBr, self.cres], [KKBr])
                tt(S.pool, SCR[:, :, 0:W], KKB[:, :, 0:W], KKB[:, :, 0:W], ALU.mult, [KKBr], [xxr])
                for half in range(2):
                    ps, pr = self.bank()
                    pv = ps[:, 0:4 * W].rearrange("p (c w) -> p c w", c=4)
                    mm(pv, blockones, SCR[:, half * 4:(half + 1) * 4, 0:W], [xxr, rcr], [pr])
                    S.op(S.dve, lambda e: e.tensor_scalar(out=SCR[:, half * 4:(half + 1) * 4, 0:W], in0=pv, scalar1=1e-24,
                                                          scalar2=None, op0=ALU.max), reads=[pr], writes=[xxr])
                act(SCR[:, :, 0:W], SCR[:, :, 0:W], AF.Ln, [xxr], [xxr])
                act(SCR[:, :, 0:W], SCR[:, :, 0:W], AF.Exp, [xxr], [xxr], scale=-0.5)
                tt(S.dve, KKB[:, :, 0:W], KKB[:, :, 0:W], SCR[:, :, 0:W], ALU.mult, [KKBr, xxr], [KKBr])
                stt(S.dve, SCR[:, :, 0:W], AB[:, :, 0:W], -1.0, vb(V_KA), ALU.add, ALU.mult, [ABr, self.cres], [xxr])
                stt(S.dve, KB[:, :, 0:W], SCR[:, :, 0:W], 1.0, KB[:, :, 0:W], ALU.add, ALU.mult, [xxr, KBr], [KBr])
                tt(S.pool, AB[:, :, 0:W], KKB[:, :, 0:W], AB[:, :, 0:W], ALU.mult, [KKBr, ABr], [ABr])
                tt(S.pool, SCR[:, :, 0:W], RB[:, :, 0:W], vb(V_RK), ALU.mult, [RBr, self.cres], [xxr])
                tt(S.pool, SCR[:, :, 0:W], SCR[:, :, 0:W], KB[:, :, 0:W], ALU.mult, [xxr, KBr], [xxr])
                wb, wr = self.ws_next(128, 1280, kc=8)
                psG, psGr = self.bank()
                psG2, psG2r = self.bank()
                for c in range(NCH):
                    mm(psG[:, 0:W], wb[:, c, 0:128], xm[1][:, c, 0:W], [wr, xmr[1]], [psGr],
                       start=(c == 0), stop=(c == NCH - 1), inc=(c == NCH - 1))
                for c in range(NCH):
                    mm(psG2[0:32, 0:W], wb[:, c, 128:160], xm[1][:, c, 0:W], [wr, xmr[1]], [psG2r],
                       start=(c == 0), stop=(c == NCH - 1), inc=(c == NCH - 1))
                act(sg[:, 0:W], psG[:, 0:W], AF.Sigmoid, [psGr], [lorar])
                act(sg2[0:32, 0:W], psG2[0:32, 0:W], AF.Sigmoid, [psG2r], [lorar])
                wbA, wrA = self.ws_next(128, 1024)
                wbB, wrB = self.ws_next(32, 1024)
                for fc in range(NCH):
                    ps, pr = self.bank()
                    mm(ps[:, 0:W], wbA[:, fc * 128:(fc + 1) * 128], sg[:, 0:W], [wrA, lorar], [pr], start=True, stop=False, inc=False)
                    mm(ps[:, 0:W], wbB[0:32, fc * 128:(fc + 1) * 128], sg2[0:32, 0:W], [wrB, lorar], [pr], start=False, stop=True)
                    act(GB[:, fc, 0:W], ps[:, 0:W], AF.Copy, [pr], [GBr])
                S.op(S.pool, lambda e: e.tensor_copy(out=Vb[0:W, :], in_=Vf[0:W, :]), reads=[Vfr], writes=[Vbr])
                psB, psBr = self.bank()
                for c in range(NCH):
                    mm(psB[0:L, 2 * c:2 * c + 2], SCR[:, c, 0:L], headsel, [xxr, rcr], [psBr], inc=(c == NCH - 1))
                act(BS[0:L, :], psB[0:L, 0:16], AF.Copy, [psBr], [BSr])
                psKt, psKtr, iKt = self.reserve_bank()
                psBt, psBtr, iBt = self.reserve_bank()
                pkt = psKt[:].bitcast(BF16)
                pbt = psBt[:].bitcast(BF16)
                p3s = {}

                def dec_mm(fc_):
                    ps_, pr_ = self.bank()
                    p3_ = ps_[:, 0:3 * L].rearrange("p (c w) -> p c w", c=3)
                    mm(p3_, SIGW[0:L, fc_ * 128:(fc_ + 1) * 128], rc[0:L, 0:3, 0:L], [SIGWr, rcr], [pr_])
                    p3s[fc_] = (p3_, pr_)

                dec_mm(0)
                for fc in range(NCH):
                    if fc + 1 < NCH:
                        dec_mm(fc + 1)
                    p3, pr = p3s.pop(fc)
                    e3, e3r = e3s[fc % 2]
                    ei, eir = eis[fc % 2]
                    act(e3[:, :, 0:L], p3, AF.Exp, [pr], [e3r])
                    act(ei[:, 0:L], p3[:, 0, :], AF.Exp, [pr], [eir], scale=-1.0)
                    act(gC[:, fc:fc + 1], e3[:, 0, L - 1:L], AF.Copy, [e3r], [gCr])
                    tt(S.dve, ART[:, fc, 0, 0:L], KKB[:, fc, 0:L], e3[:, 1, 0:L], ALU.mult, [KKBr, e3r], [ARTr])
                    tt(S.dve, ART[:, fc, 1, 0:L], RB[:, fc, 0:L], e3[:, 0, 0:L], ALU.mult, [RBr, e3r], [ARTr])
                    tt(S.pool, BH[:, fc, 0:L], AB[:, fc, 0:L], ei[:, 0:L], ALU.mult, [ABr, eir], [BHr])
                    tt(S.pool, KH[:, fc, 0:L], KB[:, fc, 0:L], ei[:, 0:L], ALU.mult, [KBr, eir], [KHr])
                    kbt, kbtr = KBt[fc % 2], KBtr[fc % 2]
                    tt(S.dve, kbt[:, 0, 0:L], KB[:, fc, 0:L], e3[:, 2, 0:L], ALU.mult, [KBr, e3r], [kbtr])
                    stt(S.dve, kbt[:, 1, 0:L], AB[:, fc, 0:L], -1.0, e3[:, 2, 0:L], ALU.mult, ALU.mult, [ABr, e3r], [kbtr])
                    S.op(S.pe, lambda e: e.transpose(pkt[0:L, fc * 128:(fc + 1) * 128], kbt[:, 0, 0:L], self.identb[:, :]),
                         reads=[kbtr, self.cres], writes=[psKtr], inc=False)
                    S.op(S.pe, lambda e: e.transpose(pbt[0:L, fc * 128:(fc + 1) * 128], kbt[:, 1, 0:L], self.identb[:, :]),
                         reads=[kbtr, self.cres], writes=[psBtr], inc=True)
                act(K_tok[0:L, :], pkt[0:L, :], AF.Copy, [psKtr], [Ktr])
                act(B_tok[0:L, :], pbt[0:L, :], AF.Copy, [psBtr], [Btr])
                self.breserved.discard(iKt)
                self.breserved.discard(iBt)
                psAt, psAtr = self.bank()
                pat = psAt[:].bitcast(BF16)
                for fc in range(NCH):
                    S.op(S.pe, lambda e: e.transpose(pat[0:L, fc * 128:(fc + 1) * 128], ART[:, fc, 0, 0:L], self.identb[:, :]),
                         reads=[ARTr, self.cres], writes=[psAtr], inc=(fc == NCH - 1))
                act(A_tok[0:L, :], pat[0:L, :], AF.Copy, [psAtr], [Atr])
                nlev = int(round(math.log2(L))) - 1
                for p0 in range(0, NCH, PBATCH):
                    pairs = list(range(p0, min(NCH, p0 + PBATCH)))
                    heads = [(p, j) for p in pairs for j in range(2)]
                    slot = {hj: i for i, hj in enumerate(heads)}

                    def hv(p, j):
                        rows = slice(64 * j, 64 * j + 64)
                        hd = 2 * p + j
                        return rows, slice(hd * 64, hd * 64 + 64)

                    for (p, j) in heads:
                        k = slot[(p, j)]
                        rows, hs = hv(p, j)
                        arT = ART[rows, p, :, 0:L]
                        aT = ART[rows, p, 0, 0:L]
                        bT = BH[rows, p, 0:L]
                        kT = KH[rows, p, 0:L]
                        pA, pAr = self.bank()
                        pAv = pA[0:L, 0:2 * L].rearrange("p (c w) -> p c w", c=2)
                        mm(pAv, bT, arT, [BHr, ARTr], [pAr])
                        pB, pBr = self.bank()
                        pBv = pB[0:L, 0:2 * L].rearrange("p (c w) -> p c w", c=2)
                        mm(pBv, kT, arT, [KHr, ARTr], [pBr])
                        pC, pCr = self.bank()
                        mm(pC[0:L, 0:L], aT, bT, [ARTr, BHr], [pCr])
                        tt(S.dve, PM[k][0:L, 0, 0:L], pAv[:, 0, :], mask_su(L), ALU.mult, [pAr, rcr], [PMr[k]])
                        tt(S.dve, ARB[k][0:L, 0:L], pAv[:, 1, :], mask_un(L), ALU.mult, [pAr, rcr], [ARBr[k]])
                        tt(S.dve, AKRK[k][0:L, :, 0:L], pBv, mask2(L), ALU.mult, [pBr, rcr], [AKRKr[k]])
                        tt(S.dve, N0[k][0:L, 0:L], pC[0:L, 0:L], mask_sl(L), ALU.mult, [pCr, rcr], [N0r[k]])
                        tt(S.pool, PM[k][0:L, 1, 0:L], identf[0:L, 0:L], PM[k][0:L, 0, 0:L], ALU.subtract, [PMr[k], rcr], [PMr[k]])
                    for (p, j) in heads:
                        k = slot[(p, j)]
                        p1, p1r = self.bank()
                        p2, p2r = self.bank()
                        mm(p1[0:L, 0:L], PM[k][0:L, 0, 0:L], N0[k][0:L, 0:L], [PMr[k], N0r[k]], [p1r])
                        mm(p2[0:L, 0:L], N0[k][0:L, 0:L], PM[k][0:L, 0, 0:L], [PMr[k], N0r[k]], [p2r])
                        act(PN[k][0:L, 0:L], p1[0:L, 0:L], AF.Copy, [p1r], [PNr[k]])
                        act(PM[k][0:L, 0, 0:L], p2[0:L, 0:L], AF.Copy, [p2r], [PMr[k]])
                    for lev in range(1, nlev + 1):
                        for (p, j) in heads:
                            k = slot[(p, j)]
                            if lev < nlev:
                                pX, pXr = self.bank()
                                pXv = pX[0:L, 0:2 * L].rearrange("p (c w) -> p c w", c=2)
                                mm(pXv, PN[k][0:L, 0:L], PM[k][0:L, :, 0:L], [PMr[k], PNr[k]], [pXr])
                                pY, pYr = self.bank()
                                mm(pY[0:L, 0:L], PM[k][0:L, 0, 0:L], PN[k][0:L, 0:L], [PMr[k], PNr[k]], [pYr])
                                act(PM[k][0:L, 0, 0:L], pXv[:, 0, :], AF.Copy, [pXr], [PMr[k]])
                                tt(S.dve, PM[k][0:L, 1, 0:L], PM[k][0:L, 1, 0:L], pXv[:, 1, :], ALU.add, [pXr, PMr[k]], [PMr[k]])
                                act(PN[k][0:L, 0:L], pY[0:L, 0:L], AF.Copy, [pYr], [PNr[k]])
                            else:
                                pX, pXr = self.bank()
                                mm(pX[0:L, 0:L], PN[k][0:L, 0:L], PM[k][0:L, 1, 0:L], [PMr[k], PNr[k]], [pXr])
                                tt(S.dve, PM[k][0:L, 1, 0:L], PM[k][0:L, 1, 0:L], pX[0:L, 0:L], ALU.add, [pXr, PMr[k]], [PMr[k]])
                    for (p, j) in heads:
                        k = slot[(p, j)]
                        pTb, pTr = self.bank()
                        pTv = pTb[:].bitcast(CH_DT)
                        S.op(S.pe, lambda e: e.transpose(pTv[0:L, 0:L], PM[k][0:L, 1, 0:L], identr[0:L, 0:L]),
                             reads=[PMr[k], rcr], writes=[pTr])
                        pZ, pZr = self.bank()
                        mm(pZ[0:L, 0:L], N0[k][0:L, 0:L], PM[k][0:L, 1, 0:L], [N0r[k], PMr[k]], [pZr])
                        act(PN[k][0:L, 0:L], pTv[0:L, 0:L], AF.Copy, [pTr], [PNr[k]])
                        tt(S.pool, PM[k][0:L, 0, 0:L], rc[0:L, 10, 0:L], PM[k][0:L, 1, 0:L], ALU.subtract, [PMr[k], rcr], [PMr[k]])
                        tt(S.dve, PM[k][0:L, 0, 0:L], PM[k][0:L, 0, 0:L], pZ[0:L, 0:L], ALU.subtract, [pZr, PMr[k]], [PMr[k]])
                    for (p, j) in heads:
                        k = slot[(p, j)]
                        rows, hs = hv(p, j)
                        pM1, pM1r = self.bank()
                        mm(pM1[0:L, 0:L], PN[k][0:L, 0:L], PM[k][0:L, 0, 0:L], [PNr[k], PMr[k]], [pM1r])
                        pP, pPr = self.bank()
                        mm(pP[0:L, 0:64], AKRK[k][0:L, 0, 0:L], Vb[0:L, hs], [AKRKr[k], Vbr], [pPr])
                        act(MTb[k][0:L, 0:L], pM1[0:L, 0:L], AF.Copy, [pM1r], [MTbr[k]])
                        act(Psb[k][0:L, :], pP[0:L, 0:64], AF.Copy, [pPr], [Psbr[k]])
                    for p in pairs:
                        q = p - p0
                        psWT, psWTr, iWT = self.reserve_bank()
                        pG, pGr = self.bank()
                        for j in range(2):
                            k = slot[(p, j)]
                            rows, hs = hv(p, j)
                            mm(pG[0:L, 64 * j:64 * j + 64], MTb[k][0:L, 0:L], Psb[k][0:L, :], [MTbr[k], Psbr[k]], [pGr])
                            if j == 0:
                                mm(psWT[0:64, 0:L], A_tok[0:L, hs], MTb[k][0:L, 0:L], [Atr, MTbr[k]], [psWTr])
                            else:
                                mm(psWT[64:128, 0:L], A_tok[0:L, hs], MTb[k][0:L, 0:L], [Atr, MTbr[k]], [psWTr],
                                   tile_position=(0, 64))
                        act(Gp[q][0:L, :], pG[0:L, 0:128], AF.Copy, [pGr], [Gpr[q]])
                        act(WTs[q][:, 0:L], psWT[:, 0:L], AF.Copy, [psWTr], [WTsr[q]])
                        self.breserved.discard(iWT)
                    for p in pairs:
                        q = p - p0
                        pU, pUr = self.bank()
                        mm(pU[0:L, 0:128], WTs[q][:, 0:L], Hb[:, p, :], [WTsr[q], Hbr[p]], [pUr])
                        tt(S.dve, Us[q][0:L, :], pU[0:L, 0:128], Gp[q][0:L, :], ALU.add, [pUr, Gpr[q]], [Usr[q]])
                    for p in pairs:
                        q = p - p0
                        pYo, pYor = self.bank()
                        mm(pYo[0:L, 0:128], ART[:, p, 1, 0:L], Hb[:, p, :], [ARTr, Hbr[p]], [pYor], start=True, stop=False, inc=False)
                        for j in range(2):
                            k = slot[(p, j)]
                            rows, hs = hv(p, j)
                            cs = slice(64 * j, 64 * j + 64)
                            mm(pYo[0:L, cs], AKRK[k][0:L, 1, 0:L], Vb[0:L, hs], [AKRKr[k], Vbr], [pYor], start=False, stop=False, inc=False)
                            mm(pYo[0:L, cs], ARB[k][0:L, 0:L], Us[q][0:L, cs], [ARBr[k], Usr[q]], [pYor], start=False, stop=(j == 1),
                               inc=(j == 1))
                        pH, pHr = self.bank()
                        ps_ = slice(p * 128, (p + 1) * 128)
                        mm(pH[:, 0:128], K_tok[0:L, ps_], Vb[0:L, ps_], [Ktr, Vbr], [pHr], start=True, stop=False, inc=False)
                        mm(pH[:, 0:128], B_tok[0:L, ps_], Us[q][0:L, :], [Btr, Usr[q]], [pHr], start=False, stop=True)
                        act(Ytok[0:L, p * 128:(p + 1) * 128], pYo[0:L, 0:128], AF.Copy, [pYor], [Ytr])
                        for j in range(2):
                            rows = slice(64 * j, 64 * j + 64)
                            stt(S.dve, Hf[rows, p, rows], Hf[rows, p, rows], gC[rows, p:p + 1], pH[rows, rows], ALU.mult, ALU.add,
                                [pHr, Hfr[p], gCr], [Hfr[p]])
                        act(Hb[:, p, :], Hf[:, p, :], AF.Copy, [Hfr[p]], [Hbr[p]])
                Y3 = lambda L_: Ytok[0:L_, :].rearrange("p (h v) -> p h v", h=16)
                YSQ = SIGW
                S.op(S.dve, lambda e: e.tensor_reduce(out=st[0:L, 0:16], in_=Y3(L), axis=AX.X, op=ALU.add),
                     reads=[Ytr], writes=[str_])
                act(YSQ[0:L, :], Ytok[0:L, :], AF.Square, [Ytr], [SIGWr])
                S.op(S.dve, lambda e: e.tensor_reduce(out=st[0:L, 16:32], in_=YSQ[0:L, :].rearrange("p (h v) -> p h v", h=16),
                                                      axis=AX.X, op=ALU.add), reads=[SIGWr], writes=[str_])
                S.op(S.dve, lambda e: e.tensor_scalar(out=st[0:L, 0:16], in0=st[0:L, 0:16], scalar1=1.0 / 64.0, scalar2=None,
                                                      op0=ALU.mult), reads=[str_], writes=[str_])
                tt(S.dve, st[0:L, 32:48], st[0:L, 0:16], st[0:L, 0:16], ALU.mult, [str_], [str_])
                stt(S.dve, st[0:L, 32:48], st[0:L, 16:32], 1.0 / 64.0, st[0:L, 32:48], ALU.mult, ALU.subtract, [str_], [str_])
                act(st[0:L, 48:64], st[0:L, 32:48], AF.Ln, [str_, rcr], [str_], bias=gnv[0:L, 0:1])
                act(st[0:L, 48:64], st[0:L, 48:64], AF.Exp, [str_], [str_], scale=-0.5)
                tt(S.dve, Y3(L), Y3(L), bc(st[0:L, 0:16], [L, 16, 64], 2), ALU.subtract, [Ytr, str_], [Ytr])
                tt(S.dve, Y3(L), Y3(L), bc(st[0:L, 48:64], [L, 16, 64], 2), ALU.mult, [Ytr, str_], [Ytr])
                tt(S.dve, Ytok[0:L, :], Ytok[0:L, :], lng_b[0:L, :], ALU.mult, [Ytr, rcr], [Ytr])
                tt(S.dve, Ytok[0:L, :], Ytok[0:L, :], lnb_b[0:L, :], ALU.add, [Ytr, rcr], [Ytr])
                tt(S.pool, YSQ[0:L, :].rearrange("p (h v) -> p h v", h=16), Vf[0:L, :].rearrange("p (h v) -> p h v", h=16),
                   bc(BS[0:L, :], [L, 16, 64], 2), ALU.mult, [Vfr, BSr], [SIGWr])
                tt(S.dve, Ytok[0:L, :], Ytok[0:L, :], YSQ[0:L, :], ALU.add, [Ytr, SIGWr], [Ytr])
                for half in range(2):
                    ps, pr = self.bank()
                    for q4 in range(4):
                        fc = half * 4 + q4
                        S.op(S.pe, lambda e: e.transpose(ps[:, q4 * L:(q4 + 1) * L], Ytok[0:L, fc * 128:(fc + 1) * 128],
                                                         identf[0:L, 0:L]),
                             reads=[Ytr, rcr], writes=[pr], inc=(q4 == 3))
                    tt(S.dve, yT[:, half * 4:(half + 1) * 4, 0:L], ps[:, 0:4 * L].rearrange("p (c w) -> p c w", c=4),
                       GB[:, half * 4:(half + 1) * 4, 0:L], ALU.mult, [pr, GBr], [yTr])
                if last:
                    for j in range(2):
                        rows = slice(64 * j, 64 * j + 64)
                        S.dma(S.sp, self.dram["wkv_p"].rearrange("(p two) k v -> two k p v", two=2)[j],
                              Hf[rows, :, rows], reads=Hfr, is_output=True)
                    es2.close()
                    self.rwkv_samples(es, li, dict(RB=RB, KB=KB, KKB=KKB, AB=AB, GB=GB, SIGW=SIGW, vTs=vTs, SCR=SCR, yT=yT,
                                                   RBr=RBr, KBr=KBr, KKBr=KKBr, ABr=ABr, GBr=GBr, SIGWr=SIGWr, vTsr=vTsr,
                                                   SCRr=xxr, yTr=yTr, rc=rc, rcr=rcr, gnv=gnv))
                if ti + 1 < 17:
                    prep(ti + 1)
                for blk in range(4):
                    wb, wr = self.ws_next(128, 2048, kc=8)
                    for oc in range(2):
                        ps, pr = self.bank()
                        for fc in range(NCH):
                            mm(ps[:, 0:W], wb[:, fc, oc * 128:(oc + 1) * 128], yT[:, fc, 0:W], [wr, yTr], [pr],
                               start=(fc == 0), stop=(fc == NCH - 1), inc=(fc == NCH - 1))
                        ec = blk * 2 + oc
                        tt(S.dve, self.hT[:, ec, a:b], self.hT[:, ec, a:b], ps[:, 0:W], ALU.add, [pr] + hrs, hrs)

    def rwkv_samples(self, es, li, T):
        S = self.S
        R = S.new_res
        rc, rcr = T["rc"], T["rcr"]
        blockones = rc[:, 7, :]
        id64 = rc[:, 8, 2:66]
        selS = rc[0:32, 8, 66:82]
        NB = 2
        sbx = lambda n, shp, dt: self.sb(es, n, shp, dt)
        DEC = sbx("DEC", [128, NCH, NSMP], F32)
        SS = sbx("SS", [128, NB, NCH, 64], F32)
        XD = sbx("XD", [128, NB, NCH, 64], F32)
        TT_ = sbx("TTs", [128, NB, NCH, 64], F32)
        XB = [sbx("XB", [128, NB, NCH, 64], F32) for _ in range(5)]
        sa = sbx("sa", [128, NB, NCH], F32)
        YS = sbx("YS", [128, NCH, NSMP], F32)
        YQ = sbx("YQ", [128, NCH, NSMP], F32)
        MU = sbx("MU", [128, NCH, NSMP], F32)
        RS = sbx("RS", [128, NCH, NSMP], F32)
        DECr, SSr, XDr, TTr, sar, YSr, YQr, MUr, RSr = [R(n) for n in "DEC SS XD TT sa YS YQ MU RS".split()]
        XBr = [R("XB%d" % i) for i in range(5)]

        def mm(out, lhsT, rhs, reads, writes, **kw):
            return S.op(S.pe, lambda e: e.matmul(out, lhsT=lhsT, rhs=rhs, start=True, stop=True, **kw), reads=reads, writes=writes)

        def act(out, in_, func, reads, writes, **kw):
            return S.op(S.act, lambda e: e.activation(out=out, in_=in_, func=func, **kw), reads=reads, writes=writes)

        def tt(eng, out, in0, in1, op, reads, writes):
            return S.op(eng, lambda e: e.tensor_tensor(out=out, in0=in0, in1=in1, op=op), reads=reads, writes=writes)

        sw = self.dram["s_wkv"].rearrange("b (p two) v k -> b two v p k", two=2)
        ow = self.dram["wkv_s"].rearrange("b (p two) v k -> b two v p k", two=2)
        for fc in range(NCH):
            ps, pr = self.bank()
            mm(ps[:, 0:NSMP], T["SIGW"][0:32, fc * 128:(fc + 1) * 128], selS, [T["SIGWr"], rcr], [pr])
            act(DEC[:, fc, :], ps[:, 0:NSMP], AF.Exp, [pr], [DECr])
        srcs = [(T["KKB"], T["KKBr"], 16), (DEC, DECr, 0), (T["AB"], T["ABr"], 16), (T["KB"], T["KBr"], 16), (T["RB"], T["RBr"], 16)]
        for b0 in range(0, NSMP, NB):
            for bb in range(NB):
                for j in range(2):
                    S.dma(S.sp, SS[64 * j:64 * j + 64, bb, :, :], sw[b0 + bb, j], writes=[SSr])
            for qi, (src, srcr, off) in enumerate(srcs):
                xin = src[:, :, off + b0:off + b0 + NB].rearrange("q p b -> q b p").unsqueeze(3).to_broadcast([128, NB, NCH, 64])
                idb = id64.unsqueeze(1).unsqueeze(1).to_broadcast([128, NB, NCH, 64])
                tt(S.dve, XD[:, :, :, :], xin, idb, ALU.mult, [srcr, rcr], [XDr])
                xdf = XD[:, :, :, :].rearrange("q b p k -> q (b p k)")
                xbf = XB[qi][:, :, :, :].rearrange("q b p k -> q (b p k)")
                for i in range(NB * NCH * 64 // 512):
                    ps, pr = self.bank()
                    mm(ps[:, :], blockones, xdf[:, i * 512:(i + 1) * 512], [XDr, rcr], [pr])
                    act(xbf[:, i * 512:(i + 1) * 512], ps[:, :], AF.Copy, [pr], [XBr[qi]])
            red = lambda out, in_, rd, wr: S.op(S.dve, lambda e: e.tensor_reduce(out=out, in_=in_, axis=AX.X, op=ALU.add),
                                                reads=rd, writes=wr)
            A4 = lambda t: t[:, :, :, :]
            tt(S.dve, A4(TT_), A4(SS), A4(XB[0]), ALU.mult, [SSr, XBr[0]], [TTr])
            red(sa[:, :, :], A4(TT_), [TTr], [sar])
            tt(S.dve, A4(SS), A4(SS), A4(XB[1]), ALU.mult, [SSr, XBr[1]], [SSr])
            tt(S.dve, A4(TT_), A4(XB[2]), sa[:, :, :].unsqueeze(3).to_broadcast([128, NB, NCH, 64]), ALU.mult, [XBr[2], sar, TTr], [TTr])
            tt(S.dve, A4(SS), A4(SS), A4(TT_), ALU.subtract, [SSr, TTr], [SSr])
            vsb = T["vTs"][:, :, b0:b0 + NB].rearrange("q p b -> q b p").unsqueeze(3).to_broadcast([128, NB, NCH, 64])
            tt(S.dve, A4(TT_), A4(XB[3]), vsb, ALU.mult, [XBr[3], T["vTsr"], TTr], [TTr])
            tt(S.dve, A4(SS), A4(SS), A4(TT_), ALU.add, [SSr, TTr], [SSr])
            tt(S.dve, A4(TT_), A4(SS), A4(XB[4]), ALU.mult, [SSr, XBr[4], TTr], [TTr])
            red(YS[:, :, b0:b0 + NB].rearrange("q p b -> q b p"), A4(TT_), [TTr], [YSr])
            for bb in range(NB):
                for j in range(2):
                    S.dma(S.sp, ow[b0 + bb, j], SS[64 * j:64 * j + 64, bb, :, :], reads=[SSr], is_output=True)
        fl = lambda t: t[:, :, :].rearrange("q p b -> q (p b)")
        ps1, ps1r = self.bank()
        mm(ps1[:, 0:128], blockones, fl(YS), [YSr, rcr], [ps1r])
        tt(S.pool, YQ[:, :, :], YS[:, :, :], YS[:, :, :], ALU.mult, [YSr], [YQr])
        ps2, ps2r = self.bank()
        mm(ps2[:, 0:128], blockones, fl(YQ), [YQr, rcr], [ps2r])
        act(fl(MU), ps1[:, 0:128], AF.Copy, [ps1r], [MUr], scale=1.0 / 64.0)
        tt(S.dve, fl(YQ), fl(MU), fl(MU), ALU.mult, [MUr, ps2r], [YQr])
        S.op(S.dve, lambda e: e.scalar_tensor_tensor(out=fl(RS), in0=ps2[:, 0:128], scalar=1.0 / 64.0, in1=fl(YQ),
                                                     op0=ALU.mult, op1=ALU.subtract), reads=[ps2r, YQr], writes=[RSr])
        act(fl(RS), fl(RS), AF.Ln, [RSr, rcr], [RSr], bias=T["gnv"][:, 0:1])
        act(fl(RS), fl(RS), AF.Exp, [RSr], [RSr], scale=-0.5)
        tt(S.dve, fl(YS), fl(YS), fl(MU), ALU.subtract, [YSr, MUr], [YSr])
        tt(S.dve, fl(YS), fl(YS), fl(RS), ALU.mult, [YSr, RSr], [YSr])
        vb = lambda row: self.vecs[:, row, :].unsqueeze(2).to_broadcast([128, NCH, NSMP])
        tt(S.dve, YS[:, :, :], YS[:, :, :], vb(V_LNG), ALU.mult, [YSr, self.cres], [YSr])
        tt(S.dve, YS[:, :, :], YS[:, :, :], vb(V_LNB), ALU.add, [YSr, self.cres], [YSr])
        S.op(S.pool, lambda e: e.tensor_copy(out=YQ[:, :, :], in_=T["SCR"][:, :, 16:32]), reads=[T["SCRr"]], writes=[YQr])
        ps3, ps3r = self.bank()
        mm(ps3[:, 0:128], blockones, fl(YQ), [YQr, rcr], [ps3r])
        tt(S.dve, fl(MU), ps3[:, 0:128], fl(T["vTs"]), ALU.mult, [ps3r, T["vTsr"]], [MUr])
        tt(S.dve, fl(YS), fl(YS), fl(MU), ALU.add, [YSr, MUr], [YSr])
        tt(S.dve, T["yT"][:, :, 16:32], YS[:, :, :], T["GB"][:, :, 16:32], ALU.mult, [YSr, T["GBr"]], [T["yTr"]])


_LAYERS = 4


def _const_tables():
    inv = (1.0 / (np.float32(10000.0) ** np.linspace(0.0, 1.0, 128, dtype=np.float32))).astype(np.float32)
    pos = np.concatenate([np.arange(TP), np.full(NSMP, 16384)]).astype(np.float32)
    ang = (pos[None, :] * inv[:, None]).astype(np.float32).astype(np.float64)
    cos_t = np.cos(ang).astype(np.float32)
    sin_t = np.sin(ang).astype(np.float32)
    l = np.concatenate([np.arange(TP) % 128, np.zeros(NSMP)]).astype(np.float64)
    qkd = np.zeros((8, TT), np.float64)
    for h in range(4):
        qkd[h] = GAMMAS[h] ** (l + 1.0)
        qkd[4 + h] = GAMMAS[h] ** (-(l + 1.0)) * (256.0 ** -0.5)
    consts = np.zeros((128, 5, 128), np.float32)
    consts[:, 0, :] = np.eye(128)
    consts[:, 1, :] = 1.0 / D
    m = np.arange(128)
    consts[:, 2, :] = (m[None, :] >= m[:, None])
    deltaR = np.broadcast_to(np.eye(16, dtype=np.float32)[None], (128, 16, 16)).copy()
    c0 = math.exp(-0.5)
    row = m[:, None]
    col = m[None, :]
    rconst = np.zeros((128, 11, 128), np.float32)
    rconst[:, 0, :] = -c0 * (row <= col)
    rconst[:, 1, :] = -c0 * (row < col)
    rconst[:, 2, :] = -c0 * (row > col)
    rconst[:, 3, :] = (col > row)
    rconst[:, 4, :] = (col >= row)
    rconst[:, 5, :] = (row > col)
    rconst[:, 6, :] = -1.0 * (col >= row)
    rconst[:, 7, :] = (row // 64 == col // 64)
    rconst[:, 8, 0:2] = (m[:, None] // 64 == np.arange(2)[None, :])
    rconst[:, 8, 2:66] = (m[:, None] % 64 == np.arange(64)[None, :])
    rconst[0:32, 8, 66:82] = -c0 * (np.arange(32)[:, None] == 16 + np.arange(16)[None, :])
    rconst[:, 9, :] = np.eye(128)
    rconst[:, 10, :] = 2.0 * np.eye(128)
    return cos_t, sin_t, qkd.astype(np.float32), consts, deltaR, rconst


def _vec_layout(v):
    return np.ascontiguousarray(np.asarray(v, np.float32).reshape(8, 128).T)


def kernel(**inp):
    f = lambda k: np.asarray(inp[k], np.float32)
    cos_t, sin_t, qkd, consts, deltaR, rconst = _const_tables()
    rows = ([f("norm_mix")[i] for i in range(4)] + [f("norm_mlp")[i] for i in range(4)] + [f("norm_final")]
            + [f("rwkv_mix")[i] for i in range(6)]
            + [f("rwkv_w0"), f("rwkv_a0"), f("rwkv_k_k"), f("rwkv_k_a"), f("rwkv_r_k").reshape(-1)]
            + [f("conv_w")[i] for i in range(3)] + [f("rwkv_ln_g"), f("rwkv_ln_b")])
    assert len(rows) == NVEC
    vecs = np.ascontiguousarray(np.stack([_vec_layout(r) for r in rows], axis=1))
    w_in = f("ret_w_in").copy()
    for base in (0, 1024):
        blk = w_in[:, :, base:base + 1024].reshape(2, D, 4, 128, 2)
        w_in[:, :, base:base + 1024] = blk.transpose(0, 1, 2, 4, 3).reshape(2, D, 1024)
    shared = {
        "vecs": vecs, "cos_t": cos_t, "sin_t": sin_t, "qkd_t": qkd, "consts": consts, "deltaR": deltaR,
        "ret_w_in": w_in, "ret_w_out": f("ret_w_out"), "rwkv_w_rkv": f("rwkv_w_rkv"), "rwkv_w_o": f("rwkv_w_o"),
        "conv_w_in": f("conv_w_in"), "conv_w_out": f("conv_w_out"), "mlp_w1": f("mlp_w1"), "mlp_w2": f("mlp_w2"),
        "rwkv_w1a1": np.ascontiguousarray(np.concatenate([f("rwkv_w1"), f("rwkv_a1")], axis=1)),
        "rwkv_g1": f("rwkv_g1"),
        "rwkv_w2a2": np.ascontiguousarray(np.concatenate([f("rwkv_w2"), f("rwkv_a2")], axis=1)),
        "rwkv_g2": f("rwkv_g2"),
        "ln_gb": np.ascontiguousarray(np.stack([f("rwkv_ln_g"), f("rwkv_ln_b"), f("rwkv_w0")])),
        "rconst": rconst,
    }
    xp, xs, meta = f("x_prompt"), f("x_sample"), f("meta_tokens")
    in_maps = []
    for b in range(N_CORES):
        sl = slice(NSMP * b, NSMP * (b + 1))
        xall = np.concatenate([meta, xp[b], xs[sl, 0, :]], axis=0)
        m = dict(shared)
        m["xT"] = np.ascontiguousarray(xall.T)
        m["s_ret0"] = np.ascontiguousarray(f("state_ret_l0")[sl])
        m["s_ret3"] = np.ascontiguousarray(f("state_ret_l3")[sl])
        m["s_shiftT"] = np.ascontiguousarray(f("state_rwkv_shift_l1")[sl].T)
        m["s_wkv"] = np.ascontiguousarray(f("state_rwkv_wkv_l1")[sl])
        m["s_convT"] = np.ascontiguousarray(f("state_conv_l2")[sl].transpose(2, 1, 0))
        in_maps.append(m)
    prog = Prog(layers=_LAYERS)
    nc = prog.build()
    in_maps = [{k: v for k, v in m.items() if k in prog.inputs} for m in in_maps]
    res = run_bass_kernel_spmd(nc, in_maps, core_ids=list(range(N_CORES)))
    R = res.results
    B = N_CORES
    y_p = np.stack([R[b]["yT"][:, 16:TP].T for b in range(B)])
    y_s = np.concatenate([R[b]["yT"][:, TP:TT].T for b in range(B)])[:, None, :]
    zz = {"ret0_p": (4, 256, 512), "ret3_p": (4, 256, 512), "ret0_s": (NSMP, 4, 256, 512), "ret3_s": (NSMP, 4, 256, 512),
          "shiftT": (D, 1 + NSMP), "wkv_p": (16, 64, 64), "wkv_s": (NSMP, 16, 64, 64), "convT_p": (D, 2),
          "convT_s": (D, 2, NSMP)}
    for b in range(B):
        for k, shp in zz.items():
            if k not in R[b]:
                R[b][k] = np.zeros(shp, np.float32)
    cat = lambda k: np.concatenate([R[b][k] for b in range(B)], axis=0)
    stk = lambda k: np.stack([R[b][k] for b in range(B)])
    shift_p = np.stack([R[b]["shiftT"][:, 0] for b in range(B)])
    shift_s = np.concatenate([R[b]["shiftT"][:, 1:].T for b in range(B)])
    conv_p = np.stack([R[b]["convT_p"].T for b in range(B)])
    conv_s = np.concatenate([R[b]["convT_s"].transpose(2, 1, 0) for b in range(B)])
    outs = (y_p, y_s, stk("ret0_p"), cat("ret0_s"), shift_p, shift_s, stk("wkv_p").transpose(0, 1, 3, 2), cat("wkv_s"),
            conv_p, conv_s, stk("ret3_p"), cat("ret3_s"))
    return tuple(np.ascontiguousarray(o, dtype=np.float32) for o in outs)
```

```python
import contextlib
import math
import numpy as np
import concourse.bass as bass
import concourse.mybir as mybir
from concourse.bass_utils import run_bass_kernel_spmd

F32 = mybir.dt.float32
F32R = mybir.dt.float32r
BF16 = mybir.dt.bfloat16
AF = mybir.ActivationFunctionType
ALU = mybir.AluOpType
AX = mybir.AxisListType

D = 1024
NCH = 8
TP = 2064
NSMP = 16
TT = TP + NSMP
EPS = 1e-6
GN_EPS = 64e-5
N_CORES = 8
GAMMAS = [1.0 - 2.0 ** (-5.0 - h) for h in range(4)]

GROUPS = [
    dict(c0=0, c1=1024, tiles=[(0, 512), (512, 1024)]),
    dict(c0=1024, c1=2080, tiles=[(1024, 1376), (1376, 1728), (1728, 2080)]),
]
V_NMIX, V_NMLP, V_NFIN, V_RMIX, V_W0, V_A0, V_KK, V_KA, V_RK, V_CW, V_LNG, V_LNB = 0, 4, 8, 9, 15, 16, 17, 18, 19, 20, 23, 24
NVEC = 25
RW_BLOCKS = [(128, 2048)] * 16 + [(128, 1024), (128, 1280), (64, 2048), (128, 1024), (32, 1024)]
RW_ORDER = list(range(8)) + [16, 18, 8, 9, 10, 11, 17, 19, 20, 12, 13, 14, 15]
NRING = 6
PBATCH = 3
CH_DT = F32R


class Res:
    __slots__ = ("w", "r", "name")

    def __init__(self, name="", r=None):
        self.w = None
        self.r = dict(r) if r else {}
        self.name = name


class SemCtr:
    __slots__ = ("sem", "count", "name")

    def __init__(self, sem, name):
        self.sem = sem
        self.count = 0
        self.name = name


class Eng:
    def __init__(self, h, name, self_sync):
        self.h = h
        self.ctr = None
        self.name = name
        self.known = {}
        self.self_sync = self_sync


SEM_LIMIT = 12000


class Sched:
    def __init__(self, nc, n_dma_sems=20):
        self.nc = nc
        self._cms = []
        self._nsem = 0
        self.pe = Eng(nc.tensor, "pe", False)
        self.act = Eng(nc.scalar, "act", True)
        self.dve = Eng(nc.vector, "dve", True)
        self.pool = Eng(nc.gpsimd, "pool", True)
        self.sp = Eng(nc.sync, "sp", False)
        self.engines = (self.pe, self.act, self.dve, self.pool)
        for e in self.engines:
            e.ctr = self._mk(e.name)
        self.dsems = [self._mk("dma%d" % i) for i in range(n_dma_sems)]
        self.dnext = 0
        self.out_dmas = []
        self.old_ctrs = []

    def _mk(self, name):
        cm = self.nc.semaphore("s%d_%s" % (self._nsem, name))
        self._nsem += 1
        s = cm.__enter__()
        self._cms.append(cm)
        return SemCtr(s, name)

    def close(self):
        for cm in reversed(self._cms):
            cm.__exit__(None, None, None)

    def new_res(self, name=""):
        r = {}
        for e in self.engines:
            if e.ctr.count:
                r[e.ctr] = e.ctr.count
        for c in self.dsems:
            if c.count:
                r[c] = c.count
        return Res(name, r)

    def _wait(self, eng, deps):
        need = {}
        for (ctr, val) in deps:
            if eng.ctr is ctr and not eng.self_sync:
                continue
            if need.get(ctr, 0) < val:
                need[ctr] = val
        for ctr, val in need.items():
            if eng.known.get(ctr, 0) < val:
                eng.h.wait_ge(ctr.sem, val)
                eng.known[ctr] = val

    @staticmethod
    def _deps(reads, writes):
        deps = []
        for r in reads:
            if r.w is not None:
                deps.append(r.w)
        for w in writes:
            if w.w is not None:
                deps.append(w.w)
            deps.extend(w.r.items())
        return deps

    @staticmethod
    def _mark(reads, writes, tag):
        ctr, val = tag
        for r in reads:
            if r.r.get(ctr, 0) < val:
                r.r[ctr] = val
        for w in writes:
            w.w = tag
            w.r = {}

    def op(self, eng, fn, reads=(), writes=(), inc=True):
        self._wait(eng, self._deps(reads, writes))
        ins = fn(eng.h)
        if inc:
            eng.ctr.count += 1
            ins.then_inc(eng.ctr.sem, 1)
            tag = (eng.ctr, eng.ctr.count)
        else:
            tag = (eng.ctr, eng.ctr.count + 1)
        self._mark(reads, writes, tag)
        if inc and eng.ctr.count >= SEM_LIMIT:
            eng.ctr = self._mk(eng.name)
        return ins

    def dma(self, eng, out, in_, reads=(), writes=(), is_output=False):
        ctr = self.dsems[self.dnext]
        if ctr.count >= SEM_LIMIT:
            self.old_ctrs.append(ctr)
            prev = (ctr, ctr.count)
            ctr = self._mk("dma%d" % self.dnext)
            self.dsems[self.dnext] = ctr
        else:
            prev = (ctr, ctr.count) if ctr.count else None
        self.dnext = (self.dnext + 1) % len(self.dsems)
        deps = self._deps(reads, writes)
        if prev is not None:
            deps.append(prev)
        self._wait(eng, deps)
        ctr.count += 16
        eng.h.dma_start(out=out, in_=in_).then_inc(ctr.sem, 16)
        tag = (ctr, ctr.count)
        self._mark(reads, writes, tag)
        if is_output:
            self.out_dmas.append(tag)
        return tag

    def finish(self):
        deps = list(self.out_dmas)
        for c in list(self.dsems) + self.old_ctrs:
            if c.count:
                deps.append((c, c.count))
        for e in self.engines:
            if e.ctr.count:
                deps.append((e.ctr, e.ctr.count))
        self._wait(self.sp, deps)


class Prog:
    def __init__(self, layers=4, with_rwkv=True):
        self.layers = layers
        self.with_rwkv = with_rwkv
        self.nc = bass.Bass("TRN2", target_bir_lowering=False)
        self.S = Sched(self.nc)
        self.es = contextlib.ExitStack()
        self.dram = {}
        self.inputs = set()
        self._uid = 0

    def din(self, name, shape, dt=F32):
        t = self.nc.dram_tensor(name, list(shape), dt, kind="ExternalInput").ap()
        self.dram[name] = t
        self.inputs.add(name)
        return t

    def dout(self, name, shape, dt=F32):
        t = self.nc.dram_tensor(name, list(shape), dt, kind="ExternalOutput").ap()
        self.dram[name] = t
        return t

    def sb(self, es, name, shape, dt):
        self._uid += 1
        return es.enter_context(self.nc.sbuf_tensor("%s_%d" % (name, self._uid), list(shape), dt))

    def bank(self):
        for _ in range(8):
            i = self.bnext
            self.bnext = (self.bnext + 1) % 8
            if i not in self.breserved:
                return self.banks[i], self.bres[i]
        raise RuntimeError("no psum bank")

    def reserve_bank(self):
        bk, r = self.bank()
        i = self.banks.index(bk)
        self.breserved.add(i)
        return bk, r, i

    def ws_add(self, ap, p, n, dst=None, dst_res=None):
        self.wplan.append(dict(ap=ap, p=p, n=n, dst=dst, dst_res=dst_res))

    def _ws_conv(self):
        i = self.w_cv
        spec = self.wplan[i]
        bf, ores = self.wbufs[i % len(self.wbufs)]
        p, n = spec["p"], spec["n"]
        if spec.get("cached") is not None:
            ci = spec["cached"]
            self.S.dma(self.S.sp, bf[0:p, 0:n], self.wcache[ci, 0:p, 0:n], reads=[self.wcres[ci]], writes=[ores])
        else:
            src = spec["ap"]
            dst = bf[0:p, 0:n]
            if len(src.shape) == 3:
                dst = dst.rearrange("p (c e) -> p c e", c=src.shape[1])
            self.S.dma(self.S.pool, dst, src, writes=[ores])
        self.w_cv += 1

    def ws_next(self, p, n, kc=None):
        i = self.w_use
        spec = self.wplan[i]
        assert spec["p"] == p and spec["n"] == n, (i, spec["p"], spec["n"], p, n)
        while self.w_cv <= min(i + len(self.wbufs) - 2, len(self.wplan) - 1):
            self._ws_conv()
        self.w_use += 1
        bf, r = self.wbufs[i % len(self.wbufs)]
        v = bf[0:p, 0:n]
        if kc is not None:
            v = v.rearrange("p (c e) -> p c e", c=kc)
        return v, r

    def rmsnorm_tile(self, c0, c1, hres, gi, out_fn, out_res, out_f32_fn=None):
        S = self.S
        w = c1 - c0
        ps, pr = self.bank()
        for c in range(NCH):
            sq, sqr = self.sqb[c % 2]
            S.op(S.act, lambda e: e.activation(out=sq[:, 0:w], in_=self.hT[:, c, c0:c1], func=AF.Square),
                 reads=hres, writes=[sqr])
            S.op(S.pe, lambda e: e.matmul(ps[:, 0:w], lhsT=self.onesD[:], rhs=sq[:, 0:w],
                                          start=(c == 0), stop=(c == NCH - 1)),
                 reads=[sqr, self.cres], writes=[pr], inc=True)
        rs, rr = self.rsb
        S.op(S.act, lambda e: e.activation(out=rs[:, 0:w], in_=ps[:, 0:w], func=AF.Ln, bias=self.epsv[:, 0:1]),
             reads=[pr, self.cres], writes=[rr])
        S.op(S.act, lambda e: e.activation(out=rs[:, 0:w], in_=rs[:, 0:w], func=AF.Exp, scale=-0.5),
             reads=[rr], writes=[rr])
        for c in range(NCH):
            S.op(S.dve, lambda e: e.scalar_tensor_tensor(out=out_fn(c), in0=self.hT[:, c, c0:c1],
                                                         scalar=self.vecs[:, gi, c:c + 1], in1=rs[:, 0:w],
                                                         op0=ALU.mult, op1=ALU.mult),
                 reads=hres + [rr, self.cres], writes=[out_res])

    def norm_group(self, g, gi):
        G = GROUPS[g]
        for ti, (a, b) in enumerate(G["tiles"]):
            la = a - G["c0"]
            self.rmsnorm_tile(a, b, [self.hres[(g, ti)]], gi,
                              lambda c: self.xn[:, c, la:la + (b - a)], self.xnres[ti])

    def plan_mlp(self, li):
        w1 = self.dram["mlp_w1"]
        w2 = self.dram["mlp_w2"]
        for blk in range(16):
            self.ws_add(w1[li].rearrange("(c p) e -> p c e", p=128)[:, :, blk * 256:(blk + 1) * 256], 128, 2048)
        for cb in range(4):
            for kg in range(4):
                self.ws_add(w2[li, kg * 1024:(kg + 1) * 1024, cb * 256:(cb + 1) * 256]
                            .rearrange("(c p) e -> p c e", p=128), 128, 2048)

    def mlp_group(self, li, g, h1, h1res, rl):
        S = self.S
        G = GROUPS[g]
        c0 = G["c0"]
        self.norm_group(g, V_NMLP + li)
        for blk in range(16):
            wb, wr = self.ws_next(128, 2048, kc=8)
            for ti, (a, b) in enumerate(G["tiles"]):
                w = b - a
                for oc in range(2):
                    ps, pr = self.bank()
                    for c in range(NCH):
                        S.op(S.pe, lambda e: e.matmul(ps[:, 0:w], lhsT=wb[:, c, oc * 128:(oc + 1) * 128],
                                                      rhs=self.xn[:, c, a - c0:b - c0],
                                                      start=(c == 0), stop=(c == NCH - 1)),
                             reads=[wr, self.xnres[ti]], writes=[pr], inc=(c == NCH - 1))
                    rt, rtr = rl[(blk * 2 + oc) % 2]
                    S.op(S.act, lambda e: e.activation(out=rt[:, 0:w], in_=ps[:, 0:w], func=AF.Relu),
                         reads=[pr], writes=[rtr])
                    fc = blk * 2 + oc
                    S.op(S.dve, lambda e: e.tensor_tensor(out=h1[:, fc, a - c0:b - c0], in0=rt[:, 0:w],
                                                          in1=rt[:, 0:w], op=ALU.mult),
                         reads=[rtr], writes=[h1res[ti]])
        for cb in range(4):
            accs = {}
            for ti in range(len(G["tiles"])):
                for oc in range(2):
                    accs[(ti, oc)] = self.reserve_bank()
            for kg in range(4):
                wb, wr = self.ws_next(128, 2048, kc=8)
                for ti, (a, b) in enumerate(G["tiles"]):
                    w = b - a
                    for oc in range(2):
                        ps, pr, _ = accs[(ti, oc)]
                        for c in range(NCH):
                            first = (kg == 0 and c == 0)
                            last = (kg == 3 and c == NCH - 1)
                            S.op(S.pe, lambda e: e.matmul(ps[:, 0:w], lhsT=wb[:, c, oc * 128:(oc + 1) * 128],
                                                          rhs=h1[:, kg * 8 + c, a - c0:b - c0],
                                                          start=first, stop=last),
                                 reads=[wr, h1res[ti]], writes=[pr], inc=(c == NCH - 1))
            for ti, (a, b) in enumerate(G["tiles"]):
                w = b - a
                for oc in range(2):
                    ps, pr, bi = accs[(ti, oc)]
                    ec = cb * 2 + oc
                    S.op(S.dve, lambda e: e.tensor_tensor(out=self.hT[:, ec, a:b], in0=self.hT[:, ec, a:b],
                                                          in1=ps[:, 0:w], op=ALU.add),
                         reads=[pr], writes=[self.hres[(g, ti)]])
                    self.breserved.discard(bi)

    def mlp_layer(self, li):
        with contextlib.ExitStack() as es:
            self.xn = self.sb(es, "xn", [128, NCH, 1056], BF16)
            self.xnres = [self.S.new_res("xn%d" % i) for i in range(3)]
            h1 = self.sb(es, "h1", [128, 32, 1056], BF16)
            h1res = [self.S.new_res("h1_%d" % i) for i in range(3)]
            rl = [(self.sb(es, "relu", [128, 512], F32), self.S.new_res("relu")) for _ in range(2)]
            for g in range(2):
                self.mlp_group(li, g, h1, h1res, rl)

    def plan_conv(self):
        wi = self.dram["conv_w_in"].rearrange("(c p) e -> p c e", p=128)
        wo = self.dram["conv_w_out"].rearrange("(c p) e -> p c e", p=128)
        for g in range(2):
            for fp in range(4):
                for part in range(3):
                    self.ws_add(wi[:, :, part * 1024 + fp * 256: part * 1024 + (fp + 1) * 256], 128, 2048)
            for blk in range(4):
                self.ws_add(wo[:, :, blk * 256:(blk + 1) * 256], 128, 2048)

    def conv_layer(self, li):
        S = self.S
        with contextlib.ExitStack() as es:
            self.xn = self.sb(es, "xn", [128, NCH, 1056], BF16)
            self.xnres = [self.S.new_res("xn%d" % i) for i in range(3)]
            bB = self.sb(es, "cv_b", [128, 2, 1056], F32)
            cB = self.sb(es, "cv_c", [128, 2, 1056], F32)
            uB = self.sb(es, "cv_u", [128, 2, 1058], F32)
            yB = self.sb(es, "cv_y", [128, 1056], F32)
            zT = self.sb(es, "cv_z", [128, 8, 1056], BF16)
            carry = self.sb(es, "cv_carry", [128, 8, 2], F32)
            st = self.sb(es, "cv_st", [128, 8, 2, NSMP], F32)
            snew = self.sb(es, "cv_snew", [128, 8, 2, NSMP], F32)
            pout = self.sb(es, "cv_pout", [128, 8, 2], F32)
            bres, cres_, ures, yres, zres, carres, stres, snres, pores = [S.new_res(n) for n in
                                                                          "b c u y z carry st snew pout".split()]
            S.dma(S.sp, st[:], self.dram["s_convT"].rearrange("(c p) j b -> p c j b", p=128), writes=[stres])
            S.op(S.pool, lambda e: e.memset(carry[:], 0.0), writes=[carres])
            for g in range(2):
                G = GROUPS[g]
                c0 = G["c0"]
                tg = G["c1"] - c0
                tp = tg - (NSMP if g == 1 else 0)
                self.norm_group(g, V_NMIX + li)
                for fp in range(4):
                    for part, (buf, br) in enumerate(((bB, bres), (cB, cres_), (None, None))):
                        wb, wr = self.ws_next(128, 2048, kc=8)
                        for ti, (a, b) in enumerate(G["tiles"]):
                            w = b - a
                            la = a - c0
                            for oc in range(2):
                                ps, pr = self.bank()
                                for c in range(NCH):
                                    S.op(S.pe, lambda e: e.matmul(ps[:, 0:w], lhsT=wb[:, c, oc * 128:(oc + 1) * 128],
                                                                  rhs=self.xn[:, c, la:la + w],
                                                                  start=(c == 0), stop=(c == NCH - 1)),
                                         reads=[wr, self.xnres[ti]], writes=[pr], inc=(c == NCH - 1))
                                if part < 2:
                                    S.op(S.act, lambda e: e.activation(out=buf[:, oc, la:la + w], in_=ps[:, 0:w],
                                                                       func=AF.Copy),
                                         reads=[pr], writes=[br])
                                else:
                                    S.op(S.dve, lambda e: e.tensor_tensor(out=uB[:, oc, 2 + la:2 + la + w],
                                                                          in0=ps[:, 0:w], in1=cB[:, oc, la:la + w],
                                                                          op=ALU.mult),
                                         reads=[pr, cres_], writes=[ures])
                    for oc in range(2):
                        fc = fp * 2 + oc
                        cw = lambda j: self.vecs[:, V_CW + j, fc:fc + 1]
                        S.op(S.dve, lambda e: e.tensor_copy(out=uB[:, oc, 0:2], in_=carry[:, fc, :]),
                             reads=[carres], writes=[ures])
                        S.op(S.dve, lambda e: e.tensor_scalar(out=yB[:, 0:tp], in0=uB[:, oc, 0:tp], scalar1=cw(0),
                                                              scalar2=None, op0=ALU.mult),
                             reads=[ures, self.cres], writes=[yres])
                        S.op(S.dve, lambda e: e.scalar_tensor_tensor(out=yB[:, 0:tp], in0=uB[:, oc, 1:tp + 1],
                                                                     scalar=cw(1), in1=yB[:, 0:tp],
                                                                     op0=ALU.mult, op1=ALU.add),
                             reads=[ures, yres, self.cres], writes=[yres])
                        S.op(S.dve, lambda e: e.scalar_tensor_tensor(out=yB[:, 0:tp], in0=uB[:, oc, 2:tp + 2],
                                                                     scalar=cw(2), in1=yB[:, 0:tp],
                                                                     op0=ALU.mult, op1=ALU.add),
                             reads=[ures, yres, self.cres], writes=[yres])
                        if g == 1:
                            S.op(S.dve, lambda e: e.tensor_scalar(out=yB[:, tp:tg], in0=st[:, fc, 0, :], scalar1=cw(0),
                                                                  scalar2=None, op0=ALU.mult),
                                 reads=[stres, self.cres], writes=[yres])
                            S.op(S.dve, lambda e: e.scalar_tensor_tensor(out=yB[:, tp:tg], in0=st[:, fc, 1, :],
                                                                         scalar=cw(1), in1=yB[:, tp:tg],
                                                                         op0=ALU.mult, op1=ALU.add),
                                 reads=[stres, yres, self.cres], writes=[yres])
                            S.op(S.dve, lambda e: e.scalar_tensor_tensor(out=yB[:, tp:tg], in0=uB[:, oc, 2 + tp:2 + tg],
                                                                         scalar=cw(2), in1=yB[:, tp:tg],
                                                                         op0=ALU.mult, op1=ALU.add),
                                 reads=[ures, yres, self.cres], writes=[yres])
                            S.op(S.act, lambda e: e.activation(out=snew[:, fc, 0, :], in_=st[:, fc, 1, :], func=AF.Copy),
                                 reads=[stres], writes=[snres])
                            S.op(S.act, lambda e: e.activation(out=snew[:, fc, 1, :], in_=uB[:, oc, 2 + tp:2 + tg],
                                                               func=AF.Copy),
                                 reads=[ures], writes=[snres])
                            S.op(S.act, lambda e: e.activation(out=pout[:, fc, :], in_=uB[:, oc, tp:tp + 2], func=AF.Copy),
                                 reads=[ures], writes=[pores])
                        else:
                            S.op(S.act, lambda e: e.activation(out=carry[:, fc, :], in_=uB[:, oc, tp:tp + 2], func=AF.Copy),
                                 reads=[ures], writes=[carres])
                        S.op(S.dve, lambda e: e.tensor_tensor(out=zT[:, fc, 0:tg], in0=yB[:, 0:tg], in1=bB[:, oc, 0:tg],
                                                              op=ALU.mult),
                             reads=[yres, bres], writes=[zres])
                for blk in range(4):
                    wb, wr = self.ws_next(128, 2048, kc=8)
                    for ti, (a, b) in enumerate(G["tiles"]):
                        w = b - a
                        la = a - c0
                        for oc in range(2):
                            ps, pr = self.bank()
                            for c in range(NCH):
                                S.op(S.pe, lambda e: e.matmul(ps[:, 0:w], lhsT=wb[:, c, oc * 128:(oc + 1) * 128],
                                                              rhs=zT[:, c, la:la + w],
                                                              start=(c == 0), stop=(c == NCH - 1)),
                                     reads=[wr, zres], writes=[pr], inc=(c == NCH - 1))
                            ec = blk * 2 + oc
                            S.op(S.dve, lambda e: e.tensor_tensor(out=self.hT[:, ec, a:b], in0=self.hT[:, ec, a:b],
                                                                  in1=ps[:, 0:w], op=ALU.add),
                                 reads=[pr], writes=[self.hres[(g, ti)]])
            S.dma(S.sp, self.dram["convT_p"].rearrange("(c p) j -> p c j", p=128), pout[:], reads=[pores],
                  is_output=True)
            S.dma(S.sp, self.dram["convT_s"].rearrange("(c p) j b -> p c j b", p=128), snew[:], reads=[snres],
                  is_output=True)

    def plan_ret(self, j):
        win = self.dram["ret_w_in"][j].rearrange("(c p) e -> p c e", p=128)
        wout = self.dram["ret_w_out"][j]
        for g in range(2):
            for h in range(4):
                self.ws_add(win[:, :, h * 256:(h + 1) * 256], 128, 2048)
                self.ws_add(win[:, :, 1024 + h * 256:1024 + (h + 1) * 256], 128, 2048)
                for half in range(2):
                    o = 2048 + h * 512 + half * 256
                    self.ws_add(win[:, :, o:o + 256], 128, 2048)
                for half in range(2):
                    o = 4096 + h * 512 + half * 256
                    self.ws_add(win[:, :, o:o + 256], 128, 2048)
                for half in range(2):
                    self.ws_add(wout[h * 512:(h + 1) * 512, half * 512:(half + 1) * 512]
                                .rearrange("(c p) e -> p c e", p=128), 128, 2048)

    def ret_layer(self, li, j):
        S = self.S
        s_in = self.dram["s_ret%d" % li]
        o_p = self.dram["ret%d_p" % li]
        o_s = self.dram["ret%d_s" % li]
        with contextlib.ExitStack() as es:
            self.xn = self.sb(es, "xn", [128, NCH, 1056], BF16)
            self.xnres = [self.S.new_res("xn%d" % i) for i in range(3)]
            self.causT = self.sb(es, "causT", [128, 128], F32)
            self.delta16 = self.sb(es, "delta16", [128, 16], F32)
            self.deltaR = self.sb(es, "deltaR", [128, 16, 16], F32)
            rcst = S.new_res("retconst")
            S.dma(S.sp, self.causT[:], self.dram["consts"][:, 2, :], writes=[rcst])
            S.dma(S.sp, self.delta16[:], self.dram["consts"][:, 0, 0:16], writes=[rcst])
            S.dma(S.sp, self.deltaR[:], self.dram["deltaR"][:, :, :], writes=[rcst])
            cosT = self.sb(es, "cos", [128, 1056], F32)
            sinT = self.sb(es, "sin", [128, 1056], F32)
            qd = self.sb(es, "qd", [128, 512], F32)
            kd = self.sb(es, "kd", [128, 512], F32)
            qT = self.sb(es, "qT", [128, 2, 1056], BF16)
            kT = self.sb(es, "kT", [128, 2, 1056], BF16)
            gT = self.sb(es, "gT", [128, 4, 1056], BF16)
            vS = self.sb(es, "vS", [128, 10, 512], BF16)
            Sf = self.sb(es, "Sf", [128, 4, 2, 512], F32)
            Sb = self.sb(es, "Sb", [128, 2, 512], BF16)
            sring = [(self.sb(es, "sring", [128, 2, 512], F32), S.new_res("sring")) for _ in range(3)]
            sbf = self.sb(es, "sbf", [128, 2, 512], BF16)
            tmp = [(self.sb(es, "rtmp", [128, 512], F32), S.new_res("rtmp")) for _ in range(3)]
            sT2 = [self.sb(es, "sT", [128, 128], BF16) for _ in range(2)]
            onb2 = [self.sb(es, "onb", [128, 512], BF16) for _ in range(2)]
            kTok2 = [self.sb(es, "kTok", [128, 256], BF16) for _ in range(2)]
            sT2r = [S.new_res("sT") for _ in range(2)]
            onb2r = [S.new_res("onb") for _ in range(2)]
            kTok2r = [S.new_res("kTok") for _ in range(2)]
            Sfres2 = [S.new_res("Sf0"), S.new_res("Sf1")]
            Sbres2 = [S.new_res("Sb0"), S.new_res("Sb1")]
            kmask = self.sb(es, "kmask", [16, 16, 256], BF16)
            qmask = self.sb(es, "qmask", [128, 2, 16, 16], BF16)
            ssq = self.sb(es, "ssq", [128, 2], F32)
            junk = self.sb(es, "junk", [128, 512], BF16)
            (tabres, qdres, kdres, qres, kres, gres, vres, Sfres, Sbres, sbfres, sTres, onres, kTokres, kmres, qmres,
             ssres, jres) = [S.new_res(n) for n in
                             "tab qd kd q k g v Sf Sb sbf sT on kTok kmask qmask ss junk".split()]
            S.op(S.pool, lambda e: e.memset(Sf[:], 0.0), writes=Sfres2)
            for g in range(2):
                G = GROUPS[g]
                c0, c1 = G["c0"], G["c1"]
                tg = c1 - c0
                self.norm_group(g, V_NMIX + li)
                S.dma(S.sp, cosT[:, 0:tg], self.dram["cos_t"][:, c0:c1], writes=[tabres])
                S.dma(S.sp, sinT[:, 0:tg], self.dram["sin_t"][:, c0:c1], writes=[tabres])
                chunks = [(i * 128, 128, False) for i in range(8)]
                if g == 1:
                    chunks += [(1024, 16, False), (1040, 16, True)]
                for h in range(4):
                    gam = GAMMAS[h]
                    if g == 0:
                        S.op(S.pool, lambda e: e.memset(Sb[:], 0.0), writes=Sbres2)
                    else:
                        S.op(S.act, lambda e: e.activation(out=Sb[:], in_=Sf[:, h, :, :], func=AF.Copy),
                             reads=Sfres2, writes=Sbres2)
                    for (dst, dres, dtab, drow, tres) in ((qT, qres, qd, h, qdres), (kT, kres, kd, 4 + h, kdres)):
                        wb, wr = self.ws_next(128, 2048, kc=8)
                        for ti, (a, b) in enumerate(G["tiles"]):
                            w = b - a
                            la = a - c0
                            S.dma(S.sp, dtab[:, 0:w], self.dram["qkd_t"][drow, a:b].partition_broadcast(128), writes=[tres])
                            pA, pAr = self.bank()
                            pB, pBr = self.bank()
                            for (pp, ppr, o) in ((pA, pAr, 0), (pB, pBr, 128)):
                                for c in range(NCH):
                                    S.op(S.pe, lambda e: e.matmul(pp[:, 0:w], lhsT=wb[:, c, o:o + 128],
                                                                  rhs=self.xn[:, c, la:la + w],
                                                                  start=(c == 0), stop=(c == NCH - 1)),
                                         reads=[wr, self.xnres[ti]], writes=[ppr], inc=(c == NCH - 1))
                            (t1, r1), (t2, r2), (t3, r3) = tmp
                            co = cosT[:, la:la + w]
                            si = sinT[:, la:la + w]
                            dd = dtab[:, 0:w]
                            tt = lambda e, o_, a_, b_, op_: e.tensor_tensor(out=o_, in0=a_, in1=b_, op=op_)
                            S.op(S.dve, lambda e: tt(e, t1[:, 0:w], pA[:, 0:w], co, ALU.mult), reads=[pAr, tabres], writes=[r1])
                            S.op(S.dve, lambda e: tt(e, t2[:, 0:w], pB[:, 0:w], si, ALU.mult), reads=[pBr, tabres], writes=[r2])
                            S.op(S.dve, lambda e: tt(e, t1[:, 0:w], t1[:, 0:w], t2[:, 0:w], ALU.subtract), reads=[r1, r2], writes=[r1])
                            S.op(S.dve, lambda e: tt(e, dst[:, 0, la:la + w], t1[:, 0:w], dd, ALU.mult), reads=[r1, tres], writes=[dres])
                            S.op(S.dve, lambda e: tt(e, t2[:, 0:w], pA[:, 0:w], si, ALU.mult), reads=[pAr, tabres], writes=[r2])
                            S.op(S.dve, lambda e: tt(e, t3[:, 0:w], pB[:, 0:w], co, ALU.mult), reads=[pBr, tabres], writes=[r3])
                            S.op(S.dve, lambda e: tt(e, t2[:, 0:w], t2[:, 0:w], t3[:, 0:w], ALU.add), reads=[r2, r3], writes=[r2])
                            S.op(S.dve, lambda e: tt(e, dst[:, 1, la:la + w], t2[:, 0:w], dd, ALU.mult), reads=[r2, tres], writes=[dres])
                    for half in range(2):
                        wb, wr = self.ws_next(128, 2048, kc=8)
                        for ci, (lo, L, smp) in enumerate(chunks):
                            ps, pr = self.bank()
                            for c in range(NCH):
                                S.op(S.pe, lambda e: e.matmul(ps[0:L, 0:256], lhsT=self.xn[:, c, lo:lo + L],
                                                              rhs=wb[:, c, :], start=(c == 0), stop=(c == NCH - 1)),
                                     reads=[wr] + self.xnres, writes=[pr], inc=(c == NCH - 1))
                            S.op(S.act, lambda e: e.activation(out=vS[0:L, ci, half * 256:(half + 1) * 256],
                                                               in_=ps[0:L, 0:256], func=AF.Copy),
                                 reads=[pr], writes=[vres])
                    for half in range(2):
                        wb, wr = self.ws_next(128, 2048, kc=8)
                        for ti, (a, b) in enumerate(G["tiles"]):
                            w = b - a
                            la = a - c0
                            for oc in range(2):
                                ps, pr = self.bank()
                                for c in range(NCH):
                                    S.op(S.pe, lambda e: e.matmul(ps[:, 0:w], lhsT=wb[:, c, oc * 128:(oc + 1) * 128],
                                                                  rhs=self.xn[:, c, la:la + w],
                                                                  start=(c == 0), stop=(c == NCH - 1)),
                                         reads=[wr, self.xnres[ti]], writes=[pr], inc=(c == NCH - 1))
                                S.op(S.act, lambda e: e.activation(out=gT[:, half * 2 + oc, la:la + w], in_=ps[:, 0:w],
                                                                   func=AF.Silu),
                                     reads=[pr], writes=[gres])
                    def st_SK(ci, lo, L):
                        cs = slice(lo, lo + L)
                        k2 = ci % 2
                        psS, psSr = self.bank()
                        for dc in range(2):
                            S.op(S.pe, lambda e: e.matmul(psS[0:L, 0:L], lhsT=kT[:, dc, cs], rhs=qT[:, dc, cs],
                                                          start=(dc == 0), stop=(dc == 1)),
                                 reads=[kres, qres], writes=[psSr], inc=(dc == 1))
                        S.op(S.dve, lambda e: e.tensor_tensor(out=sT2[k2][0:L, 0:L], in0=psS[0:L, 0:L],
                                                              in1=self.causT[0:L, 0:L], op=ALU.mult),
                             reads=[psSr, self.cres, rcst], writes=[sT2r[k2]])
                        psK, psKr = self.bank()
                        pk = psK[:].bitcast(BF16)
                        for dc in range(2):
                            S.op(S.pe, lambda e: e.transpose(pk[0:L, dc * 128:(dc + 1) * 128], kT[:, dc, cs], self.identb[:, :]),
                                 reads=[kres, self.cres, rcst], writes=[psKr], inc=(dc == 1))
                        S.op(S.act, lambda e: e.activation(out=kTok2[k2][0:L, :], in_=pk[0:L, 0:256], func=AF.Copy,
                                                           scale=gam ** L),
                             reads=[psKr], writes=[kTok2r[k2]])

                    def st_O(ci, lo, L):
                        cs = slice(lo, lo + L)
                        k2 = ci % 2
                        psO, psOr = self.bank()
                        S.op(S.pe, lambda e: e.matmul(psO[0:L, :], lhsT=sT2[k2][0:L, 0:L], rhs=vS[0:L, ci, :],
                                                      start=True, stop=False),
                             reads=[sT2r[k2], vres], writes=[psOr], inc=False)
                        for dc in range(2):
                            S.op(S.pe, lambda e: e.matmul(psO[0:L, :], lhsT=qT[:, dc, cs], rhs=Sb[:, dc, :],
                                                          start=False, stop=(dc == 1)),
                                 reads=[qres, Sbres2[dc]], writes=[psOr], inc=(dc == 1))
                        return psO, psOr

                    def st_U(ci, lo, L, last_prompt):
                        k2 = ci % 2
                        gL = gam ** L
                        for dc in range(2):
                            ps2, ps2r = self.bank()
                            S.op(S.pe, lambda e: e.matmul(ps2[:, :], lhsT=kTok2[k2][0:L, dc * 128:(dc + 1) * 128],
                                                          rhs=vS[0:L, ci, :], start=True, stop=True),
                                 reads=[kTok2r[k2], vres], writes=[ps2r])
                            S.op(S.dve, lambda e: e.scalar_tensor_tensor(out=Sf[:, h, dc, :], in0=Sf[:, h, dc, :],
                                                                         scalar=gL, in1=ps2[:, :],
                                                                         op0=ALU.mult, op1=ALU.add),
                                 reads=[ps2r, Sfres2[dc]], writes=[Sfres2[dc]])
                            if not last_prompt:
                                if dc == 0:
                                    S.op(S.act, lambda e: e.activation(out=Sb[:, 0, :], in_=Sf[:, h, 0, :], func=AF.Copy),
                                         reads=[Sfres2[0]], writes=[Sbres2[0]])
                                else:
                                    S.op(S.act, lambda e: e.activation(out=Sb[:, 1, :], in_=Sf[:, h, 1, :], func=AF.Copy),
                                         reads=[Sfres2[1]], writes=[Sbres2[1]])
                        if last_prompt:
                            S.dma(S.sp, o_p[h].rearrange("(p two) e -> p two e", two=2), Sf[:, h, :, :],
                                  reads=Sfres2, is_output=True)

                    def st_N(ci, L, psO, psOr):
                        k2 = ci % 2
                        S.op(S.act, lambda e: e.activation(out=junk[0:L, :], in_=psO[0:L, :], func=AF.Square,
                                                           accum_out=ssq[0:L, 0:1]),
                             reads=[psOr], writes=[jres, ssres])
                        S.op(S.act, lambda e: e.activation(out=ssq[0:L, 1:2], in_=ssq[0:L, 0:1], func=AF.Ln,
                                                           bias=self.epsv[0:L, 0:1], scale=1.0 / 512.0),
                             reads=[ssres, self.cres, rcst], writes=[ssres])
                        S.op(S.act, lambda e: e.activation(out=ssq[0:L, 1:2], in_=ssq[0:L, 1:2], func=AF.Exp, scale=-0.5),
                             reads=[ssres], writes=[ssres])
                        S.op(S.act, lambda e: e.activation(out=onb2[k2][0:L, :], in_=psO[0:L, :], func=AF.Copy,
                                                           scale=ssq[0:L, 1:2]),
                             reads=[psOr, ssres], writes=[onb2r[k2]])

                    def st_T(ci, lo, L):
                        cs = slice(lo, lo + L)
                        k2 = ci % 2
                        psT, psTr = self.bank()
                        pt = psT[:].bitcast(BF16)
                        for dvc in range(4):
                            S.op(S.pe, lambda e: e.transpose(pt[:, dvc * 128:dvc * 128 + L], onb2[k2][0:L, dvc * 128:(dvc + 1) * 128],
                                                             self.identb[0:L, 0:L]),
                                 reads=[onb2r[k2], self.cres, rcst], writes=[psTr], inc=(dvc == 3))
                        S.op(S.dve, lambda e: e.tensor_tensor(
                            out=gT[:, :, cs], in0=pt[:, 0:512].rearrange("p (c l) -> p c l", c=4)[:, :, 0:L],
                            in1=gT[:, :, cs], op=ALU.mult),
                             reads=[psTr, gres], writes=[gres])

                    pch = [(ci, lo, L) for ci, (lo, L, smp) in enumerate(chunks) if not smp]
                    smp_chunk = [(ci, lo, L) for ci, (lo, L, smp) in enumerate(chunks) if smp]
                    sstate = {}
                    if smp_chunk:
                        sci, slo, sL = smp_chunk[0]
                        scs = slice(slo, slo + sL)
                        psK, psKr = self.bank()
                        pk = psK[:].bitcast(BF16)
                        for dc in range(2):
                            S.op(S.pe, lambda e: e.transpose(pk[0:sL, dc * 128:(dc + 1) * 128], kT[:, dc, scs], self.identb[:, :]),
                                 reads=[kres, self.cres, rcst], writes=[psKr], inc=(dc == 1))
                        S.op(S.dve, lambda e: e.scalar_tensor_tensor(
                            out=kmask[:, :, :], in0=pk[0:16, 0:256].unsqueeze(1).to_broadcast([16, 16, 256]),
                            scalar=gam, in1=self.delta16[0:16, :].unsqueeze(2).to_broadcast([16, 16, 256]),
                            op0=ALU.mult, op1=ALU.mult),
                             reads=[psKr, self.cres, rcst], writes=[kmres])
                        S.op(S.dve, lambda e: e.tensor_tensor(
                            out=qmask[:, :, :, :], in0=qT[:, :, scs].unsqueeze(2).to_broadcast([128, 2, 16, 16]),
                            in1=self.deltaR[:, :, :].unsqueeze(1).to_broadcast([128, 2, 16, 16]), op=ALU.mult),
                             reads=[qres, self.cres, rcst], writes=[qmres])
                        psOs, psOsr, psOsi = self.reserve_bank()

                        def sload(bq):
                            st_, sr_ = sring[bq % 3]
                            S.dma(S.sp, st_[:], s_in[bq, h].rearrange("(p two) e -> p two e", two=2), writes=[sr_])

                        def sample_step(b):
                            st, sr = sring[b % 3]
                            if b + 2 < NSMP:
                                sload(b + 2)
                            for dc in range(2):
                                ps2, ps2r = self.bank()
                                S.op(S.pe, lambda e: e.matmul(ps2[:, :], lhsT=kmask[0:16, b, dc * 128:(dc + 1) * 128],
                                                              rhs=vS[0:16, sci, :], start=True, stop=True),
                                     reads=[kmres, vres], writes=[ps2r])
                                S.op(S.dve, lambda e: e.scalar_tensor_tensor(out=st[:, dc, :], in0=st[:, dc, :],
                                                                             scalar=gam, in1=ps2[:, :],
                                                                             op0=ALU.mult, op1=ALU.add),
                                     reads=[ps2r, sr], writes=[sr])
                            S.dma(S.sp, o_s[b, h].rearrange("(p two) e -> p two e", two=2), st[:], reads=[sr],
                                  is_output=True)
                            S.op(S.act, lambda e: e.activation(out=sbf[:], in_=st[:], func=AF.Copy),
                                 reads=[sr], writes=[sbfres])
                            for dc in range(2):
                                last = (b == NSMP - 1 and dc == 1)
                                S.op(S.pe, lambda e: e.matmul(psOs[0:16, :], lhsT=qmask[:, dc, b, :], rhs=sbf[:, dc, :],
                                                              start=(b == 0 and dc == 0), stop=last),
                                     reads=[qmres, sbfres], writes=[psOsr], inc=(dc == 1))

                        sload(0)
                        sload(1)
                        sstate["next"] = 0

                    def samples(n):
                        if not smp_chunk:
                            return
                        for _ in range(n):
                            if sstate["next"] < NSMP:
                                sample_step(sstate["next"])
                                sstate["next"] += 1

                    st_SK(*pch[0])
                    prevT = None
                    for idx, (ci, lo, L) in enumerate(pch):
                        if idx + 1 < len(pch):
                            st_SK(*pch[idx + 1])
                        psO, psOr = st_O(ci, lo, L)
                        st_U(ci, lo, L, g == 1 and idx == len(pch) - 1)
                        st_N(ci, L, psO, psOr)
                        if prevT is not None:
                            st_T(*prevT)
                        prevT = (ci, lo, L)
                        samples(2)
                    st_T(*prevT)
                    if smp_chunk:
                        samples(NSMP)
                        self.breserved.discard(psOsi)
                        st_N(sci, sL, psOs, psOsr)
                        st_T(sci, slo, sL)
                    for half in range(2):
                        wb, wr = self.ws_next(128, 2048, kc=4)
                        for ti, (a, b) in enumerate(G["tiles"]):
                            w = b - a
                            la = a - c0
                            for oc in range(4):
                                ps, pr = self.bank()
                                for dvc in range(4):
                                    S.op(S.pe, lambda e: e.matmul(ps[:, 0:w], lhsT=wb[:, dvc, oc * 128:(oc + 1) * 128],
                                                                  rhs=gT[:, dvc, la:la + w],
                                                                  start=(dvc == 0), stop=(dvc == 3)),
                                         reads=[wr, gres], writes=[pr], inc=(dvc == 3))
                                ec = half * 4 + oc
                                S.op(S.dve, lambda e: e.tensor_tensor(out=self.hT[:, ec, a:b], in0=self.hT[:, ec, a:b],
                                                                      in1=ps[:, 0:w], op=ALU.add),
                                     reads=[pr], writes=[self.hres[(g, ti)]])

    def build(self):
        nc, S = self.nc, self.S
        es = self.es
        xT = self.din("xT", [D, TT])
        self.din("vecs", [128, NVEC, 8])
        self.din("cos_t", [128, TT])
        self.din("sin_t", [128, TT])
        self.din("qkd_t", [8, TT])
        self.din("consts", [128, 5, 128])
        self.din("deltaR", [128, 16, 16])
        L = self.layers
        if L >= 1:
            self.din("s_ret0", [NSMP, 4, 256, 512])
            self.din("ret_w_in", [2, D, 6144])
            self.din("ret_w_out", [2, 2048, D])
            self.din("mlp_w1", [4, D, 4 * D])
            self.din("mlp_w2", [4, 4 * D, D])
        if L >= 2:
            self.din("s_shiftT", [D, NSMP])
            self.din("s_wkv", [NSMP, 16, 64, 64])
            self.din("rwkv_w_rkv", [3, D, D])
            self.din("rwkv_w_o", [D, D])
            self.din("rwkv_w1a1", [D, 128])
            self.din("rwkv_g1", [D, 160])
            self.din("rwkv_w2a2", [64, 2048])
            self.din("rwkv_g2", [160, D])
            self.din("ln_gb", [3, D])
            self.din("rconst", [128, 11, 128])
            self.wcache = self.nc.dram_tensor("wcache", [len(RW_BLOCKS), 128, 2048], BF16).ap()
            self.wcres = [Res("wc%d" % i) for i in range(len(RW_BLOCKS))]
        if L >= 3:
            self.din("s_convT", [D, 2, NSMP])
            self.din("conv_w_in", [D, 3 * D])
            self.din("conv_w_out", [D, D])
        if L >= 4:
            self.din("s_ret3", [NSMP, 4, 256, 512])
        yT = self.dout("yT", [D, TT])
        if L >= 1:
            self.dout("ret0_p", [4, 256, 512])
            self.dout("ret0_s", [NSMP, 4, 256, 512])
        if L >= 2:
            self.dout("shiftT", [D, 1 + NSMP])
            self.dout("wkv_p", [16, 64, 64])
            self.dout("wkv_s", [NSMP, 16, 64, 64])
        if L >= 3:
            self.dout("convT_p", [D, 2])
            self.dout("convT_s", [D, 2, NSMP])
        if L >= 4:
            self.dout("ret3_p", [4, 256, 512])
            self.dout("ret3_s", [NSMP, 4, 256, 512])

        self.hT = self.sb(es, "hT", [128, NCH, TT], F32)
        self.hres = {}
        for g, G in enumerate(GROUPS):
            for ti in range(len(G["tiles"])):
                self.hres[(g, ti)] = Res("h%d_%d" % (g, ti))
        self.vecs = self.sb(es, "vecs", [128, NVEC, 8], F32)
        self.identb = self.sb(es, "identb", [128, 128], BF16)
        self.onesD = self.sb(es, "onesD", [128, 128], BF16)
        self.epsv = self.sb(es, "epsv", [128, 1], F32)
        self.cres = Res("consts")
        self.sqb = [(self.sb(es, "sq", [128, 512], BF16), Res("sq")) for _ in range(2)]
        self.rsb = (self.sb(es, "rs", [128, 512], F32), Res("rs"))
        self.wbufs = [(self.sb(es, "wbf", [128, 2048], BF16), Res("wbf")) for _ in range(NRING)]
        self.banks = [es.enter_context(nc.psum_tensor("bank%d" % i, [128, 512], F32)) for i in range(8)]
        self.bres = [Res("bank%d" % i) for i in range(8)]
        self.bnext = 0
        self.breserved = set()
        self.wplan = []
        self.w_dma = self.w_cv = self.w_use = 0

        for li in range(self.layers):
            kind = li % 3
            if kind == 0:
                self.plan_ret(li // 3)
            elif kind == 1:
                self.plan_rwkv()
            else:
                self.plan_conv()
            for g in range(2):
                self.plan_mlp(li)

        es_c = contextlib.ExitStack()
        cst = self.sb(es_c, "cst", [128, 5, 128], F32)
        S.dma(S.sp, self.hT[:], xT.rearrange("(c p) t -> p c t", p=128),
              writes=[self.hres[k] for k in self.hres])
        S.dma(S.sp, self.vecs[:], self.dram["vecs"][:, :, :], writes=[self.cres])
        S.dma(S.sp, cst[:], self.dram["consts"][:, :, :], writes=[self.cres])
        S.op(S.pool, lambda e: e.tensor_copy(out=self.identb[:], in_=cst[:, 0, :]), reads=[self.cres], writes=[self.cres])
        S.op(S.pool, lambda e: e.tensor_copy(out=self.onesD[:], in_=cst[:, 1, :]), reads=[self.cres], writes=[self.cres])
        S.op(S.pool, lambda e: e.memset(self.epsv[:], EPS), writes=[self.cres])
        es_c.close()

        for li in range(self.layers):
            kind = li % 3
            if kind == 0:
                self.ret_layer(li, li // 3)
            elif kind == 1:
                self.rwkv_layer(li)
            else:
                self.conv_layer(li)
            self.mlp_layer(li)

        with contextlib.ExitStack() as es2:
            yo = [(self.sb(es2, "yo", [128, NCH, 512], F32), S.new_res("yo")) for _ in range(2)]
            k = 0
            for g, G in enumerate(GROUPS):
                for ti, (a, b) in enumerate(G["tiles"]):
                    yb, yr = yo[k % 2]
                    k += 1
                    self.rmsnorm_tile(a, b, [self.hres[(g, ti)]], V_NFIN, lambda c: yb[:, c, 0:b - a], yr)
                    S.dma(S.sp, yT.rearrange("(c p) t -> p c t", p=128)[:, :, a:b], yb[:, :, 0:b - a], reads=[yr],
                          is_output=True)
        S.finish()
        es.close()
        S.close()
        return nc

    def hres_for(self, a, b):
        out = []
        for g, G in enumerate(GROUPS):
            for ti, (x, y) in enumerate(G["tiles"]):
                if x < b and a < y:
                    out.append(self.hres[(g, ti)])
        return out

    def rw_src(self, i):
        d = self.dram
        if i < 12:
            j, blk = divmod(i, 4)
            return d["rwkv_w_rkv"][j].rearrange("(c p) e -> p c e", p=128)[:, :, blk * 256:(blk + 1) * 256]
        if i < 16:
            blk = i - 12
            return d["rwkv_w_o"].rearrange("(c p) e -> p c e", p=128)[:, :, blk * 256:(blk + 1) * 256]
        if i == 16:
            return d["rwkv_w1a1"].rearrange("(c p) e -> p c e", p=128)
        if i == 17:
            return d["rwkv_g1"].rearrange("(c p) e -> p c e", p=128)
        if i == 18:
            return d["rwkv_w2a2"][:, :]
        if i == 19:
            return d["rwkv_g2"][0:128, :]
        return d["rwkv_g2"][128:160, :]

    def plan_rwkv(self):
        for i, (p, n) in enumerate(RW_BLOCKS):
            self.ws_add(self.rw_src(i), p, n)
        for t in range(17):
            for i in RW_ORDER:
                p, n = RW_BLOCKS[i]
                self.wplan.append(dict(ap=None, p=p, n=n, dst=None, dst_res=None, cached=i))

    def rwkv_layer(self, li):
        S = self.S
        C0 = math.exp(-0.5)
        with contextlib.ExitStack() as es:
            sb = lambda n, shp, dt: self.sb(es, n, shp, dt)
            R = S.new_res
            for i, (p, n) in enumerate(RW_BLOCKS):
                wb, wr = self.ws_next(p, n)
                S.dma(S.sp, self.wcache[i, 0:p, 0:n], wb, reads=[wr], writes=[self.wcres[i]])
            rc = sb("rconst", [128, 11, 128], F32)
            rcr = R("rconst")
            S.dma(S.sp, rc[:], self.dram["rconst"][:, :, :], writes=[rcr])
            mask_su = lambda L: rc[0:L, 3, 0:L]
            mask2 = lambda L: rc[0:L, 3:5, 0:L]
            mask_sl = lambda L: rc[0:L, 5, 0:L]
            mask_un = lambda L: rc[0:L, 6, 0:L]
            blockones = rc[:, 7, :]
            headsel = rc[:, 8, 0:2]
            id64 = rc[:, 8, 2:66]
            selS = rc[0:32, 8, 66:82]
            identf = rc[:, 9, :]
            lng_b = sb("lng_b", [128, D], F32)
            lnb_b = sb("lnb_b", [128, D], F32)
            w0_b = sb("w0_b", [128, D], F32)
            S.dma(S.sp, lng_b[:], self.dram["ln_gb"][0, :].partition_broadcast(128), writes=[rcr])
            S.dma(S.sp, lnb_b[:], self.dram["ln_gb"][1, :].partition_broadcast(128), writes=[rcr])
            S.dma(S.sp, w0_b[:], self.dram["ln_gb"][2, :].partition_broadcast(128), writes=[rcr])
            gnv = sb("gnv", [128, 1], F32)
            S.op(S.pool, lambda e: e.memset(gnv[:], GN_EPS), writes=[rcr])
            xnf = sb("xnf", [128, NCH, 129], F32)
            xx = sb("xx", [128, NCH, 128], F32)
            xm = [sb("xm", [128, NCH, 128], BF16) for _ in range(2)]
            RB = sb("RB", [128, NCH, 128], F32)
            KB = sb("KB", [128, NCH, 128], F32)
            KKB = sb("KKB", [128, NCH, 128], F32)
            AB = sb("AB", [128, NCH, 128], F32)
            GB = sb("GB", [128, NCH, 128], BF16)
            thw = sb("thw", [64, 128], BF16)
            la = sb("la", [64, 128], BF16)
            sg = sb("sg", [128, 128], BF16)
            sg2 = sb("sg2", [32, 128], BF16)
            SIGW = sb("SIGW", [128, D], F32)
            Vf = sb("Vf", [128, D], F32)
            Vb = sb("Vb", [128, D], BF16)
            yT = sb("yT", [128, NCH, 128], BF16)
            Hf = sb("Hf", [128, NCH, 128], F32)
            Hb = sb("Hb", [128, NCH, 128], BF16)
            shS = sb("shS", [128, NCH, NSMP], F32)
            vTs = sb("vTs", [128, NCH, NSMP], F32)
            (xnr, xxr, RBr, KBr, KKBr, ABr, GBr, lorar, SIGWr, Vfr, Vbr, yTr, shr, vTsr) = [
                R(n) for n in "xnf xx RB KB KKB AB GB lora SIGW Vf Vb yT shS vTs".split()]
            Hfr = [R("Hf%d" % i) for i in range(NCH)]
            Hbr = [R("Hb%d" % i) for i in range(NCH)]
            xmr = [R("xm0"), R("xm1")]
            S.dma(S.sp, shS[:], self.dram["s_shiftT"].rearrange("(c p) b -> p c b", p=128), writes=[shr])
            S.op(S.pool, lambda e: e.memset(Hf[:], 0.0), writes=Hfr)
            S.op(S.pool, lambda e: e.memset(Hb[:], 0.0), writes=Hbr)
            S.op(S.pool, lambda e: e.memset(xnf[:, :, 0:1], 0.0), writes=[xnr])

            def mm(out, lhsT, rhs, reads, writes, start=True, stop=True, inc=True, **kw):
                return S.op(S.pe, lambda e: e.matmul(out, lhsT=lhsT, rhs=rhs, start=start, stop=stop, **kw),
                            reads=reads, writes=writes, inc=inc)

            def act(out, in_, func, reads, writes, **kw):
                return S.op(S.act, lambda e: e.activation(out=out, in_=in_, func=func, **kw), reads=reads, writes=writes)

            def tt(eng, out, in0, in1, op, reads, writes):
                return S.op(eng, lambda e: e.tensor_tensor(out=out, in0=in0, in1=in1, op=op), reads=reads, writes=writes)

            def stt(eng, out, in0, scalar, in1, op0, op1, reads, writes):
                return S.op(eng, lambda e: e.scalar_tensor_tensor(out=out, in0=in0, scalar=scalar, in1=in1,
                                                                 op0=op0, op1=op1), reads=reads, writes=writes)

            def bc(ap, shape, axis):
                return ap.unsqueeze(axis).to_broadcast(shape)

            es2 = contextlib.ExitStack()
            sb2 = lambda n, shp, dt: self.sb(es2, n, shp, dt)
            ART = sb2("ART", [128, NCH, 2, 128], BF16)
            BH = sb2("BH", [128, NCH, 128], BF16)
            KH = sb2("KH", [128, NCH, 128], BF16)
            KBt = [sb2("KBt", [128, 2, 128], BF16) for _ in range(2)]
            A_tok = sb2("A_tok", [128, D], BF16)
            K_tok = sb2("K_tok", [128, D], BF16)
            B_tok = sb2("B_tok", [128, D], BF16)
            e3s = [(sb2("e3", [128, 3, 128], F32), R("e3")) for _ in range(2)]
            eis = [(sb2("ei", [128, 128], F32), R("ei")) for _ in range(2)]
            gC = sb2("gC", [128, NCH], F32)
            BS = sb2("BS", [128, 16], F32)
            st = sb2("st", [128, 64], F32)
            Ytok = sb2("Ytok", [128, D], F32)
            NHB = 2 * PBATCH
            PM = [sb2("PM", [128, 2, 128], CH_DT) for _ in range(NHB)]
            PN = [sb2("PN", [128, 128], CH_DT) for _ in range(NHB)]
            N0 = [sb2("N0", [128, 128], CH_DT) for _ in range(NHB)]
            N0r = [R("N0") for _ in range(NHB)]
            identr = sb2("identr", [128, 128], CH_DT)
            S.op(S.pool, lambda e: e.tensor_copy(out=identr[:], in_=rc[:, 9, :]), reads=[rcr], writes=[rcr])
            MTb = [sb2("MTb", [128, 128], BF16) for _ in range(NHB)]
            AKRK = [sb2("AKRK", [128, 2, 128], BF16) for _ in range(NHB)]
            ARB = [sb2("ARB", [128, 128], BF16) for _ in range(NHB)]
            Psb = [sb2("Psb", [128, 64], BF16) for _ in range(NHB)]
            Gp = [sb2("Gp", [128, 128], F32) for _ in range(PBATCH)]
            WTs = [sb2("WTs", [128, 128], BF16) for _ in range(PBATCH)]
            Us = [sb2("Us", [128, 128], BF16) for _ in range(PBATCH)]
            (ARTr, BHr, KHr, Atr, Ktr, Btr, gCr, BSr, str_, Ytr) = [
                R(n) for n in "ART BH KH A_tok K_tok B_tok gC BS st Ytok".split()]
            KBtr = [R("KBt0"), R("KBt1")]
            PMr = [R("PM") for _ in range(NHB)]
            PNr = [R("PN") for _ in range(NHB)]
            MTbr = [R("MTb") for _ in range(NHB)]
            AKRKr = [R("AKRK") for _ in range(NHB)]
            ARBr = [R("ARB") for _ in range(NHB)]
            Psbr = [R("Psb") for _ in range(NHB)]
            Gpr = [R("Gp") for _ in range(PBATCH)]
            WTsr = [R("WTs") for _ in range(PBATCH)]
            Usr = [R("Us") for _ in range(PBATCH)]

            for ti in range(17):
                last = (ti == 16)
                a = ti * 128
                W = 32 if last else 128
                L = 16 if last else 128
                b = a + W
                hrs = self.hres_for(a, b)
                if ti > 0:
                    S.op(S.dve, lambda e: e.tensor_copy(out=xnf[:, :, 0:1], in_=xnf[:, :, 128:129]),
                         reads=[xnr], writes=[xnr])
                self.rmsnorm_tile(a, b, hrs, V_NMIX + li, lambda c: xnf[:, c, 1:1 + W], xnr)
                tt(S.pool, xx[:, :, 0:L], xnf[:, :, 0:L], xnf[:, :, 1:L + 1], ALU.subtract, [xnr], [xxr])
                if last:
                    tt(S.pool, xx[:, :, 16:32], shS[:, :, :], xnf[:, :, 17:33], ALU.subtract, [xnr, shr], [xxr])
                    S.dma(S.sp, self.dram["shiftT"].rearrange("(c p) t -> p c t", p=128), xnf[:, :, 16:33],
                          reads=[xnr], is_output=True)

                def build_xm(j, k):
                    for c in range(NCH):
                        stt(S.dve, xm[k][:, c, 0:W], xx[:, c, 0:W], self.vecs[:, V_RMIX + j, c:c + 1],
                            xnf[:, c, 1:1 + W], ALU.mult, ALU.add, [xxr, xnr, self.cres], [xmr[k]])

                for j, (dst, dr) in enumerate(((RB, RBr), (KB, KBr))):
                    build_xm(j, j)
                    for blk in range(4):
                        wb, wr = self.ws_next(128, 2048, kc=8)
                        for oc in range(2):
                            ps, pr = self.bank()
                            for c in range(NCH):
                                mm(ps[:, 0:W], wb[:, c, oc * 128:(oc + 1) * 128], xm[j][:, c, 0:W], [wr, xmr[j]], [pr],
                                   start=(c == 0), stop=(c == NCH - 1), inc=(c == NCH - 1))
                            act(dst[:, blk * 2 + oc, 0:W], ps[:, 0:W], AF.Copy, [pr], [dr])
                build_xm(3, 1)
                build_xm(4, 0)
                wb, wr = self.ws_next(128, 1024, kc=8)
                psW, psWr = self.bank()
                psA, psAr = self.bank()
                for c in range(NCH):
                    mm(psW[0:64, 0:W], wb[:, c, 0:64], xm[1][:, c, 0:W], [wr, xmr[1]], [psWr],
                       start=(c == 0), stop=(c == NCH - 1), inc=(c == NCH - 1))
                for c in range(NCH):
                    mm(psA[0:64, 0:W], wb[:, c, 64:128], xm[0][:, c, 0:W], [wr, xmr[0]], [psAr],
                       start=(c == 0), stop=(c == NCH - 1), inc=(c == NCH - 1))
                act(thw[0:64, 0:W], psW[0:64, 0:W], AF.Tanh, [psWr], [lorar])
                act(la[0:64, 0:W], psA[0:64, 0:W], AF.Copy, [psAr], [lorar])
                wb, wr = self.ws_next(64, 2048)
                for half in range(2):
                    ps, pr = self.bank()
                    mm(ps[0:W, :], thw[0:64, 0:W], wb[0:64, half * 512:(half + 1) * 512], [wr, lorar], [pr])
                    tt(S.dve, SIGW[0:W, half * 512:(half + 1) * 512], ps[0:W, :], w0_b[0:W, half * 512:(half + 1) * 512],
                       ALU.add, [pr, rcr], [SIGWr])
                act(SIGW[0:W, :], SIGW[0:W, :], AF.Sigmoid, [SIGWr], [SIGWr])
                for fc in range(NCH):
                    ps, pr = self.bank()
                    mm(ps[:, 0:W], wb[0:64, 1024 + fc * 128:1024 + (fc + 1) * 128], la[0:64, 0:W], [wr, lorar], [pr])
                    act(AB[:, fc, 0:W], ps[:, 0:W], AF.Sigmoid, [pr, self.cres], [ABr], bias=self.vecs[:, V_A0, fc:fc + 1])
                build_xm(2, 0)
                build_xm(5, 1)
                for blk in range(4):
                    wb, wr = self.ws_next(128, 2048, kc=8)
                    ps, pr = self.bank()
                    for c in range(NCH):
                        mm(ps[0:W, 0:256], xm[0][:, c, 0:W], wb[:, c, :], [wr, xmr[0]], [pr],
                           start=(c == 0), stop=(c == NCH - 1), inc=(c == NCH - 1))
                    act(Vf[0:W, blk * 256:(blk + 1) * 256], ps[0:W, 0:256], AF.Copy, [pr], [Vfr])
                    if last:
                        for oc in range(2):
                            ps, pr = self.bank()
                            for c in range(NCH):
                                mm(ps[:, 0:NSMP], wb[:, c, oc * 128:(oc + 1) * 128], xm[0][:, c, 16:32], [wr, xmr[0]], [pr],
                                   start=(c == 0), stop=(c == NCH - 1), inc=(c == NCH - 1))
                            act(vTs[:, blk * 2 + oc, :], ps[:, 0:NSMP], AF.Copy, [pr], [vTsr])
                vb = lambda row: bc(self.vecs[:, row, :], [128, NCH, W], 2)
                SCR = xx
                tt(S.dve, KKB[:, :, 0:W], KB[:, :, 0:W], vb(V_KK), ALU.mult, [KBr, self.cres], [KKBr])
                tt(S.pool, SCR[:, :, 0:W], KKB[:, :, 0:W], KKB[:, :, 0:W], ALU.mult, [KKBr], [xxr])
                for half in range(2):
                    ps, pr = self.bank()
                    pv = ps[:, 0:4 * W].rearrange("p (c w) -> p c w", c=4)
                    mm(pv, blockones, SCR[:, half * 4:(half + 1) * 4, 0:W], [xxr, rcr], [pr])
                    S.op(S.dve, lambda e: e.tensor_scalar(out=SCR[:, half * 4:(half + 1) * 4, 0:W], in0=pv, scalar1=1e-24,
                                                          scalar2=None, op0=ALU.max), reads=[pr], writes=[xxr])
                act(SCR[:, :, 0:W], SCR[:, :, 0:W], AF.Ln, [xxr], [xxr])
                act(SCR[:, :, 0:W], SCR[:, :, 0:W], AF.Exp, [xxr], [xxr], scale=-0.5)
                tt(S.dve, KKB[:, :, 0:W], KKB[:, :, 0:W], SCR[:, :, 0:W], ALU.mult, [KKBr, xxr], [KKBr])
                stt(S.dve, SCR[:, :, 0:W], AB[:, :, 0:W], -1.0, vb(V_KA), ALU.add, ALU.mult, [ABr, self.cres], [xxr])
                stt(S.dve, KB[:, :, 0:W], SCR[:, :, 0:W], 1.0, KB[:, :, 0:W], ALU.add, ALU.mult, [xxr, KBr], [KBr])
                tt(S.pool, AB[:, :, 0:W], KKB[:, :, 0:W], AB[:, :, 0:W], ALU.mult, [KKBr, ABr], [ABr])
                tt(S.pool, SCR[:, :, 0:W], RB[:, :, 0:W], vb(V_RK), ALU.mult, [RBr, self.cres], [xxr])
                tt(S.pool, SCR[:, :, 0:W], SCR[:, :, 0:W], KB[:, :, 0:W], ALU.mult, [xxr, KBr], [xxr])
                wb, wr = self.ws_next(128, 1280, kc=8)
                psG, psGr = self.bank()
                psG2, psG2r = self.bank()
                for c in range(NCH):
                    mm(psG[:, 0:W], wb[:, c, 0:128], xm[1][:, c, 0:W], [wr, xmr[1]], [psGr],
                       start=(c == 0), stop=(c == NCH - 1), inc=(c == NCH - 1))
                for c in range(NCH):
                    mm(psG2[0:32, 0:W], wb[:, c, 128:160], xm[1][:, c, 0:W], [wr, xmr[1]], [psG2r],
                       start=(c == 0), stop=(c == NCH - 1), inc=(c == NCH - 1))
                act(sg[:, 0:W], psG[:, 0:W], AF.Sigmoid, [psGr], [lorar])
                act(sg2[0:32, 0:W], psG2[0:32, 0:W], AF.Sigmoid, [psG2r], [lorar])
                wbA, wrA = self.ws_next(128, 1024)
                wbB, wrB = self.ws_next(32, 1024)
                for fc in range(NCH):
                    ps, pr = self.bank()
                    mm(ps[:, 0:W], wbA[:, fc * 128:(fc + 1) * 128], sg[:, 0:W], [wrA, lorar], [pr], start=True, stop=False, inc=False)
                    mm(ps[:, 0:W], wbB[0:32, fc * 128:(fc + 1) * 128], sg2[0:32, 0:W], [wrB, lorar], [pr], start=False, stop=True)
                    act(GB[:, fc, 0:W], ps[:, 0:W], AF.Copy, [pr], [GBr])
                S.op(S.pool, lambda e: e.tensor_copy(out=Vb[0:W, :], in_=Vf[0:W, :]), reads=[Vfr], writes=[Vbr])
                psB, psBr = self.bank()
                for c in range(NCH):
                    mm(psB[0:L, 2 * c:2 * c + 2], SCR[:, c, 0:L], headsel, [xxr, rcr], [psBr], inc=(c == NCH - 1))
                act(BS[0:L, :], psB[0:L, 0:16], AF.Copy, [psBr], [BSr])
                psKt, psKtr, iKt = self.reserve_bank()
                psBt, psBtr, iBt = self.reserve_bank()
                pkt = psKt[:].bitcast(BF16)
                pbt = psBt[:].bitcast(BF16)
                p3s = {}

                def dec_mm(fc_):
                    ps_, pr_ = self.bank()
                    p3_ = ps_[:, 0:3 * L].rearrange("p (c w) -> p c w", c=3)
                    mm(p3_, SIGW[0:L, fc_ * 128:(fc_ + 1) * 128], rc[0:L, 0:3, 0:L], [SIGWr, rcr], [pr_])
                    p3s[fc_] = (p3_, pr_)

                dec_mm(0)
                for fc in range(NCH):
                    if fc + 1 < NCH:
                        dec_mm(fc + 1)
                    p3, pr = p3s.pop(fc)
                    e3, e3r = e3s[fc % 2]
                    ei, eir = eis[fc % 2]
                    act(e3[:, :, 0:L], p3, AF.Exp, [pr], [e3r])
                    act(ei[:, 0:L], p3[:, 0, :], AF.Exp, [pr], [eir], scale=-1.0)
                    act(gC[:, fc:fc + 1], e3[:, 0, L - 1:L], AF.Copy, [e3r], [gCr])
                    tt(S.dve, ART[:, fc, 0, 0:L], KKB[:, fc, 0:L], e3[:, 1, 0:L], ALU.mult, [KKBr, e3r], [ARTr])
                    tt(S.dve, ART[:, fc, 1, 0:L], RB[:, fc, 0:L], e3[:, 0, 0:L], ALU.mult, [RBr, e3r], [ARTr])
                    tt(S.pool, BH[:, fc, 0:L], AB[:, fc, 0:L], ei[:, 0:L], ALU.mult, [ABr, eir], [BHr])
                    tt(S.pool, KH[:, fc, 0:L], KB[:, fc, 0:L], ei[:, 0:L], ALU.mult, [KBr, eir], [KHr])
                    kbt, kbtr = KBt[fc % 2], KBtr[fc % 2]
                    tt(S.dve, kbt[:, 0, 0:L], KB[:, fc, 0:L], e3[:, 2, 0:L], ALU.mult, [KBr, e3r], [kbtr])
                    stt(S.dve, kbt[:, 1, 0:L], AB[:, fc, 0:L], -1.0, e3[:, 2, 0:L], ALU.mult, ALU.mult, [ABr, e3r], [kbtr])
                    S.op(S.pe, lambda e: e.transpose(pkt[0:L, fc * 128:(fc + 1) * 128], kbt[:, 0, 0:L], self.identb[:, :]),
                         reads=[kbtr, self.cres], writes=[psKtr], inc=False)
                    S.op(S.pe, lambda e: e.transpose(pbt[0:L, fc * 128:(fc + 1) * 128], kbt[:, 1, 0:L], self.identb[:, :]),
                         reads=[kbtr, self.cres], writes=[psBtr], inc=True)
                act(K_tok[0:L, :], pkt[0:L, :], AF.Copy, [psKtr], [Ktr])
                act(B_tok[0:L, :], pbt[0:L, :], AF.Copy, [psBtr], [Btr])
                self.breserved.discard(iKt)
                self.breserved.discard(iBt)
                psAt, psAtr = self.bank()
                pat = psAt[:].bitcast(BF16)
                for fc in range(NCH):
                    S.op(S.pe, lambda e: e.transpose(pat[0:L, fc * 128:(fc + 1) * 128], ART[:, fc, 0, 0:L], self.identb[:, :]),
                         reads=[ARTr, self.cres], writes=[psAtr], inc=(fc == NCH - 1))
                act(A_tok[0:L, :], pat[0:L, :], AF.Copy, [psAtr], [Atr])
                nlev = int(round(math.log2(L))) - 1
                for p0 in range(0, NCH, PBATCH):
                    pairs = list(range(p0, min(NCH, p0 + PBATCH)))
                    heads = [(p, j) for p in pairs for j in range(2)]
                    slot = {hj: i for i, hj in enumerate(heads)}

                    def hv(p, j):
                        rows = slice(64 * j, 64 * j + 64)
                        hd = 2 * p + j
                        return rows, slice(hd * 64, hd * 64 + 64)

                    for (p, j) in heads:
                        k = slot[(p, j)]
                        rows, hs = hv(p, j)
                        arT = ART[rows, p, :, 0:L]
                        aT = ART[rows, p, 0, 0:L]
                        bT = BH[rows, p, 0:L]
                        kT = KH[rows, p, 0:L]
                        pA, pAr = self.bank()
                        pAv = pA[0:L, 0:2 * L].rearrange("p (c w) -> p c w", c=2)
                        mm(pAv, bT, arT, [BHr, ARTr], [pAr])
                        pB, pBr = self.bank()
                        pBv = pB[0:L, 0:2 * L].rearrange("p (c w) -> p c w", c=2)
                        mm(pBv, kT, arT, [KHr, ARTr], [pBr])
                        pC, pCr = self.bank()
                        mm(pC[0:L, 0:L], aT, bT, [ARTr, BHr], [pCr])
                        tt(S.dve, PM[k][0:L, 0, 0:L], pAv[:, 0, :], mask_su(L), ALU.mult, [pAr, rcr], [PMr[k]])
                        tt(S.dve, ARB[k][0:L, 0:L], pAv[:, 1, :], mask_un(L), ALU.mult, [pAr, rcr], [ARBr[k]])
                        tt(S.dve, AKRK[k][0:L, :, 0:L], pBv, mask2(L), ALU.mult, [pBr, rcr], [AKRKr[k]])
                        tt(S.dve, N0[k][0:L, 0:L], pC[0:L, 0:L], mask_sl(L), ALU.mult, [pCr, rcr], [N0r[k]])
                        tt(S.pool, PM[k][0:L, 1, 0:L], identf[0:L, 0:L], PM[k][0:L, 0, 0:L], ALU.subtract, [PMr[k], rcr], [PMr[k]])
                    for (p, j) in heads:
                        k = slot[(p, j)]
                        p1, p1r = self.bank()
                        p2, p2r = self.bank()
                        mm(p1[0:L, 0:L], PM[k][0:L, 0, 0:L], N0[k][0:L, 0:L], [PMr[k], N0r[k]], [p1r])
                        mm(p2[0:L, 0:L], N0[k][0:L, 0:L], PM[k][0:L, 0, 0:L], [PMr[k], N0r[k]], [p2r])
                        act(PN[k][0:L, 0:L], p1[0:L, 0:L], AF.Copy, [p1r], [PNr[k]])
                        act(PM[k][0:L, 0, 0:L], p2[0:L, 0:L], AF.Copy, [p2r], [PMr[k]])
                    for lev in range(1, nlev + 1):
                        for (p, j) in heads:
                            k = slot[(p, j)]
                            if lev == nlev - 1:
                                pX, pXr = self.bank()
                                mm(pX[0:L, 0:L], PN[k][0:L, 0:L], PM[k][0:L, 1, 0:L], [PMr[k], PNr[k]], [pXr])
                                pY, pYr = self.bank()
                                mm(pY[0:L, 0:L], PM[k][0:L, 0, 0:L], PN[k][0:L, 0:L], [PMr[k], PNr[k]], [pYr])
                                tt(S.dve, PM[k][0:L, 1, 0:L], PM[k][0:L, 1, 0:L], pX[0:L, 0:L], ALU.add, [pXr, PMr[k]], [PMr[k]])
                                act(PN[k][0:L, 0:L], pY[0:L, 0:L], AF.Copy, [pYr], [PNr[k]])
                            elif lev < nlev:
                                pX, pXr = self.bank()
                                pXv = pX[0:L, 0:2 * L].rearrange("p (c w) -> p c w", c=2)
                                mm(pXv, PN[k][0:L, 0:L], PM[k][0:L, :, 0:L], [PMr[k], PNr[k]], [pXr])
                                pY, pYr = self.bank()
                                mm(pY[0:L, 0:L], PM[k][0:L, 0, 0:L], PN[k][0:L, 0:L], [PMr[k], PNr[k]], [pYr])
                                act(PM[k][0:L, 0, 0:L], pXv[:, 0, :], AF.Copy, [pXr], [PMr[k]])
                                tt(S.dve, PM[k][0:L, 1, 0:L], PM[k][0:L, 1, 0:L], pXv[:, 1, :], ALU.add, [pXr, PMr[k]], [PMr[k]])
                                act(PN[k][0:L, 0:L], pY[0:L, 0:L], AF.Copy, [pYr], [PNr[k]])
                            else:
                                pX, pXr = self.bank()
                                mm(pX[0:L, 0:L], PN[k][0:L, 0:L], PM[k][0:L, 1, 0:L], [PMr[k], PNr[k]], [pXr])
                                tt(S.dve, PM[k][0:L, 1, 0:L], PM[k][0:L, 1, 0:L], pX[0:L, 0:L], ALU.add, [pXr, PMr[k]], [PMr[k]])
                    for (p, j) in heads:
                        k = slot[(p, j)]
                        pTb, pTr = self.bank()
                        pTv = pTb[:].bitcast(CH_DT)
                        S.op(S.pe, lambda e: e.transpose(pTv[0:L, 0:L], PM[k][0:L, 1, 0:L], identr[0:L, 0:L]),
                             reads=[PMr[k], rcr], writes=[pTr])
                        pZ, pZr = self.bank()
                        mm(pZ[0:L, 0:L], N0[k][0:L, 0:L], PM[k][0:L, 1, 0:L], [N0r[k], PMr[k]], [pZr])
                        act(PN[k][0:L, 0:L], pTv[0:L, 0:L], AF.Copy, [pTr], [PNr[k]])
                        tt(S.pool, PM[k][0:L, 0, 0:L], rc[0:L, 10, 0:L], PM[k][0:L, 1, 0:L], ALU.subtract, [PMr[k], rcr], [PMr[k]])
                        tt(S.dve, PM[k][0:L, 0, 0:L], PM[k][0:L, 0, 0:L], pZ[0:L, 0:L], ALU.subtract, [pZr, PMr[k]], [PMr[k]])
                    for (p, j) in heads:
                        k = slot[(p, j)]
                        rows, hs = hv(p, j)
                        pM1, pM1r = self.bank()
                        mm(pM1[0:L, 0:L], PN[k][0:L, 0:L], PM[k][0:L, 0, 0:L], [PNr[k], PMr[k]], [pM1r])
                        pP, pPr = self.bank()
                        mm(pP[0:L, 0:64], AKRK[k][0:L, 0, 0:L], Vb[0:L, hs], [AKRKr[k], Vbr], [pPr])
                        act(MTb[k][0:L, 0:L], pM1[0:L, 0:L], AF.Copy, [pM1r], [MTbr[k]])
                        act(Psb[k][0:L, :], pP[0:L, 0:64], AF.Copy, [pPr], [Psbr[k]])
                    for p in pairs:
                        q = p - p0
                        psWT, psWTr, iWT = self.reserve_bank()
                        pG, pGr = self.bank()
                        for j in range(2):
                            k = slot[(p, j)]
                            rows, hs = hv(p, j)
                            mm(pG[0:L, 64 * j:64 * j + 64], MTb[k][0:L, 0:L], Psb[k][0:L, :], [MTbr[k], Psbr[k]], [pGr])
                            if j == 0:
                                mm(psWT[0:64, 0:L], A_tok[0:L, hs], MTb[k][0:L, 0:L], [Atr, MTbr[k]], [psWTr])
                            else:
                                mm(psWT[64:128, 0:L], A_tok[0:L, hs], MTb[k][0:L, 0:L], [Atr, MTbr[k]], [psWTr],
                                   tile_position=(0, 64))
                        act(Gp[q][0:L, :], pG[0:L, 0:128], AF.Copy, [pGr], [Gpr[q]])
                        act(WTs[q][:, 0:L], psWT[:, 0:L], AF.Copy, [psWTr], [WTsr[q]])
                        self.breserved.discard(iWT)
                    for p in pairs:
                        q = p - p0
                        pU, pUr = self.bank()
                        mm(pU[0:L, 0:128], WTs[q][:, 0:L], Hb[:, p, :], [WTsr[q], Hbr[p]], [pUr])
                        tt(S.dve, Us[q][0:L, :], pU[0:L, 0:128], Gp[q][0:L, :], ALU.add, [pUr, Gpr[q]], [Usr[q]])
                    for p in pairs:
                        q = p - p0
                        pYo, pYor = self.bank()
                        mm(pYo[0:L, 0:128], ART[:, p, 1, 0:L], Hb[:, p, :], [ARTr, Hbr[p]], [pYor], start=True, stop=False, inc=False)
                        for j in range(2):
                            k = slot[(p, j)]
                            rows, hs = hv(p, j)
                            cs = slice(64 * j, 64 * j + 64)
                            mm(pYo[0:L, cs], AKRK[k][0:L, 1, 0:L], Vb[0:L, hs], [AKRKr[k], Vbr], [pYor], start=False, stop=False, inc=False)
                            mm(pYo[0:L, cs], ARB[k][0:L, 0:L], Us[q][0:L, cs], [ARBr[k], Usr[q]], [pYor], start=False, stop=(j == 1),
                               inc=(j == 1))
                        pH, pHr = self.bank()
                        ps_ = slice(p * 128, (p + 1) * 128)
                        mm(pH[:, 0:128], K_tok[0:L, ps_], Vb[0:L, ps_], [Ktr, Vbr], [pHr], start=True, stop=False, inc=False)
                        mm(pH[:, 0:128], B_tok[0:L, ps_], Us[q][0:L, :], [Btr, Usr[q]], [pHr], start=False, stop=True)
                        act(Ytok[0:L, p * 128:(p + 1) * 128], pYo[0:L, 0:128], AF.Copy, [pYor], [Ytr])
                        for j in range(2):
                            rows = slice(64 * j, 64 * j + 64)
                            stt(S.dve, Hf[rows, p, rows], Hf[rows, p, rows], gC[rows, p:p + 1], pH[rows, rows], ALU.mult, ALU.add,
                                [pHr, Hfr[p], gCr], [Hfr[p]])
                        act(Hb[:, p, :], Hf[:, p, :], AF.Copy, [Hfr[p]], [Hbr[p]])
                Y3 = lambda L_: Ytok[0:L_, :].rearrange("p (h v) -> p h v", h=16)
                YSQ = SIGW
                S.op(S.dve, lambda e: e.tensor_reduce(out=st[0:L, 0:16], in_=Y3(L), axis=AX.X, op=ALU.add),
                     reads=[Ytr], writes=[str_])
                act(YSQ[0:L, :], Ytok[0:L, :], AF.Square, [Ytr], [SIGWr])
                S.op(S.dve, lambda e: e.tensor_reduce(out=st[0:L, 16:32], in_=YSQ[0:L, :].rearrange("p (h v) -> p h v", h=16),
                                                      axis=AX.X, op=ALU.add), reads=[SIGWr], writes=[str_])
                S.op(S.dve, lambda e: e.tensor_scalar(out=st[0:L, 0:16], in0=st[0:L, 0:16], scalar1=1.0 / 64.0, scalar2=None,
                                                      op0=ALU.mult), reads=[str_], writes=[str_])
                tt(S.dve, st[0:L, 32:48], st[0:L, 0:16], st[0:L, 0:16], ALU.mult, [str_], [str_])
                stt(S.dve, st[0:L, 32:48], st[0:L, 16:32], 1.0 / 64.0, st[0:L, 32:48], ALU.mult, ALU.subtract, [str_], [str_])
                act(st[0:L, 48:64], st[0:L, 32:48], AF.Ln, [str_, rcr], [str_], bias=gnv[0:L, 0:1])
                act(st[0:L, 48:64], st[0:L, 48:64], AF.Exp, [str_], [str_], scale=-0.5)
                tt(S.dve, Y3(L), Y3(L), bc(st[0:L, 0:16], [L, 16, 64], 2), ALU.subtract, [Ytr, str_], [Ytr])
                tt(S.dve, Y3(L), Y3(L), bc(st[0:L, 48:64], [L, 16, 64], 2), ALU.mult, [Ytr, str_], [Ytr])
                tt(S.dve, Ytok[0:L, :], Ytok[0:L, :], lng_b[0:L, :], ALU.mult, [Ytr, rcr], [Ytr])
                tt(S.dve, Ytok[0:L, :], Ytok[0:L, :], lnb_b[0:L, :], ALU.add, [Ytr, rcr], [Ytr])
                tt(S.pool, YSQ[0:L, :].rearrange("p (h v) -> p h v", h=16), Vf[0:L, :].rearrange("p (h v) -> p h v", h=16),
                   bc(BS[0:L, :], [L, 16, 64], 2), ALU.mult, [Vfr, BSr], [SIGWr])
                tt(S.dve, Ytok[0:L, :], Ytok[0:L, :], YSQ[0:L, :], ALU.add, [Ytr, SIGWr], [Ytr])
                for half in range(2):
                    ps, pr = self.bank()
                    for q4 in range(4):
                        fc = half * 4 + q4
                        S.op(S.pe, lambda e: e.transpose(ps[:, q4 * L:(q4 + 1) * L], Ytok[0:L, fc * 128:(fc + 1) * 128],
                                                         identf[0:L, 0:L]),
                             reads=[Ytr, rcr], writes=[pr], inc=(q4 == 3))
                    tt(S.dve, yT[:, half * 4:(half + 1) * 4, 0:L], ps[:, 0:4 * L].rearrange("p (c w) -> p c w", c=4),
                       GB[:, half * 4:(half + 1) * 4, 0:L], ALU.mult, [pr, GBr], [yTr])
                if last:
                    for j in range(2):
                        rows = slice(64 * j, 64 * j + 64)
                        S.dma(S.sp, self.dram["wkv_p"].rearrange("(p two) k v -> two k p v", two=2)[j],
                              Hf[rows, :, rows], reads=Hfr, is_output=True)
                    es2.close()
                    self.rwkv_samples(es, li, dict(RB=RB, KB=KB, KKB=KKB, AB=AB, GB=GB, SIGW=SIGW, vTs=vTs, SCR=SCR, yT=yT,
                                                   RBr=RBr, KBr=KBr, KKBr=KKBr, ABr=ABr, GBr=GBr, SIGWr=SIGWr, vTsr=vTsr,
                                                   SCRr=xxr, yTr=yTr, rc=rc, rcr=rcr, gnv=gnv))
                for blk in range(4):
                    wb, wr = self.ws_next(128, 2048, kc=8)
                    for oc in range(2):
                        ps, pr = self.bank()
                        for fc in range(NCH):
                            mm(ps[:, 0:W], wb[:, fc, oc * 128:(oc + 1) * 128], yT[:, fc, 0:W], [wr, yTr], [pr],
                               start=(fc == 0), stop=(fc == NCH - 1), inc=(fc == NCH - 1))
                        ec = blk * 2 + oc
                        tt(S.dve, self.hT[:, ec, a:b], self.hT[:, ec, a:b], ps[:, 0:W], ALU.add, [pr] + hrs, hrs)

    def rwkv_samples(self, es, li, T):
        S = self.S
        R = S.new_res
        rc, rcr = T["rc"], T["rcr"]
        blockones = rc[:, 7, :]
        id64 = rc[:, 8, 2:66]
        selS = rc[0:32, 8, 66:82]
        NB = 2
        sbx = lambda n, shp, dt: self.sb(es, n, shp, dt)
        DEC = sbx("DEC", [128, NCH, NSMP], F32)
        SS = sbx("SS", [128, NB, NCH, 64], F32)
        XD = sbx("XD", [128, NB, NCH, 64], F32)
        TT_ = sbx("TTs", [128, NB, NCH, 64], F32)
        XB = [sbx("XB", [128, NB, NCH, 64], F32) for _ in range(5)]
        sa = sbx("sa", [128, NB, NCH], F32)
        YS = sbx("YS", [128, NCH, NSMP], F32)
        YQ = sbx("YQ", [128, NCH, NSMP], F32)
        MU = sbx("MU", [128, NCH, NSMP], F32)
        RS = sbx("RS", [128, NCH, NSMP], F32)
        DECr, SSr, XDr, TTr, sar, YSr, YQr, MUr, RSr = [R(n) for n in "DEC SS XD TT sa YS YQ MU RS".split()]
        XBr = [R("XB%d" % i) for i in range(5)]

        def mm(out, lhsT, rhs, reads, writes, **kw):
            return S.op(S.pe, lambda e: e.matmul(out, lhsT=lhsT, rhs=rhs, start=True, stop=True, **kw), reads=reads, writes=writes)

        def act(out, in_, func, reads, writes, **kw):
            return S.op(S.act, lambda e: e.activation(out=out, in_=in_, func=func, **kw), reads=reads, writes=writes)

        def tt(eng, out, in0, in1, op, reads, writes):
            return S.op(eng, lambda e: e.tensor_tensor(out=out, in0=in0, in1=in1, op=op), reads=reads, writes=writes)

        sw = self.dram["s_wkv"].rearrange("b (p two) v k -> b two v p k", two=2)
        ow = self.dram["wkv_s"].rearrange("b (p two) v k -> b two v p k", two=2)
        for fc in range(NCH):
            ps, pr = self.bank()
            mm(ps[:, 0:NSMP], T["SIGW"][0:32, fc * 128:(fc + 1) * 128], selS, [T["SIGWr"], rcr], [pr])
            act(DEC[:, fc, :], ps[:, 0:NSMP], AF.Exp, [pr], [DECr])
        srcs = [(T["KKB"], T["KKBr"], 16), (DEC, DECr, 0), (T["AB"], T["ABr"], 16), (T["KB"], T["KBr"], 16), (T["RB"], T["RBr"], 16)]
        for b0 in range(0, NSMP, NB):
            for bb in range(NB):
                for j in range(2):
                    S.dma(S.sp, SS[64 * j:64 * j + 64, bb, :, :], sw[b0 + bb, j], writes=[SSr])
            for qi, (src, srcr, off) in enumerate(srcs):
                xin = src[:, :, off + b0:off + b0 + NB].rearrange("q p b -> q b p").unsqueeze(3).to_broadcast([128, NB, NCH, 64])
                idb = id64.unsqueeze(1).unsqueeze(1).to_broadcast([128, NB, NCH, 64])
                tt(S.dve, XD[:, :, :, :], xin, idb, ALU.mult, [srcr, rcr], [XDr])
                xdf = XD[:, :, :, :].rearrange("q b p k -> q (b p k)")
                xbf = XB[qi][:, :, :, :].rearrange("q b p k -> q (b p k)")
                for i in range(NB * NCH * 64 // 512):
                    ps, pr = self.bank()
                    mm(ps[:, :], blockones, xdf[:, i * 512:(i + 1) * 512], [XDr, rcr], [pr])
                    act(xbf[:, i * 512:(i + 1) * 512], ps[:, :], AF.Copy, [pr], [XBr[qi]])
            red = lambda out, in_, rd, wr: S.op(S.dve, lambda e: e.tensor_reduce(out=out, in_=in_, axis=AX.X, op=ALU.add),
                                                reads=rd, writes=wr)
            A4 = lambda t: t[:, :, :, :]
            tt(S.dve, A4(TT_), A4(SS), A4(XB[0]), ALU.mult, [SSr, XBr[0]], [TTr])
            red(sa[:, :, :], A4(TT_), [TTr], [sar])
            tt(S.dve, A4(SS), A4(SS), A4(XB[1]), ALU.mult, [SSr, XBr[1]], [SSr])
            tt(S.dve, A4(TT_), A4(XB[2]), sa[:, :, :].unsqueeze(3).to_broadcast([128, NB, NCH, 64]), ALU.mult, [XBr[2], sar, TTr], [TTr])
            tt(S.dve, A4(SS), A4(SS), A4(TT_), ALU.subtract, [SSr, TTr], [SSr])
            vsb = T["vTs"][:, :, b0:b0 + NB].rearrange("q p b -> q b p").unsqueeze(3).to_broadcast([128, NB, NCH, 64])
            tt(S.dve, A4(TT_), A4(XB[3]), vsb, ALU.mult, [XBr[3], T["vTsr"], TTr], [TTr])
            tt(S.dve, A4(SS), A4(SS), A4(TT_), ALU.add, [SSr, TTr], [SSr])
            tt(S.dve, A4(TT_), A4(SS), A4(XB[4]), ALU.mult, [SSr, XBr[4], TTr], [TTr])
            red(YS[:, :, b0:b0 + NB].rearrange("q p b -> q b p"), A4(TT_), [TTr], [YSr])
            for bb in range(NB):
                for j in range(2):
                    S.dma(S.sp, ow[b0 + bb, j], SS[64 * j:64 * j + 64, bb, :, :], reads=[SSr], is_output=True)
        fl = lambda t: t[:, :, :].rearrange("q p b -> q (p b)")
        ps1, ps1r = self.bank()
        mm(ps1[:, 0:128], blockones, fl(YS), [YSr, rcr], [ps1r])
        tt(S.pool, YQ[:, :, :], YS[:, :, :], YS[:, :, :], ALU.mult, [YSr], [YQr])
        ps2, ps2r = self.bank()
        mm(ps2[:, 0:128], blockones, fl(YQ), [YQr, rcr], [ps2r])
        act(fl(MU), ps1[:, 0:128], AF.Copy, [ps1r], [MUr], scale=1.0 / 64.0)
        tt(S.dve, fl(YQ), fl(MU), fl(MU), ALU.mult, [MUr, ps2r], [YQr])
        S.op(S.dve, lambda e: e.scalar_tensor_tensor(out=fl(RS), in0=ps2[:, 0:128], scalar=1.0 / 64.0, in1=fl(YQ),
                                                     op0=ALU.mult, op1=ALU.subtract), reads=[ps2r, YQr], writes=[RSr])
        act(fl(RS), fl(RS), AF.Ln, [RSr, rcr], [RSr], bias=T["gnv"][:, 0:1])
        act(fl(RS), fl(RS), AF.Exp, [RSr], [RSr], scale=-0.5)
        tt(S.dve, fl(YS), fl(YS), fl(MU), ALU.subtract, [YSr, MUr], [YSr])
        tt(S.dve, fl(YS), fl(YS), fl(RS), ALU.mult, [YSr, RSr], [YSr])
        vb = lambda row: self.vecs[:, row, :].unsqueeze(2).to_broadcast([128, NCH, NSMP])
        tt(S.dve, YS[:, :, :], YS[:, :, :], vb(V_LNG), ALU.mult, [YSr, self.cres], [YSr])
        tt(S.dve, YS[:, :, :], YS[:, :, :], vb(V_LNB), ALU.add, [YSr, self.cres], [YSr])
        S.op(S.pool, lambda e: e.tensor_copy(out=YQ[:, :, :], in_=T["SCR"][:, :, 16:32]), reads=[T["SCRr"]], writes=[YQr])
        ps3, ps3r = self.bank()
        mm(ps3[:, 0:128], blockones, fl(YQ), [YQr, rcr], [ps3r])
        tt(S.dve, fl(MU), ps3[:, 0:128], fl(T["vTs"]), ALU.mult, [ps3r, T["vTsr"]], [MUr])
        tt(S.dve, fl(YS), fl(YS), fl(MU), ALU.add, [YSr, MUr], [YSr])
        tt(S.dve, T["yT"][:, :, 16:32], YS[:, :, :], T["GB"][:, :, 16:32], ALU.mult, [YSr, T["GBr"]], [T["yTr"]])


_LAYERS = 4


def _const_tables():
    inv = (1.0 / (np.float32(10000.0) ** np.linspace(0.0, 1.0, 128, dtype=np.float32))).astype(np.float32)
    pos = np.concatenate([np.arange(TP), np.full(NSMP, 16384)]).astype(np.float32)
    ang = (pos[None, :] * inv[:, None]).astype(np.float32).astype(np.float64)
    cos_t = np.cos(ang).astype(np.float32)
    sin_t = np.sin(ang).astype(np.float32)
    l = np.concatenate([np.arange(TP) % 128, np.zeros(NSMP)]).astype(np.float64)
    qkd = np.zeros((8, TT), np.float64)
    for h in range(4):
        qkd[h] = GAMMAS[h] ** (l + 1.0)
        qkd[4 + h] = GAMMAS[h] ** (-(l + 1.0)) * (256.0 ** -0.5)
    consts = np.zeros((128, 5, 128), np.float32)
    consts[:, 0, :] = np.eye(128)
    consts[:, 1, :] = 1.0 / D
    m = np.arange(128)
    consts[:, 2, :] = (m[None, :] >= m[:, None])
    deltaR = np.broadcast_to(np.eye(16, dtype=np.float32)[None], (128, 16, 16)).copy()
    c0 = math.exp(-0.5)
    row = m[:, None]
    col = m[None, :]
    rconst = np.zeros((128, 11, 128), np.float32)
    rconst[:, 0, :] = -c0 * (row <= col)
    rconst[:, 1, :] = -c0 * (row < col)
    rconst[:, 2, :] = -c0 * (row > col)
    rconst[:, 3, :] = (col > row)
    rconst[:, 4, :] = (col >= row)
    rconst[:, 5, :] = (row > col)
    rconst[:, 6, :] = -1.0 * (col >= row)
    rconst[:, 7, :] = (row // 64 == col // 64)
    rconst[:, 8, 0:2] = (m[:, None] // 64 == np.arange(2)[None, :])
    rconst[:, 8, 2:66] = (m[:, None] % 64 == np.arange(64)[None, :])
    rconst[0:32, 8, 66:82] = -c0 * (np.arange(32)[:, None] == 16 + np.arange(16)[None, :])
    rconst[:, 9, :] = np.eye(128)
    rconst[:, 10, :] = 2.0 * np.eye(128)
    return cos_t, sin_t, qkd.astype(np.float32), consts, deltaR, rconst


def _vec_layout(v):
    return np.ascontiguousarray(np.asarray(v, np.float32).reshape(8, 128).T)


def kernel(**inp):
    f = lambda k: np.asarray(inp[k], np.float32)
    cos_t, sin_t, qkd, consts, deltaR, rconst = _const_tables()
    rows = ([f("norm_mix")[i] for i in range(4)] + [f("norm_mlp")[i] for i in range(4)] + [f("norm_final")]
            + [f("rwkv_mix")[i] for i in range(6)]
            + [f("rwkv_w0"), f("rwkv_a0"), f("rwkv_k_k"), f("rwkv_k_a"), f("rwkv_r_k").reshape(-1)]
            + [f("conv_w")[i] for i in range(3)] + [f("rwkv_ln_g"), f("rwkv_ln_b")])
    assert len(rows) == NVEC
    vecs = np.ascontiguousarray(np.stack([_vec_layout(r) for r in rows], axis=1))
    w_in = f("ret_w_in").copy()
    for base in (0, 1024):
        blk = w_in[:, :, base:base + 1024].reshape(2, D, 4, 128, 2)
        w_in[:, :, base:base + 1024] = blk.transpose(0, 1, 2, 4, 3).reshape(2, D, 1024)
    shared = {
        "vecs": vecs, "cos_t": cos_t, "sin_t": sin_t, "qkd_t": qkd, "consts": consts, "deltaR": deltaR,
        "ret_w_in": w_in, "ret_w_out": f("ret_w_out"), "rwkv_w_rkv": f("rwkv_w_rkv"), "rwkv_w_o": f("rwkv_w_o"),
        "conv_w_in": f("conv_w_in"), "conv_w_out": f("conv_w_out"), "mlp_w1": f("mlp_w1"), "mlp_w2": f("mlp_w2"),
        "rwkv_w1a1": np.ascontiguousarray(np.concatenate([f("rwkv_w1"), f("rwkv_a1")], axis=1)),
        "rwkv_g1": f("rwkv_g1"),
        "rwkv_w2a2": np.ascontiguousarray(np.concatenate([f("rwkv_w2"), f("rwkv_a2")], axis=1)),
        "rwkv_g2": f("rwkv_g2"),
        "ln_gb": np.ascontiguousarray(np.stack([f("rwkv_ln_g"), f("rwkv_ln_b"), f("rwkv_w0")])),
        "rconst": rconst,
    }
    xp, xs, meta = f("x_prompt"), f("x_sample"), f("meta_tokens")
    in_maps = []
    for b in range(N_CORES):
        sl = slice(NSMP * b, NSMP * (b + 1))
        xall = np.concatenate([meta, xp[b], xs[sl, 0, :]], axis=0)
        m = dict(shared)
        m["xT"] = np.ascontiguousarray(xall.T)
        m["s_ret0"] = np.ascontiguousarray(f("state_ret_l0")[sl])
        m["s_ret3"] = np.ascontiguousarray(f("state_ret_l3")[sl])
        m["s_shiftT"] = np.ascontiguousarray(f("state_rwkv_shift_l1")[sl].T)
        m["s_wkv"] = np.ascontiguousarray(f("state_rwkv_wkv_l1")[sl])
        m["s_convT"] = np.ascontiguousarray(f("state_conv_l2")[sl].transpose(2, 1, 0))
        in_maps.append(m)
    prog = Prog(layers=_LAYERS)
    nc = prog.build()
    in_maps = [{k: v for k, v in m.items() if k in prog.inputs} for m in in_maps]
    res = run_bass_kernel_spmd(nc, in_maps, core_ids=list(range(N_CORES)))
    R = res.results
    B = N_CORES
    y_p = np.stack([R[b]["yT"][:, 16:TP].T for b in range(B)])
    y_s = np.concatenate([R[b]["yT"][:, TP:TT].T for b in range(B)])[:, None, :]
    zz = {"ret0_p": (4, 256, 512), "ret3_p": (4, 256, 512), "ret0_s": (NSMP, 4, 256, 512), "ret3_s": (NSMP, 4, 256, 512),
          "shiftT": (D, 1 + NSMP), "wkv_p": (16, 64, 64), "wkv_s": (NSMP, 16, 64, 64), "convT_p": (D, 2),
          "convT_s": (D, 2, NSMP)}
    for b in range(B):
        for k, shp in zz.items():
            if k not in R[b]:
                R[b][k] = np.zeros(shp, np.float32)
    cat = lambda k: np.concatenate([R[b][k] for b in range(B)], axis=0)
    stk = lambda k: np.stack([R[b][k] for b in range(B)])
    shift_p = np.stack([R[b]["shiftT"][:, 0] for b in range(B)])
    shift_s = np.concatenate([R[b]["shiftT"][:, 1:].T for b in range(B)])
    conv_p = np.stack([R[b]["convT_p"].T for b in range(B)])
    conv_s = np.concatenate([R[b]["convT_s"].transpose(2, 1, 0) for b in range(B)])
    outs = (y_p, y_s, stk("ret0_p"), cat("ret0_s"), shift_p, shift_s, stk("wkv_p").transpose(0, 1, 3, 2), cat("wkv_s"),
            conv_p, conv_s, stk("ret3_p"), cat("ret3_s"))
    return tuple(np.ascontiguousarray(o, dtype=np.float32) for o in outs)
```
